# Optimizing a Trainium2 kernel written in Bass

```python
import math
import jax, jax.numpy as jnp
from jax import lax
import numpy as np

D_MODEL = 1024
BATCH = 4
SEQ = 4096
DEPTH = 4
DEC_BATCH = 128
DEC_SEQ = 8
PAST_LEN = 2048
PAGE_SIZE = 128

N_EVEN = (DEPTH + 1) // 2
N_ODD = DEPTH // 2
EPS = 1e-6
CONV_CH = D_MODEL // 2
CONV_W = 31
HEAD_DIM = 64
HPG = 4
DIL_GROUPS = ((128, 1), (512, 4), (2048, 16))
N_DIL = len(DIL_GROUPS)
ATT_W = N_DIL * HPG * HEAD_DIM
ATT_OUT = HPG * HEAD_DIM
ROT_DIM = HEAD_DIM // 4
ROPE_THETA = 500000.0
BAND_BLOCK = 128
IN_COLS = 2 * CONV_CH + 3 * ATT_W
MIX_OUT = CONV_CH + ATT_OUT
S5_GROUP = 16
S5_GROUPS = D_MODEL // S5_GROUP
S5_STATE = 64
S5_CHUNK = 128
D_FF = int(math.ceil(8 * D_MODEL / 3 / 256)) * 256

kernel_name = 'hybrid_conv_dilattn_s5_adaln_step'


def rmsnorm(x, g):
    xf = x.astype(jnp.float32)
    y = xf * lax.rsqrt(jnp.mean(xf * xf, axis=-1, keepdims=True) + EPS)
    return (y * g.astype(jnp.float32)).astype(x.dtype)


def layernorm(x, g, b):
    xf = x.astype(jnp.float32)
    mu = jnp.mean(xf, axis=-1, keepdims=True)
    xc = xf - mu
    y = xc * lax.rsqrt(jnp.mean(xc * xc, axis=-1, keepdims=True) + EPS)
    return (y * g.astype(jnp.float32) + b.astype(jnp.float32)).astype(x.dtype)


def rope(x, pos):
    half = ROT_DIM // 2
    inv = jnp.float32(ROPE_THETA) ** (-(2.0 / ROT_DIM) * jnp.arange(half, dtype=jnp.float32))
    ang = pos.astype(jnp.float32)[:, None] * inv[None, :]
    bshape = (1, pos.shape[0]) + (1,) * (x.ndim - 3) + (half,)
    cos = jnp.cos(ang).reshape(bshape)
    sin = jnp.sin(ang).reshape(bshape)
    xr = x[..., :ROT_DIM].astype(jnp.float32)
    x1, x2 = xr[..., :half], xr[..., half:]
    rot = jnp.concatenate([x1 * cos - x2 * sin, x2 * cos + x1 * sin], axis=-1).astype(x.dtype)
    return jnp.concatenate([rot, x[..., ROT_DIM:]], axis=-1)


def dilated_attn_prompt(q, k, v, window, dil):
    n, s, h, dh = q.shape
    ls = s // dil
    bb = BAND_BLOCK
    nb = -(-ls // bb)
    lp = nb * bb
    span = window // dil

    def strided(t):
        t = t.reshape(n, ls, dil, h, dh).transpose(0, 2, 1, 3, 4)
        t = jnp.pad(t, ((0, 0), (0, 0), (0, lp - ls), (0, 0), (0, 0)))
        return t.reshape(n, dil, nb, bb, h, dh)

    def with_prev(t):
        prev = jnp.pad(t, ((0, 0), (0, 0), (1, 0), (0, 0), (0, 0), (0, 0)))[:, :, :nb]
        return jnp.concatenate([prev, t], axis=3)

    qs = strided(q)
    kb = with_prev(strided(k))
    vb = with_prev(strided(v))
    sc = jnp.einsum('nrbqhd,nrbkhd->nrbhqk', qs, kb).astype(jnp.float32)
    blk = jnp.arange(nb)[:, None, None] * bb
    qi = blk + jnp.arange(bb)[None, :, None]
    ki = blk - bb + jnp.arange(2 * bb)[None, None, :]
    dist = qi - ki
    mask = (ki >= 0) & (dist >= 0) & (dist <= span)
    sc = jnp.where(mask[None, None, :, None], sc, -jnp.inf)
    m = jnp.max(sc, axis=-1, keepdims=True)
    p = jnp.exp(sc - m)
    l = jnp.sum(p, axis=-1, keepdims=True)
    o = jnp.einsum('nrbhqk,nrbkhd->nrbqhd', p / l, vb.astype(jnp.float32))
    lse = (m + jnp.log(l))[..., 0]
    o = o.reshape(n, dil, lp, h, dh)[:, :, :ls].transpose(0, 2, 1, 3, 4).reshape(n, s, h, dh)
    lse = lse.transpose(0, 1, 2, 4, 3).reshape(n, dil, lp, h)[:, :, :ls]
    lse = lse.transpose(0, 2, 1, 3).reshape(n, s, h)
    return o, lse


def dilated_attn_sample(q, k, v, buf, window, dil):
    n, t, h, dh = q.shape
    wb = buf.shape[1]
    span = window // dil
    kf = jnp.concatenate([buf[:, :, 0].astype(k.dtype), k], axis=1)
    vf = jnp.concatenate([buf[:, :, 1].astype(v.dtype), v], axis=1)
    idx = wb + jnp.arange(t)[:, None] - dil * jnp.arange(span + 1)[None, :]
    valid = idx >= 0
    idx = jnp.maximum(idx, 0)
    kg = kf[:, idx]
    vg = vf[:, idx]
    sc = jnp.einsum('nthd,ntkhd->nthk', q, kg).astype(jnp.float32)
    sc = jnp.where(valid[None, :, None, :], sc, -jnp.inf)
    m = jnp.max(sc, axis=-1, keepdims=True)
    p = jnp.exp(sc - m)
    l = jnp.sum(p, axis=-1, keepdims=True)
    o = jnp.einsum('nthk,ntkhd->nthd', p / l, vg.astype(jnp.float32))
    lse = (m + jnp.log(l))[..., 0]
    return o, lse


def even_mixer(h, pos, conv_hist, kv_bufs, w_in, conv_w, conv_b, ln_g, ln_b, w_o):
    n, L, _ = h.shape
    z = h @ w_in
    a_val = z[..., :CONV_CH]
    a_gate = z[..., CONV_CH:2 * CONV_CH]
    qkv = z[..., 2 * CONV_CH:].reshape(n, L, 3, N_DIL, HPG, HEAD_DIM)
    u = a_val * jax.nn.sigmoid(a_gate)
    full = jnp.concatenate([conv_hist.astype(u.dtype), u], axis=1)
    yc = lax.conv_general_dilated(full, conv_w[:, None, :].astype(full.dtype), (1,), 'VALID',
                                  dimension_numbers=('NWC', 'WIO', 'NWC'),
                                  feature_group_count=CONV_CH) + conv_b
    yc = jax.nn.silu(layernorm(yc, ln_g, ln_b))
    new_hist = full[:, -(CONV_W - 1):]
    q = rope(qkv[:, :, 0], pos) * (HEAD_DIM ** -0.5)
    k = rope(qkv[:, :, 1], pos)
    v = qkv[:, :, 2]
    outs, lses, kv_new = [], [], []
    for gi, (win, dil) in enumerate(DIL_GROUPS):
        qg, kg, vg = q[:, :, gi], k[:, :, gi], v[:, :, gi]
        if kv_bufs is None:
            o, lse = dilated_attn_prompt(qg, kg, vg, win, dil)
            keep = min(win, L)
            kv_new.append(jnp.stack([kg[:, L - keep:], vg[:, L - keep:]], axis=2))
        else:
            o, lse = dilated_attn_sample(qg, kg, vg, kv_bufs[gi], win, dil)
            kv_new.append(jnp.stack([kg, vg], axis=2))
        outs.append(o)
        lses.append(lse)
    wts = jax.nn.softmax(jnp.stack(lses), axis=0)[..., None]
    att = jnp.sum(wts * jnp.stack(outs), axis=0).astype(h.dtype).reshape(n, L, ATT_OUT)
    mix = jnp.concatenate([yc.astype(h.dtype), att], axis=-1) @ w_o
    return mix, new_hist, kv_new


def complex_affine_combine(e1, e2):
    a1r, a1i, b1r, b1i = e1
    a2r, a2i, b2r, b2i = e2
    return (a2r * a1r - a2i * a1i, a2r * a1i + a2i * a1r,
            a2r * b1r - a2i * b1i + b2r, a2r * b1i + a2i * b1r + b2i)


def s5_mixer(u, s0, lam_re, lam_im, log_dt, b_re, b_im, c_re, c_im, d_skip, w_glu, b_glu):
    f32 = jnp.float32
    n, L, _ = u.shape
    ug = u.astype(f32).reshape(n, L, S5_GROUPS, S5_GROUP)
    lr = lam_re.astype(f32)
    li = lam_im.astype(f32)
    dt = jnp.exp(log_dt.astype(f32))[:, None]
    mag = jnp.exp(lr * dt)
    ph = li * dt
    ab_re = mag * jnp.cos(ph)
    ab_im = mag * jnp.sin(ph)
    den = lr * lr + li * li
    nr = ab_re - 1.0
    f_re = (nr * lr + ab_im * li) / den
    f_im = (ab_im * lr - nr * li) / den
    br = b_re.astype(f32)
    bi = b_im.astype(f32)
    bb_re = f_re[..., None] * br - f_im[..., None] * bi
    bb_im = f_re[..., None] * bi + f_im[..., None] * br
    cr = c_re.astype(f32)
    ci = c_im.astype(f32)
    chunk = S5_CHUNK if L % S5_CHUNK == 0 else L
    nc = L // chunk
    uc = ug.reshape(n, nc, chunk, S5_GROUPS, S5_GROUP).transpose(1, 0, 2, 3, 4)

    def step(carry, ub):
        sr, si = carry
        bur = jnp.einsum('ncgh,gph->ncgp', ub, bb_re)
        bui = jnp.einsum('ncgh,gph->ncgp', ub, bb_im)
        bur = bur.at[:, 0].add(ab_re * sr - ab_im * si)
        bui = bui.at[:, 0].add(ab_re * si + ab_im * sr)
        ar = jnp.broadcast_to(ab_re, bur.shape)
        ai = jnp.broadcast_to(ab_im, bur.shape)
        _, _, hr, hi = lax.associative_scan(complex_affine_combine, (ar, ai, bur, bui), axis=1)
        y = jnp.einsum('ncgp,ghp->ncgh', hr, cr) - jnp.einsum('ncgp,ghp->ncgh', hi, ci)
        return (hr[:, -1], hi[:, -1]), y

    s0f = s0.astype(f32)
    (sr, si), ys = lax.scan(step, (s0f[..., 0], s0f[..., 1]), uc)
    y = ys.transpose(1, 0, 2, 3, 4).reshape(n, L, D_MODEL) + d_skip.astype(f32) * u.astype(f32)
    zz = jax.nn.gelu(y)
    g = zz @ w_glu.astype(f32) + b_glu.astype(f32)
    out = g[..., :D_MODEL] * jax.nn.sigmoid(g[..., D_MODEL:])
    return out.astype(u.dtype), jnp.stack([sr, si], axis=-1)


def swiglu(h, wg, wu, wd):
    return (jax.nn.silu(h @ wg) * (h @ wu)) @ wd


def run_trunk(x, c, pos, conv_st, kv_st, s5_st, W):
    n = x.shape[0]
    new_conv, new_s5 = [], []
    new_kv = [[] for _ in range(N_DIL)]
    ei = 0
    oi = 0
    for layer in range(DEPTH):
        mod = jax.nn.silu(c.astype(jnp.float32)) @ W['w_ada'][layer].astype(jnp.float32)
        mod = (mod + W['b_ada'][layer].astype(jnp.float32)).astype(x.dtype)[:, None, :]
        sh1, sc1, g1, sh2, sc2, g2 = jnp.split(mod, 6, axis=-1)
        h = rmsnorm(x, W['norm_g'][layer, 0]) * (1 + sc1) + sh1
        if layer % 2 == 0:
            hist = jnp.zeros((n, CONV_W - 1, CONV_CH), x.dtype) if conv_st is None else conv_st[ei]
            bufs = None if kv_st is None else [kv[ei] for kv in kv_st]
            mix, hist_new, kv_new = even_mixer(h, pos, hist, bufs, W['w_in'][ei], W['conv_w'][ei],
                                               W['conv_b'][ei], W['conv_ln_g'][ei],
                                               W['conv_ln_b'][ei], W['w_o'][ei])
            new_conv.append(hist_new)
            for gi in range(N_DIL):
                new_kv[gi].append(kv_new[gi])
            ei += 1
        else:
            s0 = jnp.zeros((n, S5_GROUPS, S5_STATE, 2), jnp.float32) if s5_st is None else s5_st[oi]
            mix, s_new = s5_mixer(h, s0, W['s5_lam_re'][oi], W['s5_lam_im'][oi], W['s5_log_dt'][oi],
                                  W['s5_b_re'][oi], W['s5_b_im'][oi], W['s5_c_re'][oi],
                                  W['s5_c_im'][oi], W['s5_d'][oi], W['s5_w_glu'][oi],
                                  W['s5_b_glu'][oi])
            new_s5.append(s_new)
            oi += 1
        x = x + g1 * mix
        h = rmsnorm(x, W['norm_g'][layer, 1]) * (1 + sc2) + sh2
        x = x + g2 * swiglu(h, W['w_ff_gate'][layer], W['w_ff_up'][layer], W['w_ff_down'][layer])
    y = rmsnorm(x, W['final_g'])
    return y, jnp.stack(new_conv), [jnp.stack(kv) for kv in new_kv], jnp.stack(new_s5)


def setup_inputs(seed: int = 0) -> dict:
    key = jax.random.key(seed)
    ks = jax.random.split(key, 32)
    f32 = jnp.float32

    def nrm(i, shape, s):
        return jax.random.normal(ks[i], shape, f32) * s

    wb = [min(w, PAST_LEN) for (w, _) in DIL_GROUPS]
    lam_im0 = jnp.pi * jnp.arange(S5_STATE, dtype=f32)
    return {
        'x_prompt': nrm(0, (BATCH, SEQ, D_MODEL), 1.0),
        'x_sample': nrm(1, (DEC_BATCH, DEC_SEQ, D_MODEL), 1.0),
        'cache_conv': nrm(2, (N_EVEN, DEC_BATCH, CONV_W - 1, CONV_CH), 0.5),
        'cache_kv_g0': nrm(3, (N_EVEN, DEC_BATCH, wb[0], 2, HPG, HEAD_DIM), 1.0),
        'cache_kv_g1': nrm(4, (N_EVEN, DEC_BATCH, wb[1], 2, HPG, HEAD_DIM), 1.0),
        'cache_kv_g2': nrm(5, (N_EVEN, DEC_BATCH, wb[2], 2, HPG, HEAD_DIM), 1.0),
        'state_s5': nrm(6, (N_ODD, DEC_BATCH, S5_GROUPS, S5_STATE, 2), 1.0),
        'c_prompt': nrm(7, (BATCH, D_MODEL), 1.0),
        'c_sample': nrm(8, (DEC_BATCH, D_MODEL), 1.0),
        'norm_g': 1.0 + nrm(9, (DEPTH, 2, D_MODEL), 0.01),
        'final_g': 1.0 + nrm(10, (D_MODEL,), 0.01),
        'w_ada': nrm(11, (DEPTH, D_MODEL, 6 * D_MODEL), 0.5 * D_MODEL ** -0.5),
        'b_ada': nrm(12, (DEPTH, 6 * D_MODEL), 0.01),
        'w_in': nrm(13, (N_EVEN, D_MODEL, IN_COLS), D_MODEL ** -0.5),
        'conv_w': nrm(14, (N_EVEN, CONV_W, CONV_CH), CONV_W ** -0.5),
        'conv_b': nrm(15, (N_EVEN, CONV_CH), 0.01),
        'conv_ln_g': 1.0 + nrm(16, (N_EVEN, CONV_CH), 0.01),
        'conv_ln_b': nrm(17, (N_EVEN, CONV_CH), 0.01),
        'w_o': nrm(18, (N_EVEN, MIX_OUT, D_MODEL), MIX_OUT ** -0.5),
        's5_lam_re': -0.5 + nrm(19, (N_ODD, S5_GROUPS, S5_STATE), 0.01),
        's5_lam_im': lam_im0 + nrm(20, (N_ODD, S5_GROUPS, S5_STATE), 0.01),
        's5_log_dt': jax.random.uniform(ks[21], (N_ODD, S5_GROUPS), f32,
                                        minval=math.log(1e-3), maxval=math.log(1e-1)),
        's5_b_re': nrm(22, (N_ODD, S5_GROUPS, S5_STATE, S5_GROUP), (2 * S5_GROUP) ** -0.5),
        's5_b_im': nrm(23, (N_ODD, S5_GROUPS, S5_STATE, S5_GROUP), (2 * S5_GROUP) ** -0.5),
        's5_c_re': nrm(24, (N_ODD, S5_GROUPS, S5_GROUP, S5_STATE), S5_STATE ** -0.5),
        's5_c_im': nrm(25, (N_ODD, S5_GROUPS, S5_GROUP, S5_STATE), S5_STATE ** -0.5),
        's5_d': nrm(26, (N_ODD, D_MODEL), 1.0),
        's5_w_glu': nrm(27, (N_ODD, D_MODEL, 2 * D_MODEL), D_MODEL ** -0.5),
        's5_b_glu': nrm(28, (N_ODD, 2 * D_MODEL), 0.01),
        'w_ff_gate': nrm(29, (DEPTH, D_MODEL, D_FF), D_MODEL ** -0.5),
        'w_ff_up': nrm(30, (DEPTH, D_MODEL, D_FF), D_MODEL ** -0.5),
        'w_ff_down': nrm(31, (DEPTH, D_FF, D_MODEL), D_FF ** -0.5),
    }


def reference(x_prompt, x_sample, cache_conv, cache_kv_g0, cache_kv_g1, cache_kv_g2, state_s5,
              c_prompt, c_sample, norm_g, final_g, w_ada, b_ada, w_in, conv_w, conv_b, conv_ln_g,
              conv_ln_b, w_o, s5_lam_re, s5_lam_im, s5_log_dt, s5_b_re, s5_b_im, s5_c_re, s5_c_im,
              s5_d, s5_w_glu, s5_b_glu, w_ff_gate, w_ff_up, w_ff_down):
    W = dict(norm_g=norm_g, final_g=final_g, w_ada=w_ada, b_ada=b_ada, w_in=w_in, conv_w=conv_w,
             conv_b=conv_b, conv_ln_g=conv_ln_g, conv_ln_b=conv_ln_b, w_o=w_o,
             s5_lam_re=s5_lam_re, s5_lam_im=s5_lam_im, s5_log_dt=s5_log_dt, s5_b_re=s5_b_re,
             s5_b_im=s5_b_im, s5_c_re=s5_c_re, s5_c_im=s5_c_im, s5_d=s5_d, s5_w_glu=s5_w_glu,
             s5_b_glu=s5_b_glu, w_ff_gate=w_ff_gate, w_ff_up=w_ff_up, w_ff_down=w_ff_down)
    pos_p = jnp.arange(x_prompt.shape[1])
    pos_s = PAST_LEN + jnp.arange(x_sample.shape[1])
    y_prompt, conv_p, kvs_p, s5_p = run_trunk(x_prompt, c_prompt, pos_p, None, None, None, W)
    y_sample, conv_s, kvs_s, s5_s = run_trunk(x_sample, c_sample, pos_s, cache_conv,
                                              [cache_kv_g0, cache_kv_g1, cache_kv_g2], state_s5, W)
    kv0_p, kv1_p, kv2_p = kvs_p
    kv0_s, kv1_s, kv2_s = kvs_s
    return (y_prompt, y_sample, conv_p, kv0_p, kv1_p, kv2_p, s5_p,
            conv_s, kv0_s, kv1_s, kv2_s, s5_s)
```

```python
import contextlib

import numpy as np
import concourse.bass as bass
import concourse.mybir as mybir

F32 = mybir.dt.float32
BF16 = mybir.dt.bfloat16
AF = mybir.ActivationFunctionType
ALU = mybir.AluOpType
AX = mybir.AxisListType

EPOCH = 12000
NDSEM = 6


class Prog:
    def __init__(self, nc, stack):
        self.nc = nc
        self.stack = stack
        self.names = ['pe', 'dve', 'act', 'pool', 'sp']
        self.ops = {e: [] for e in self.names}
        self.cnt = {e: 0 for e in self.names}
        self.csem = {e: self._newsem() for e in self.names}
        self.dsem = {e: [self._newsem() for _ in range(NDSEM)] for e in ('sp', 'pool', 'act')}
        self.dcnt = {e: 0 for e in ('sp', 'pool', 'act')}
        self.dlast = {e: [0] * NDSEM for e in ('sp', 'pool', 'act')}
        self.last_w = {}
        self.readers = {}
        self.waited = {e: {} for e in self.names}
        self.pending = {e: [] for e in self.names}

    def _newsem(self):
        self.nsem = getattr(self, 'nsem', 0) + 1
        return self.stack.enter_context(self.nc.semaphore(name="sem%d" % self.nsem))

    def _deps(self, eng, r, w, is_dma=False):
        toks = []
        for k in r:
            t = self.last_w.get(k)
            if t is not None:
                toks.append((t, True))
        for k in w:
            t = self.last_w.get(k)
            if t is not None:
                toks.append((t, False))
            for t in self.readers.get(k, {}).values():
                toks.append((t, False))
        need = {}
        for (sem, val, teng, isdma), raw in toks:
            if teng == eng and not isdma and not raw and not is_dma:
                continue
            if self.waited[eng].get(id(sem), 0) >= val:
                continue
            cur = need.get(id(sem))
            if cur is None or cur[1] < val:
                need[id(sem)] = (sem, val)
        for sem, val in need.values():
            self.waited[eng][id(sem)] = val
        out = list(need.values()) + self.pending[eng]
        self.pending[eng] = []
        return out

    def _record(self, tok, r, w):
        for k in w:
            self.last_w[k] = tok
            self.readers[k] = {}
        for k in r:
            d = self.readers.setdefault(k, {})
            d[id(tok[0])] = tok

    def barrier(self):
        toks = []
        for e in self.names:
            if self.cnt[e] > 0:
                toks.append((self.csem[e], self.cnt[e], e))
        for q in self.dsem:
            for i, sem in enumerate(self.dsem[q]):
                if self.dlast[q][i] > 0:
                    toks.append((sem, self.dlast[q][i], None))
        for e in self.names:
            for s, v, te in toks:
                if te == e:
                    continue
                if self.waited[e].get(id(s), 0) < v:
                    self.pending[e].append((s, v))
                    self.waited[e][id(s)] = v

    def op(self, eng, fn, r=(), w=()):
        waits = self._deps(eng, r, w)
        if self.cnt[eng] >= EPOCH:
            self.csem[eng] = self._newsem()
            self.cnt[eng] = 0
        self.cnt[eng] += 1
        sem = self.csem[eng]
        tok = (sem, self.cnt[eng], eng, False)
        self.ops[eng].append((fn, waits, sem, 1))
        self._record(tok, r, w)
        return tok

    def dma(self, q, out, in_, r=(), w=(), **kw):
        waits = self._deps(q, r, w, True)
        n = self.dcnt[q]
        self.dcnt[q] += 1
        i = n % NDSEM
        sem = self.dsem[q][i]
        prev = self.dlast[q][i]
        if prev > 0 and self.waited[q].get(id(sem), 0) < prev:
            waits.append((sem, prev))
            self.waited[q][id(sem)] = prev
        val = prev + 16
        self.dlast[q][i] = val
        tok = (sem, val, q, True)
        self.ops[q].append((lambda e: e.dma_start(out=out, in_=in_, **kw), waits, sem, 16))
        self._record(tok, r, w)
        return tok

    def emit(self):
        nc = self.nc
        fin = {}
        for q in self.dsem:
            for i, sem in enumerate(self.dsem[q]):
                if self.dlast[q][i] > 0:
                    fin[id(sem)] = (sem, self.dlast[q][i])
        with nc.allow_non_contiguous_dma(reason="small strided parameter loads"), nc.Block() as block:
            def run(engname):
                def body(e):
                    for fn, waits, sem, inc in self.ops[engname]:
                        for s, v in waits:
                            e.wait_ge(s, v)
                        fn(e).then_inc(sem, inc)
                    if engname == 'sp':
                        for s, v in fin.values():
                            e.wait_ge(s, v)
                return body
            block.tensor(run('pe'))
            block.vector(run('dve'))
            block.scalar(run('act'))
            block.gpsimd(run('pool'))
            block.sync(run('sp'))


D = 1024
NCH = 8
DFF = 2816
NFF = 22
INC = 3328
EPS = 1e-6
NS = 128
DILS = ((128, 1), (512, 4), (2048, 16))


class Ctx:
    pass


def build(SEQ=4096, LAYERS=4, stop_after=None):
    from concourse.bass_utils import run_bass_kernel_spmd
    nc = bass.Bass("TRN2", target_bir_lowering=False)
    NT = SEQ + NS
    NSUB = SEQ // 128
    NE = 2
    NO = 2
    keep = [min(w, SEQ) for w, _ in DILS]

    def din(name, shape, dt=F32):
        return nc.dram_tensor(name, list(shape), dt, kind="ExternalInput")

    def dout(name, shape):
        return nc.dram_tensor(name, list(shape), F32, kind="ExternalOutput")

    def dscr(name, shape, dt=F32):
        return nc.dram_tensor(name, list(shape), dt, kind="Internal")

    I = {}
    for name, shape in [
        ("xp", (SEQ, D)), ("xs", (NS, D)), ("call", (17, D)),
        ("cconv", (NE, 480, 512)), ("ckv0", (NE, 16, 128, 512)), ("ckv1", (NE, 16, 512, 512)),
        ("ckv2", (NE, 16, 2048, 512)), ("st5", (NO, 16, 8192)),
        ("norm_g", (4, 2, D)), ("final_g", (1, D)), ("w_ada", (4, D, 6 * D)), ("b_ada", (4, 6 * D)),
        ("w_in", (NE, D, INC)), ("conv_w", (NE, 31, 512)), ("conv_b", (NE, 512)),
        ("conv_ln_g", (NE, 512)), ("conv_ln_b", (NE, 512)), ("w_o", (NE, 768, D)),
        ("s5_lam_re", (NO, 64, 64)), ("s5_lam_im", (NO, 64, 64)), ("s5_log_dt", (NO, 64)),
        ("s5_b_re", (NO, 64, 64, 16)), ("s5_b_im", (NO, 64, 64, 16)),
        ("s5_c_re", (NO, 64, 16, 64)), ("s5_c_im", (NO, 64, 16, 64)),
        ("s5_d", (NO, D)), ("s5_w_glu", (NO, D, 2 * D)), ("s5_b_glu", (NO, 2 * D)),
        ("w_ff_gate", (4, D, DFF)), ("w_ff_up", (4, D, DFF)), ("w_ff_down", (4, DFF, D)),
        ("c_ident", (128, 128)), ("c_ropep", (128, NSUB, 16)), ("c_ropes", (128, 16)),
        ("c_amask", (128, 2, 128)), ("c_smask", (128, 128 + 128 + 24)), ("c_m01", (128, 128)),
    ]:
        I[name] = din(name, shape)
    O = {}
    for name, shape in [
        ("y_p", (SEQ, D)), ("y_s", (NS, D)), ("conv_p", (NE, 30, 512)),
        ("kv0_p", (NE, keep[0], 512)), ("kv1_p", (NE, keep[1], 512)), ("kv2_p", (NE, keep[2], 512)),
        ("s5_p", (NO, 32, 256)), ("conv_s", (NE, 16, 30, 512)),
        ("kv0_s", (NE, NS, 512)), ("kv1_s", (NE, NS, 512)), ("kv2_s", (NE, NS, 512)),
        ("s5_s", (NO, 16, 8192)),
    ]:
        O[name] = dout(name, shape)
    XF = dscr("XF", (D, NT))
    YC = dscr("YC", (512, NT), BF16)
    QKV = dscr("QKVs", (SEQ, 2304), BF16)
    ATT = dscr("ATTs", (3, NT, 260))

    with contextlib.ExitStack() as gst:
        P = Prog(nc, gst)
        cnt = [0]

        def sb(st, shape, dt=F32):
            cnt[0] += 1
            return st.enter_context(nc.sbuf_tensor("t%d" % cnt[0], list(shape), dt))

        PS = [gst.enter_context(nc.psum_tensor("ps%d" % i, [128, 512], F32)) for i in range(7)]
        PSB = gst.enter_context(nc.psum_tensor("psb", [128, 1024], BF16))
        psk = ["ps%d" % i for i in range(7)]

        def MM(out, lhsT, rhs, start, stop, r, w):
            return P.op('pe', lambda e: e.matmul(out, lhsT, rhs, start=start, stop=stop), r, w)

        def TR(out, in_, idt, r, w):
            return P.op('pe', lambda e: e.transpose(out, in_, idt), r, w)

        def TT(eng, out, a, b, op, r, w):
            return P.op(eng, lambda e: e.tensor_tensor(out, a, b, op), r, w)

        def TS(eng, out, a, s1, s2, op0, op1, r, w):
            if s2 is None:
                return P.op(eng, lambda e: e.tensor_scalar(out, a, s1, None, op0), r, w)
            return P.op(eng, lambda e: e.tensor_scalar(out, a, s1, s2, op0, op1), r, w)

        def STT(out, a, s, b, op0, op1, r, w):
            return P.op('dve', lambda e: e.scalar_tensor_tensor(out, a, s, b, op0, op1), r, w)

        def ACT(out, in_, func, r, w, scale=1.0, bias=None):
            if bias is None:
                return P.op('act', lambda e: e.activation(out, in_, func, scale=scale), r, w)
            return P.op('act', lambda e: e.activation(out, in_, func, scale=scale, bias=bias), r, w)

        def CP(eng, out, in_, r, w):
            if eng == 'act':
                return P.op('act', lambda e: e.activation(out, in_, AF.Copy), r, w)
            return P.op(eng, lambda e: e.tensor_copy(out, in_), r, w)

        def MS(eng, out, val, w):
            return P.op(eng, lambda e: e.memset(out, val), (), w)

        def RED(out, in_, op, r, w, eng='dve'):
            return P.op(eng, lambda e: e.tensor_reduce(out, in_, AX.X, op), r, w)

        def RCP(out, in_, r, w):
            return P.op('dve', lambda e: e.reciprocal(out, in_), r, w)

        def SCAN(out, d0, d1, init, r, w):
            return P.op('dve', lambda e: e.tensor_tensor_scan(out, d0, d1, init, ALU.mult, ALU.add), r, w)

        def DMA(q, out, in_, r, w):
            return P.dma(q, out, in_, r, w)

        ident = sb(gst, [128, 128]); identb = sb(gst, [128, 128], BF16); identn = sb(gst, [128, 128])
        onesb = sb(gst, [128, 128], BF16); onesf = sb(gst, [128, 128])
        epsT = sb(gst, [128, 1])
        MOD = sb(gst, [128, 4, 48, 17])
        m01 = sb(gst, [128, 128])
        DMA('sp', ident[:], I["c_ident"].ap(), (), ['ident'])
        DMA('sp', m01[:], I["c_m01"].ap(), (), ['m01'])
        CP('dve', identb[:], ident[:], ['ident'], ['identb'])
        TS('dve', identn[:], ident[:], -1.0, None, ALU.mult, None, ['ident'], ['identn'])
        MS('pool', onesb[:], 1.0, ['onesb'])
        MS('pool', onesf[:], 1.0 / 512.0, ['onesf'])
        MS('pool', epsT[:], EPS, ['epsT'])

        XFv = XF.ap().rearrange("(c p) t -> p c t", p=128)

        def modbc(l, i, T, sample):
            a = MOD[:, l, i * 8:(i + 1) * 8, :]
            if not sample:
                return a[:, :, 0:1].to_broadcast([128, 8, T])
            return a[:, :, 1:17].unsqueeze(3).to_broadcast([128, 8, 16, 8])

        def modbc1(l, i, m, T, sample):
            a = MOD[:, l, i * 8 + m, :]
            if not sample:
                return a[:, 0:1].to_broadcast([128, T])
            return a[:, 1:17].unsqueeze(2).to_broadcast([128, 16, 8])

        def v3(ap, sample):
            return ap.rearrange("p c (s t) -> p c s t", t=8) if sample else ap

        def v2(ap, sample):
            return ap.rearrange("p (s t) -> p s t", t=8) if sample else ap

        with contextlib.ExitStack() as st:
            ct = sb(st, [17, D]); sct = sb(st, [17, D]); scT = sb(st, [128, 8, 17])
            DMA('sp', ct[:], I["call"].ap(), (), ['ct'])
            ACT(sct[:], ct[:], AF.Silu, ['ct'], ['sct'])
            for c in range(8):
                TR(PS[0][:, c * 17:(c + 1) * 17], sct[:, c * 128:(c + 1) * 128], ident[0:17, 0:17],
                   ['sct', 'ident'], [psk[0]])
            CP('dve', scT[:].rearrange("p c s -> p (c s)"), PS[0][:, 0:136], [psk[0]], ['scT'])
            slabs = [sb(st, [128, 8, 512]) for _ in range(2)]
            bts = [sb(st, [17, 512]) for _ in range(2)]
            mts = [sb(st, [17, 512]) for _ in range(2)]
            n = 0
            for l in range(4):
                for j in range(12):
                    b = n % 2
                    DMA('sp' if n % 2 == 0 else 'act', slabs[b][:],
                        I["w_ada"].ap()[l].rearrange("(k p) n -> p k n", p=128)[:, :, j * 512:(j + 1) * 512],
                        (), ['slab%d' % b])
                    DMA('pool', bts[b][:], I["b_ada"].ap()[l:l + 1, j * 512:(j + 1) * 512].to_broadcast([17, 512]),
                        (), ['bt%d' % b])
                    pm = PS[1 + b]
                    for k in range(8):
                        MM(pm[0:17, :], scT[:, k, :], slabs[b][:, k, :], k == 0, k == 7,
                           ['scT', 'slab%d' % b], [psk[1 + b]])
                    TT('dve', mts[b][:], pm[0:17, :], bts[b][:], ALU.add, [psk[1 + b], 'bt%d' % b], ['mt%d' % b])
                    pt = PS[3 + b]
                    for q in range(4):
                        TR(pt[:, q * 17:(q + 1) * 17], mts[b][:, q * 128:(q + 1) * 128], ident[0:17, 0:17],
                           ['mt%d' % b, 'ident'], [psk[3 + b]])
                    CP('act', MOD[:, l, 4 * j:4 * j + 4, :].rearrange("p c s -> p (c s)"), pt[:, 0:68],
                       [psk[3 + b]], ['MOD'])
                    n += 1
            ng = sb(st, [128, 4, 2, 8])
            for l in range(4):
                for i2 in range(2):
                    DMA('sp', ng[:, l, i2, :], I["norm_g"].ap()[l, i2].rearrange("(c p) -> p c", p=128),
                        (), ['ng'])
            for l in range(4):
                for i2, idx in ((0, 1), (1, 4)):
                    a = MOD[:, l, idx * 8:(idx + 1) * 8, :]
                    TS('dve', a, a, 1.0, None, ALU.add, None, ['MOD'], ['MOD'])
                    TT('dve', a, a, ng[:, l, i2, :].unsqueeze(2).to_broadcast([128, 8, 17]), ALU.mult,
                       ['MOD', 'ng'], ['MOD'])
        P.barrier()

        with contextlib.ExitStack() as st:
            xin = [sb(st, [128, D]) for _ in range(2)]
            xo = [sb(st, [128, 8, 128]) for _ in range(2)]
            for n in range(NSUB + 1):
                b = n % 2
                src = I["xp"].ap()[n * 128:(n + 1) * 128, :] if n < NSUB else I["xs"].ap()
                DMA('sp', xin[b][:], src, (), ['xin%d' % b])
                for hh in range(2):
                    pt = PS[2 * b + hh]
                    for q in range(4):
                        c = hh * 4 + q
                        TR(pt[:, q * 128:(q + 1) * 128], xin[b][:, c * 128:(c + 1) * 128], ident[:],
                           ['xin%d' % b, 'ident'], [psk[2 * b + hh]])
                    CP('dve' if hh == 0 else 'act', xo[b][:, hh * 4:hh * 4 + 4, :].rearrange("p c t -> p (c t)"),
                       pt[:], [psk[2 * b + hh]], ['xo%d' % b])
                DMA('pool', XFv[:, :, n * 128:(n + 1) * 128], xo[b][:], ['xo%d' % b], ['XF'])
        P.barrier()

        wl = [0]

        def load_w(dst, w_ap, K, N, key, stg):
            for k in range(K):
                b = wl[0] % 2
                eng = ('dve', 'act', 'pool')[wl[0] % 3]
                wl[0] += 1
                DMA('sp' if b == 0 else 'act', stg[b][:, 0:N], w_ap[k * 128:(k + 1) * 128, :], (), ['stg%d' % b])
                CP(eng, dst[:, k, :], stg[b][:, 0:N], ['stg%d' % b], [key])

        def norm_mod(xt, sqb, hb, rstd, T, l, i_sh, sample, hf=None, kx='xt', kh='hb'):
            ACT(sqb[:, :, 0:T], xt[:, :, 0:T], AF.Square, [kx], ['sqb'])
            for c in range(8):
                MM(PS[0][:, 0:T], onesb[:], sqb[:, c, 0:T], c == 0, c == 7, ['onesb', 'sqb'], [psk[0]])
            ACT(rstd[:, 0:T], PS[0][:, 0:T], AF.Sqrt, [psk[0], 'epsT'], ['rstd'], scale=1.0 / D, bias=epsT[:])
            RCP(rstd[:, 0:T], rstd[:, 0:T], ['rstd'], ['rstd'])
            x3 = xt[:, :, 0:T]
            TT('dve', x3, x3, rstd[:, 0:T].unsqueeze(1).to_broadcast([128, 8, T]), ALU.mult, [kx, 'rstd'], [kx])
            TT('pool', v3(x3, sample), v3(x3, sample), modbc(l, i_sh + 1, T, sample), ALU.mult, [kx, 'MOD'], [kx])
            if hf is not None:
                TT('dve', v3(hf[:, :, 0:T], sample), v3(x3, sample), modbc(l, i_sh, T, sample), ALU.add,
                   [kx, 'MOD'], ['hf'])
                CP('act', hb[:, :, 0:T], hf[:, :, 0:T], ['hf'], [kh])
            else:
                TT('dve', v3(hb[:, :, 0:T], sample), v3(x3, sample), modbc(l, i_sh, T, sample), ALU.add,
                   [kx, 'MOD'], [kh])

        def tiles(T):
            out = [(c0, T, False) for c0 in range(0, SEQ, T)]
            out.append((SEQ, NS, True))
            return out

        def even_E1(l, ei, st):
            T0 = 512
            win = sb(st, [128, 8, INC], BF16)
            with contextlib.ExitStack() as st2:
                stg = [sb(st2, [128, INC]) for _ in range(2)]
                load_w(win, I["w_in"].ap()[ei], 8, INC, 'win', stg)
                P.barrier()
            cw = sb(st, [128, 4, 31]); cb = sb(st, [128, 4]); lg = sb(st, [128, 4]); lb = sb(st, [128, 4])
            for ch in range(4):
                DMA('pool', cw[:, ch, :], I["conv_w"].ap()[ei].rearrange("j c -> c j")[ch * 128:(ch + 1) * 128, :],
                    (), ['cw'])
            for t_, nm in ((cb, "conv_b"), (lg, "conv_ln_g"), (lb, "conv_ln_b")):
                DMA('pool', t_[:], I[nm].ap()[ei].rearrange("(c p) -> p c", p=128), (), ['cw'])
            ropep = sb(st, [128, NSUB, 16]); ropes = sb(st, [128, 16])
            DMA('pool', ropep[:], I["c_ropep"].ap(), (), ['rope'])
            DMA('pool', ropes[:], I["c_ropes"].ap(), (), ['rope'])
            xt = sb(st, [128, 8, T0]); sqb = sb(st, [128, 8, T0], BF16); hb = sb(st, [128, 8, T0], BF16)
            rstd = sb(st, [128, T0])
            uext = sb(st, [128, 4, 30 + T0]); uxs = sb(st, [128, 4, 16, 38])
            sg = sb(st, [128, T0]); yc = sb(st, [128, 4, T0]); ycb = sb(st, [128, 4, T0], BF16)
            lnr = sb(st, [128, T0]); sq4 = sb(st, [128, 4, T0])
            qkv = sb(st, [128, 2304]); qkvb = sb(st, [128, 2304], BF16); rt = sb(st, [128, 4, 24, 8])
            cvt = sb(st, [128, 512]); cin = sb(st, [120, 512]); cmp_ = sb(st, [128, 4, 16, 30])
            MS('pool', uext[:, :, 0:30], 0.0, ['uext'])
            for q4 in range(4):
                DMA('sp', cin[:], I["cconv"].ap()[ei, q4 * 120:(q4 + 1) * 120, :], (), ['cin'])
                for ch in range(4):
                    TR(PS[1][:, ch * 120:(ch + 1) * 120], cin[:, ch * 128:(ch + 1) * 128], ident[0:120, 0:120],
                       ['cin', 'ident'], [psk[1]])
                CP('dve', uxs[:, :, q4 * 4:(q4 + 1) * 4, 0:30],
                   PS[1][:, 0:480].rearrange("p (c s j) -> p c s j", c=4, s=4), [psk[1]], ['uxs'])
            for (c0, T, sample) in tiles(T0):
                DMA('sp', xt[:, :, 0:T], XFv[:, :, c0:c0 + T], ['XF'], ['xt'])
                norm_mod(xt, sqb, hb, rstd, T, l, 0, sample)
                for ch in range(4):
                    pv, pg = PS[1 + (ch % 2) * 2], PS[2 + (ch % 2) * 2]
                    kv_, kg_ = psk[1 + (ch % 2) * 2], psk[2 + (ch % 2) * 2]
                    for k in range(8):
                        MM(pv[:, 0:T], win[:, k, ch * 128:(ch + 1) * 128], hb[:, k, 0:T], k == 0, k == 7,
                           ['win', 'hb'], [kv_])
                    for k in range(8):
                        MM(pg[:, 0:T], win[:, k, 512 + ch * 128:512 + (ch + 1) * 128], hb[:, k, 0:T], k == 0, k == 7,
                           ['win', 'hb'], [kg_])
                    ACT(sg[:, 0:T], pg[:, 0:T], AF.Sigmoid, [kg_], ['sg'])
                    if not sample:
                        TT('dve', uext[:, ch, 30:30 + T], pv[:, 0:T], sg[:, 0:T], ALU.mult, [kv_, 'sg'], ['uext'])
                    else:
                        TT('dve', uxs[:, ch, :, 30:38], v2(pv[:, 0:T], True), v2(sg[:, 0:T], True), ALU.mult,
                           [kv_, 'sg'], ['uxs'])
                for s4 in range(T // 128):
                    n = (c0 // 128) + s4
                    for nb in range(5):
                        wd = 512 if nb < 4 else 256
                        pq = PS[1 + (nb % 4)]
                        for k in range(8):
                            MM(pq[:, 0:wd], hb[:, k, s4 * 128:(s4 + 1) * 128],
                               win[:, k, 1024 + nb * 512:1024 + nb * 512 + wd], k == 0, k == 7,
                               ['hb', 'win'], [psk[1 + (nb % 4)]])
                        CP('act', qkv[:, nb * 512:nb * 512 + wd], pq[:, 0:wd],
                           [psk[1 + (nb % 4)]], ['qkv'])
                    qk = qkv[:, 0:1536].rearrange("p (h d) -> p h d", d=64)
                    x1, x2 = qk[:, :, 0:8], qk[:, :, 8:16]
                    rp = ropes[:] if sample else ropep[:, n, :]
                    cs = rp[:, 0:8].unsqueeze(1).to_broadcast([128, 24, 8])
                    sn = rp[:, 8:16].unsqueeze(1).to_broadcast([128, 24, 8])
                    TT('pool', rt[:, 0], x1, cs, ALU.mult, ['qkv', 'rope'], ['rt'])
                    TT('pool', rt[:, 1], x2, sn, ALU.mult, ['qkv', 'rope'], ['rt'])
                    TT('pool', rt[:, 2], x2, cs, ALU.mult, ['qkv', 'rope'], ['rt'])
                    TT('pool', rt[:, 3], x1, sn, ALU.mult, ['qkv', 'rope'], ['rt'])
                    TT('pool', x1, rt[:, 0], rt[:, 1], ALU.subtract, ['rt'], ['qkv'])
                    TT('pool', x2, rt[:, 2], rt[:, 3], ALU.add, ['rt'], ['qkv'])
                    for g in range(3):
                        kcols = qkv[:, 768 + 256 * g:768 + 256 * (g + 1)]
                        vcols = qkv[:, 1536 + 256 * g:1536 + 256 * (g + 1)]
                        if sample:
                            dsto = O["kv%d_s" % g].ap()[ei]
                        else:
                            r0 = n * 128 - (SEQ - keep[g])
                            if r0 < 0:
                                continue
                            dsto = O["kv%d_p" % g].ap()[ei, r0:r0 + 128, :]
                        DMA('pool', dsto[:, 0:256], kcols, ['qkv'], ['o_kv'])
                        DMA('pool', dsto[:, 256:512], vcols, ['qkv'], ['o_kv'])
                    if not sample:
                        CP('act', qkvb[:], qkv[:], ['qkv'], ['qkvb'])
                        DMA('sp', QKV.ap()[n * 128:(n + 1) * 128, :], qkvb[:], ['qkvb'], ['QKV'])
                    else:
                        CP('act', C.qs[:], qkv[:], ['qkv'], ['qs'])
                for ch in range(4):
                    if not sample:
                        src = lambda j: uext[:, ch, j:j + T]
                        dst = yc[:, ch, 0:T]
                        uk = 'uext'
                    else:
                        src = lambda j: uxs[:, ch, :, j:j + 8]
                        dst = v2(yc[:, ch, 0:T], True)
                        uk = 'uxs'
                    TS('dve', dst, src(0), cw[:, ch, 0:1], cb[:, ch:ch + 1], ALU.mult, ALU.add, [uk, 'cw'], ['yc'])
                    for j in range(1, 31):
                        STT(dst, src(j), cw[:, ch, j:j + 1], dst, ALU.mult, ALU.add, [uk, 'cw', 'yc'], ['yc'])
                for ch in range(4):
                    MM(PS[5][:, 0:T], onesf[:], yc[:, ch, 0:T], ch == 0, ch == 3, ['onesf', 'yc'], [psk[5]])
                TT('dve', yc[:, :, 0:T], yc[:, :, 0:T], PS[5][:, 0:T].unsqueeze(1).to_broadcast([128, 4, T]),
                   ALU.subtract, ['yc', psk[5]], ['yc'])
                ACT(sq4[:, :, 0:T], yc[:, :, 0:T], AF.Square, ['yc'], ['sq4'])
                for ch in range(4):
                    MM(PS[5][:, 0:T], onesf[:], sq4[:, ch, 0:T], ch == 0, ch == 3, ['onesf', 'sq4'], [psk[5]])
                ACT(lnr[:, 0:T], PS[5][:, 0:T], AF.Sqrt, [psk[5], 'epsT'], ['lnr'], bias=epsT[:])
                RCP(lnr[:, 0:T], lnr[:, 0:T], ['lnr'], ['lnr'])
                TT('dve', yc[:, :, 0:T], yc[:, :, 0:T], lnr[:, 0:T].unsqueeze(1).to_broadcast([128, 4, T]), ALU.mult,
                   ['yc', 'lnr'], ['yc'])
                TT('pool', yc[:, :, 0:T], yc[:, :, 0:T], lg[:].unsqueeze(2).to_broadcast([128, 4, T]), ALU.mult,
                   ['yc', 'cw'], ['yc'])
                TT('pool', yc[:, :, 0:T], yc[:, :, 0:T], lb[:].unsqueeze(2).to_broadcast([128, 4, T]), ALU.add,
                   ['yc', 'cw'], ['yc'])
                ACT(ycb[:, :, 0:T], yc[:, :, 0:T], AF.Silu, ['yc'], ['ycb'])
                DMA('pool', YC.ap().rearrange("(c p) t -> p c t", p=128)[:, :, c0:c0 + T], ycb[:, :, 0:T],
                    ['ycb'], ['YC'])
                if not sample:
                    if c0 + T == SEQ:
                        for ch in range(4):
                            TR(PS[6][0:30, ch * 128:(ch + 1) * 128], uext[:, ch, T:T + 30], ident[:],
                               ['uext', 'ident'], [psk[6]])
                        CP('dve', cvt[0:30, :], PS[6][0:30, :], [psk[6]], ['cvt'])
                        DMA('pool', O["conv_p"].ap()[ei], cvt[0:30, :], ['cvt'], ['o_convp'])
                    CP('pool', uext[:, :, 0:30], uext[:, :, T:T + 30], ['uext'], ['uext'])
                else:
                    CP('dve', cmp_[:], uxs[:, :, :, 8:38], ['uxs'], ['cmp'])
                    for q4 in range(4):
                        for ch in range(4):
                            TR(PS[6][0:120, ch * 128:(ch + 1) * 128],
                               cmp_[:, ch, q4 * 4:(q4 + 1) * 4, :].rearrange("p s j -> p (s j)"), ident[:],
                               ['cmp', 'ident'], [psk[6]])
                        CP('dve', cvt[0:120, :], PS[6][0:120, :], [psk[6]], ['cvt'])
                        DMA('pool', O["conv_s"].ap()[ei, q4 * 4:(q4 + 1) * 4].rearrange("s j c -> (s j) c"),
                            cvt[0:120, :], ['cvt'], ['o_convs'])

        C = Ctx()

        def even_E2(st):
            am = sb(st, [128, 2, 128]); amb = sb(st, [128, 2, 128], BF16)
            DMA('sp', am[:], I["c_amask"].ap(), (), ['am'])
            CP('dve', amb[:], am[:], ['am'], ['amb'])
            qk_in = [sb(st, [128, 2, 256], BF16) for _ in range(2)]
            vx = [sb(st, [128, 4, 65], BF16) for _ in range(3)]
            qkT = [sb(st, [128, 4, 128], BF16) for _ in range(3)]
            pT = [sb(st, [128, 2, 128], BF16) for _ in range(4)]
            ob = [sb(st, [128, 260]) for _ in range(2)]
            for i in range(3):
                MS('pool', vx[i][:, :, 64:65], 1.0, ['vx%d' % i])
            blk = 0
            for g, (win_, dil) in enumerate(DILS):
                nb = SEQ // dil // 128
                qv = QKV.ap().rearrange("(b i d) c -> d b i c", i=128, d=dil)
                av = ATT.ap()[g, 0:SEQ, :].rearrange("(b i d) c -> d b i c", i=128, d=dil)
                for r in range(dil):
                    for b in range(nb):
                        ib = blk % 2
                        cur = blk % 3
                        prv = (blk - 1) % 3
                        rows = qv[r, b]
                        DMA('sp', qk_in[ib][:, 0, :], rows[:, 256 * g:256 * (g + 1)], ['QKV'], ['qkin%d' % ib])
                        DMA('sp', qk_in[ib][:, 1, :], rows[:, 768 + 256 * g:768 + 256 * (g + 1)], ['QKV'],
                            ['qkin%d' % ib])
                        DMA('sp', vx[cur][:, :, 0:64],
                            rows[:, 1536 + 256 * g:1536 + 256 * (g + 1)].rearrange("p (h d) -> p h d", d=64),
                            ['QKV'], ['vx%d' % cur])
                        for c in range(4):
                            TR(PSB[:, c * 128:(c + 1) * 128], qk_in[ib][:, c // 2, (c % 2) * 128:(c % 2 + 1) * 128],
                               identb[:], ['qkin%d' % ib, 'identb'], ['psb'])
                        CP('dve', qkT[cur][:].rearrange("p c t -> p (c t)"), PSB[:, 0:512], ['psb'], ['qkT%d' % cur])
                        po = PS[5 + ib]
                        lo = 0 if b > 0 else 1
                        for h in range(4):
                            c, hp = h // 2, (h % 2) * 64
                            pss = PS[1 + h]
                            ksk = psk[1 + h]
                            MM(pss[:, 128:256], qkT[cur][hp:hp + 64, 2 + c, :], qkT[cur][hp:hp + 64, c, :], True, False,
                               ['qkT%d' % cur], [ksk])
                            MM(pss[:, 128:256], identb[:], amb[:, 1, :], False, True, ['identb', 'amb'], [ksk])
                            if b > 0:
                                MM(pss[:, 0:128], qkT[prv][hp:hp + 64, 2 + c, :], qkT[cur][hp:hp + 64, c, :], True, False,
                                   ['qkT%d' % cur, 'qkT%d' % prv], [ksk])
                                MM(pss[:, 0:128], identb[:], amb[:, 0, :], False, True, ['identb', 'amb'], [ksk])
                            ACT(pT[h][:, lo:2, :].rearrange("p a t -> p (a t)"), pss[:, lo * 128:256], AF.Exp,
                                [ksk], ['pT%d' % h], scale=0.125)
                        for h in range(4):
                            pt_ = pT[h]
                            ptk = 'pT%d' % h
                            if b > 0:
                                MM(po[:, h * 65:(h + 1) * 65], pt_[:, 0, :], vx[prv][:, h, :], True, False,
                                   [ptk, 'vx%d' % prv], [psk[5 + ib]])
                            MM(po[:, h * 65:(h + 1) * 65], pt_[:, 1, :], vx[cur][:, h, :], b == 0, True,
                               [ptk, 'vx%d' % cur], [psk[5 + ib]])
                        CP('dve', ob[ib][:], po[:, 0:260], [psk[5 + ib]], ['ob%d' % ib])
                        DMA('pool', av[r, b], ob[ib][:], ['ob%d' % ib], ['ATT'])
                        blk += 1
                    blk += 1

        def even_SA(ei, st):
            KC = 16
            sm = sb(st, [128, 280]); DMA('sp', sm[:], I["c_smask"].ap(), (), ['sm'])
            kt = [sb(st, [128, KC, 256]) for _ in range(4)]
            prod = sb(st, [128, KC, 256])
            S = sb(st, [128, 136, 4]); Pm = sb(st, [128, 136, 4])
            kn = sb(st, [128, 8, 512])
            mx = sb(st, [128, 4]); ls = sb(st, [128, 4]); lse = sb(st, [128, 3, 4]); og = sb(st, [128, 3, 256])
            part = sb(st, [128, 256]); wts = sb(st, [128, 3, 4]); mm_ = sb(st, [128, 4]); den = sb(st, [128, 4])
            att = sb(st, [128, 260]); zer = sb(st, [128, 260])
            caches = [I["ckv0"], I["ckv1"], I["ckv2"]]
            wbs = [128, 512, 2048]
            n = 0
            for g in range(3):
                wb, dil = wbs[g], DILS[g][1]
                base = ei * 16 * wb * 512

                def cache_load(q, dst, kc, off, key):
                    m0 = kc * KC
                    for t in range(8):
                        if g == 0:
                            row0, rs = m0, 1
                        elif g == 1:
                            row0, rs = (t % 4) + 4 * m0, 4
                        else:
                            row0, rs = t + 16 * m0, 16
                        src = bass.AP(caches[g], base + row0 * 512 + off, [[wb * 512, 16], [rs * 512, KC], [1, 256]])
                        DMA(q, dst[t::8], src, (), [key])

                for t in range(8):
                    DMA('sp', kn[t::8], bass.AP(O["kv%d_s" % g], ei * NS * 512, [[8 * 512, 16], [512, 8], [1, 512]]),
                        ['o_kv'], ['kn'])
                qg = C.qs[:, 256 * g:256 * (g + 1)]
                for kc in range(8):
                    b = n % 4; n += 1
                    cache_load('sp' if b % 2 == 0 else 'act', kt[b], kc, 0, 'kt%d' % b)
                    TT('pool', prod[:], kt[b][:], qg.unsqueeze(1).to_broadcast([128, KC, 256]), ALU.mult,
                       ['kt%d' % b, 'qs'], ['prod'])
                    RED(S[:, kc * KC:(kc + 1) * KC, :].rearrange("p k h -> p (k h)"),
                        prod[:].rearrange("p k (h d) -> p (k h) d", d=64), ALU.add, ['prod'], ['S'])
                TT('pool', prod[:, 0:8, :], kn[:, :, 0:256], qg.unsqueeze(1).to_broadcast([128, 8, 256]), ALU.mult,
                   ['kn', 'qs'], ['prod'])
                RED(S[:, 128:136, :].rearrange("p k h -> p (k h)"),
                    prod[:, 0:8, :].rearrange("p k (h d) -> p (k h) d", d=64), ALU.add, ['prod'], ['S'])
                if g < 2:
                    TT('dve', S[:, 0:128, :], S[:, 0:128, :],
                       sm[:, 128 * g:128 * (g + 1)].unsqueeze(2).to_broadcast([128, 128, 4]), ALU.add, ['S', 'sm'], ['S'])
                TT('dve', S[:, 128:136, :], S[:, 128:136, :],
                   sm[:, 256 + 8 * g:256 + 8 * (g + 1)].unsqueeze(2).to_broadcast([128, 8, 4]), ALU.add, ['S', 'sm'], ['S'])
                Sv = S[:].rearrange("p k h -> p h k")
                RED(mx[:], Sv, ALU.max, ['S'], ['mx'])
                TT('dve', S[:], S[:], mx[:].unsqueeze(1).to_broadcast([128, 136, 4]), ALU.subtract, ['S', 'mx'], ['S'])
                ACT(Pm[:], S[:], AF.Exp, ['S'], ['Pm'], scale=0.125)
                RED(ls[:], Pm[:].rearrange("p k h -> p h k"), ALU.add, ['Pm'], ['ls'])
                ACT(lse[:, g, :], ls[:], AF.Ln, ['ls'], ['lse'])
                STT(lse[:, g, :], mx[:], 0.125, lse[:, g, :], ALU.mult, ALU.add, ['mx', 'lse'], ['lse'])
                RCP(ls[:], ls[:], ['ls'], ['ls'])
                TT('dve', Pm[:], Pm[:], ls[:].unsqueeze(1).to_broadcast([128, 136, 4]), ALU.mult, ['Pm', 'ls'], ['Pm'])
                for kc in range(9):
                    if kc < 8:
                        b = n % 4; n += 1
                        cache_load('sp' if b % 2 == 0 else 'act', kt[b], kc, 256, 'kt%d' % b)
                        vsrc, nk, vk = kt[b][:], KC, 'kt%d' % b
                        pp = Pm[:, kc * KC:(kc + 1) * KC, :]
                    else:
                        vsrc, nk, vk = kn[:, :, 256:512], 8, 'kn'
                        pp = Pm[:, 128:136, :]
                    TT('pool', prod[:, 0:nk, :].rearrange("p k (h d) -> p k h d", d=64),
                       vsrc.rearrange("p k (h d) -> p k h d", d=64),
                       pp.unsqueeze(3).to_broadcast([128, nk, 4, 64]), ALU.mult, [vk, 'Pm'], ['prod'])
                    dst = og[:, g, :] if kc == 0 else part[:]
                    RED(dst, prod[:, 0:nk, :].rearrange("p k c -> p c k"), ALU.add, ['prod'], ['og' if kc == 0 else 'part'])
                    if kc > 0:
                        TT('dve', og[:, g, :], og[:, g, :], part[:], ALU.add, ['og', 'part'], ['og'])
            RED(mm_[:], lse[:].rearrange("p g h -> p h g"), ALU.max, ['lse'], ['mm'])
            TT('dve', wts[:], lse[:], mm_[:].unsqueeze(1).to_broadcast([128, 3, 4]), ALU.subtract, ['lse', 'mm'], ['wts'])
            ACT(wts[:], wts[:], AF.Exp, ['wts'], ['wts'])
            RED(den[:], wts[:].rearrange("p g h -> p h g"), ALU.add, ['wts'], ['den'])
            RCP(den[:], den[:], ['den'], ['den'])
            TT('dve', wts[:], wts[:], den[:].unsqueeze(1).to_broadcast([128, 3, 4]), ALU.mult, ['wts', 'den'], ['wts'])
            TT('dve', og[:].rearrange("p g (h d) -> p g h d", d=64), og[:].rearrange("p g (h d) -> p g h d", d=64),
               wts[:].unsqueeze(3).to_broadcast([128, 3, 4, 64]), ALU.mult, ['og', 'wts'], ['og'])
            RED(att[:, 0:256], og[:].rearrange("p g c -> p c g"), ALU.add, ['og'], ['att'])
            MS('pool', att[:, 256:260], 1.0, ['att'])
            MS('pool', zer[:], 0.0, ['zer'])
            a65 = sb(st, [128, 260])
            CP('dve', a65[:].rearrange("p (h e) -> p h e", e=65)[:, :, 0:64],
               att[:, 0:256].rearrange("p (h d) -> p h d", d=64), ['att'], ['a65'])
            MS('pool', a65[:].rearrange("p (h e) -> p h e", e=65)[:, :, 64:65], 1.0, ['a65'])
            DMA('pool', ATT.ap()[0, SEQ:NT, :], a65[:], ['a65'], ['ATT'])
            DMA('pool', ATT.ap()[1, SEQ:NT, :], zer[:], ['zer'], ['ATT'])
            DMA('pool', ATT.ap()[2, SEQ:NT, :], zer[:], ['zer'], ['ATT'])

        def even_E3(l, ei, st):
            T0 = 512
            wo = sb(st, [128, 6, D], BF16)
            stg = [sb(st, [128, D]) for _ in range(2)]
            load_w(wo, I["w_o"].ap()[ei], 6, D, 'wo', stg)
            xt = sb(st, [128, 8, T0]); ycb = sb(st, [128, 4, T0], BF16); afm = sb(st, [128, 2, T0], BF16)
            a3 = [sb(st, [128, 3, 260]) for _ in range(2)]; atm = sb(st, [128, 256]); rc = sb(st, [128, 4])
            tmp = sb(st, [128, T0])
            YCv = YC.ap().rearrange("(c p) t -> p c t", p=128)
            n = 0
            for (c0, T, sample) in tiles(T0):
                DMA('sp', xt[:, :, 0:T], XFv[:, :, c0:c0 + T], ['XF'], ['xt'])
                DMA('act', ycb[:, :, 0:T], YCv[:, :, c0:c0 + T], ['YC'], ['ycb'])
                for s4 in range(T // 128):
                    b = n % 2; n += 1
                    t0 = c0 + s4 * 128
                    DMA('sp', a3[b][:], ATT.ap()[:, t0:t0 + 128, :].rearrange("g t c -> t g c"), ['ATT'], ['a3%d' % b])
                    TT('pool', a3[b][:, 0, :], a3[b][:, 0, :], a3[b][:, 1, :], ALU.add, ['a3%d' % b], ['a3%d' % b])
                    TT('pool', a3[b][:, 0, :], a3[b][:, 0, :], a3[b][:, 2, :], ALU.add, ['a3%d' % b], ['a3%d' % b])
                    a65 = a3[b][:, 0, :].rearrange("p (h e) -> p h e", e=65)
                    RCP(rc[:], a65[:, :, 64], ['a3%d' % b], ['rc'])
                    TT('dve', atm[:].rearrange("p (h d) -> p h d", d=64), a65[:, :, 0:64],
                       rc[:].unsqueeze(2).to_broadcast([128, 4, 64]), ALU.mult, ['a3%d' % b, 'rc'], ['atm'])
                    for c in range(2):
                        TR(PS[6][:, c * 128:(c + 1) * 128], atm[:, c * 128:(c + 1) * 128], ident[:], ['atm', 'ident'],
                           [psk[6]])
                    CP('act', afm[:, :, s4 * 128:(s4 + 1) * 128], PS[6][:, 0:256].rearrange("p (c t) -> p c t", c=2),
                       [psk[6]], ['afm'])
                for m in range(8):
                    pm = PS[1 + (m % 4)]; pk = psk[1 + (m % 4)]
                    for kk in range(6):
                        rhs = ycb[:, kk, 0:T] if kk < 4 else afm[:, kk - 4, 0:T]
                        MM(pm[:, 0:T], wo[:, kk, m * 128:(m + 1) * 128], rhs, kk == 0, kk == 5,
                           ['wo', 'ycb', 'afm'], [pk])
                    TT('dve', v2(tmp[:, 0:T], sample), v2(pm[:, 0:T], sample), modbc1(l, 2, m, T, sample), ALU.mult,
                       [pk, 'MOD'], ['tmp'])
                    TT('pool', xt[:, m, 0:T], xt[:, m, 0:T], tmp[:, 0:T], ALU.add, ['xt', 'tmp'], ['xt'])
                DMA('pool', XFv[:, :, c0:c0 + T], xt[:, :, 0:T], ['xt'], ['XF'])

        def ffn(l, st):
            T0 = 256
            wg = sb(st, [128, 8, DFF], BF16); wu = sb(st, [128, 8, DFF], BF16); wd = sb(st, [128, NFF, D], BF16)
            with contextlib.ExitStack() as st2:
                stg = [sb(st2, [128, DFF]) for _ in range(2)]
                load_w(wg, I["w_ff_gate"].ap()[l], 8, DFF, 'wg', stg)
                load_w(wu, I["w_ff_up"].ap()[l], 8, DFF, 'wu', stg)
                load_w(wd, I["w_ff_down"].ap()[l], NFF, D, 'wd', stg)
                P.barrier()
            xts = [sb(st, [128, 8, T0]) for _ in range(2)]; sqb = sb(st, [128, 8, T0], BF16)
            hbs = [sb(st, [128, 8, T0], BF16) for _ in range(2)]; rstd = sb(st, [128, T0]); act = sb(st, [128, NFF, T0], BF16)
            sg = [sb(st, [128, T0]) for _ in range(2)]; tmp = sb(st, [128, T0])
            tl = tiles(T0)

            def prep(it):
                c0, T, sample = tl[it]
                bb = it % 2
                kx, kh = 'xt%d' % bb, 'hb%d' % bb
                DMA('sp', xts[bb][:, :, 0:T], XFv[:, :, c0:c0 + T], ['XF'], [kx])
                norm_mod(xts[bb], sqb, hbs[bb], rstd, T, l, 3, sample, kx=kx, kh=kh)
                DMA('sp', xts[bb][:, :, 0:T], XFv[:, :, c0:c0 + T], ['XF'], [kx])

            prep(0)
            for it, (c0, T, sample) in enumerate(tl):
                bb = it % 2
                xt, hb = xts[bb], hbs[bb]
                kx, kh = 'xt%d' % bb, 'hb%d' % bb
                for j in range(NFF):
                    if j == 12 and it + 1 < len(tl):
                        prep(it + 1)
                    b = j % 2
                    pg, pu = PS[1 + 2 * b], PS[2 + 2 * b]
                    kg, ku = psk[1 + 2 * b], psk[2 + 2 * b]
                    for k in range(8):
                        MM(pg[:, 0:T], wg[:, k, j * 128:(j + 1) * 128], hb[:, k, 0:T], k == 0, k == 7, ['wg', kh], [kg])
                    for k in range(8):
                        MM(pu[:, 0:T], wu[:, k, j * 128:(j + 1) * 128], hb[:, k, 0:T], k == 0, k == 7, ['wu', kh], [ku])
                    ACT(sg[b][:, 0:T], pg[:, 0:T], AF.Silu, [kg], ['sg%d' % b])
                    TT('dve', act[:, j, 0:T], pu[:, 0:T], sg[b][:, 0:T], ALU.mult, [ku, 'sg%d' % b], ['act'])
                for m in range(8):
                    pm = PS[5 + (m % 2)]; pk = psk[5 + (m % 2)]
                    for j in range(NFF):
                        MM(pm[:, 0:T], wd[:, j, m * 128:(m + 1) * 128], act[:, j, 0:T], j == 0, j == NFF - 1,
                           ['wd', 'act'], [pk])
                    TT('dve', v2(tmp[:, 0:T], sample), v2(pm[:, 0:T], sample), modbc1(l, 5, m, T, sample), ALU.mult,
                       [pk, 'MOD'], ['tmp'])
                    TT('pool', xt[:, m, 0:T], xt[:, m, 0:T], tmp[:, 0:T], ALU.add, [kx, 'tmp'], [kx])
                DMA('pool', XFv[:, :, c0:c0 + T], xt[:, :, 0:T], [kx], ['XF'])

        def odd_O1(l, oi, st):
            T0 = 256
            PI = float(np.pi)
            bbr = sb(st, [128, 32, 128], BF16); bbi = sb(st, [128, 32, 128], BF16)
            ccr = sb(st, [128, 32, 128], BF16); cci = sb(st, [128, 32, 128], BF16); ccn = sb(st, [128, 32, 128], BF16)
            cosT = sb(st, [128, 32, 128]); sinT = sb(st, [128, 32, 128])
            rho = sb(st, [128, 32]); dsk = sb(st, [128, 8]); bgl = sb(st, [128, 16])
            wgl = sb(st, [128, 8, 2 * D], BF16)
            Sre = sb(st, [128, 32]); Sim = sb(st, [128, 32])
            S0r = sb(st, [128, 32, 16]); S0i = sb(st, [128, 32, 16]); Sor = sb(st, [128, 32, 16]); Soi = sb(st, [128, 32, 16])
            with contextlib.ExitStack() as st2:
                stg = [sb(st2, [128, 2 * D]) for _ in range(2)]
                load_w(wgl, I["s5_w_glu"].ap()[oi], 8, 2 * D, 'wgl', stg)
                DMA('pool', dsk[:], I["s5_d"].ap()[oi].rearrange("(c p) -> p c", p=128), (), ['dsk'])
                DMA('pool', bgl[:], I["s5_b_glu"].ap()[oi].rearrange("(c p) -> p c", p=128), (), ['bgl'])
                lr = sb(st2, [128, 32]); li = sb(st2, [128, 32]); dt = sb(st2, [128, 32])
                DMA('sp', lr[:], I["s5_lam_re"].ap()[oi].rearrange("(sc gl) p -> (gl p) sc", gl=2), (), ['lr'])
                DMA('sp', li[:], I["s5_lam_im"].ap()[oi].rearrange("(sc gl) p -> (gl p) sc", gl=2), (), ['li'])
                for gl in range(2):
                    DMA('sp', dt[gl * 64:(gl + 1) * 64, :],
                        bass.AP(I["s5_log_dt"], oi * 64 + gl, [[0, 64], [2, 32]]), (), ['dt'])
                br = sb(st2, [128, 32, 16]); bi = sb(st2, [128, 32, 16])
                DMA('sp', br[:], I["s5_b_re"].ap()[oi].rearrange("(sc gl) p h -> (gl p) sc h", gl=2), (), ['br'])
                DMA('sp', bi[:], I["s5_b_im"].ap()[oi].rearrange("(sc gl) p h -> (gl p) sc h", gl=2), (), ['bi'])
                ACT(dt[:], dt[:], AF.Exp, ['dt'], ['dt'])
                mag = sb(st2, [128, 32]); th = sb(st2, [128, 32]); ta = sb(st2, [128, 32]); tb = sb(st2, [128, 32])
                TT('dve', mag[:], lr[:], dt[:], ALU.mult, ['lr', 'dt'], ['mag'])
                ACT(mag[:], mag[:], AF.Exp, ['mag'], ['mag'])
                TT('dve', th[:], li[:], dt[:], ALU.mult, ['li', 'dt'], ['th'])
                cs0 = sb(st2, [128, 32]); sn0 = sb(st2, [128, 32])

                def sin_of(dst, shift, key):
                    TS('dve', ta[:], th[:], shift + 2 * PI, None, ALU.add, None, ['th'], ['ta'])
                    CP('dve', tb[:], ta[:], ['ta'], ['tb'])
                    for j in range(7):
                        thr = (2 * j + 1) * PI
                        TS('dve', dst[:], ta[:], thr, -2 * PI, ALU.is_ge, ALU.mult, ['ta'], [key])
                        TT('dve', tb[:], tb[:], dst[:], ALU.add, ['tb', key], ['tb'])
                    ACT(dst[:], tb[:], AF.Sin, ['tb'], [key])

                sin_of(sn0, 0.0, 'sn0')
                sin_of(cs0, PI / 2, 'cs0')
                abr = sb(st2, [128, 32]); abi = sb(st2, [128, 32]); fre = sb(st2, [128, 32]); fim = sb(st2, [128, 32])
                den = sb(st2, [128, 32])
                TT('dve', abr[:], mag[:], cs0[:], ALU.mult, ['mag', 'cs0'], ['abr'])
                TT('dve', abi[:], mag[:], sn0[:], ALU.mult, ['mag', 'sn0'], ['abi'])
                CP('dve', rho[:], mag[:], ['mag'], ['rho'])
                TT('dve', den[:], lr[:], lr[:], ALU.mult, ['lr'], ['den'])
                TT('dve', ta[:], li[:], li[:], ALU.mult, ['li'], ['ta'])
                TT('dve', den[:], den[:], ta[:], ALU.add, ['den', 'ta'], ['den'])
                RCP(den[:], den[:], ['den'], ['den'])
                TS('dve', tb[:], abr[:], -1.0, None, ALU.add, None, ['abr'], ['tb'])
                TT('dve', fre[:], tb[:], lr[:], ALU.mult, ['tb', 'lr'], ['fre'])
                TT('dve', ta[:], abi[:], li[:], ALU.mult, ['abi', 'li'], ['ta'])
                TT('dve', fre[:], fre[:], ta[:], ALU.add, ['fre', 'ta'], ['fre'])
                TT('dve', fre[:], fre[:], den[:], ALU.mult, ['fre', 'den'], ['fre'])
                TT('dve', fim[:], abi[:], lr[:], ALU.mult, ['abi', 'lr'], ['fim'])
                TT('dve', ta[:], tb[:], li[:], ALU.mult, ['tb', 'li'], ['ta'])
                TT('dve', fim[:], fim[:], ta[:], ALU.subtract, ['fim', 'ta'], ['fim'])
                TT('dve', fim[:], fim[:], den[:], ALU.mult, ['fim', 'den'], ['fim'])
                Br = sb(st2, [128, 32, 16]); Bi = sb(st2, [128, 32, 16]); t16 = sb(st2, [128, 32, 16])
                fr_b = fre[:].unsqueeze(2).to_broadcast([128, 32, 16]); fi_b = fim[:].unsqueeze(2).to_broadcast([128, 32, 16])
                TT('dve', Br[:], br[:], fr_b, ALU.mult, ['br', 'fre'], ['Br'])
                TT('dve', t16[:], bi[:], fi_b, ALU.mult, ['bi', 'fim'], ['t16'])
                TT('dve', Br[:], Br[:], t16[:], ALU.subtract, ['Br', 't16'], ['Br'])
                TT('dve', Bi[:], bi[:], fr_b, ALU.mult, ['bi', 'fre'], ['Bi'])
                TT('dve', t16[:], br[:], fi_b, ALU.mult, ['br', 'fim'], ['t16'])
                TT('dve', Bi[:], Bi[:], t16[:], ALU.add, ['Bi', 't16'], ['Bi'])
                Z = sb(st2, [128, 128])
                for which, (srcB, dstB, key) in enumerate(((Br, bbr, 'bbr'), (Bi, bbi, 'bbi'))):
                    for sc in range(32):
                        g0 = (2 * sc) % 8
                        MS('pool', Z[:], 0.0, ['Z'])
                        CP('pool', Z[0:64, g0 * 16:(g0 + 1) * 16], srcB[0:64, sc, :], [key[0:1] + 'B' + key[2:], 'Br', 'Bi'], ['Z'])
                        CP('pool', Z[64:128, (g0 + 1) * 16:(g0 + 2) * 16], srcB[64:128, sc, :], ['Br', 'Bi'], ['Z'])
                        TR(PS[1 + sc % 2][:, 0:128], Z[:], ident[:], ['Z', 'ident'], [psk[1 + sc % 2]])
                        CP('act', dstB[:, sc, :], PS[1 + sc % 2][:, 0:128], [psk[1 + sc % 2]], [key])
                cn = sb(st2, [128, 128]); ctr = sb(st2, [128, 128])
                MS('pool', ccr[:], 0.0, ['ccr'])
                MS('pool', cci[:], 0.0, ['cci'])
                for which, (nm, dstC, key, sgn) in enumerate((("s5_c_re", ccr, 'ccr', 1.0), ("s5_c_im", cci, 'cci', -1.0))):
                    for c in range(8):
                        src = I[nm].ap()[oi, c * 8:(c + 1) * 8].rearrange("g h p -> (g h) p")
                        DMA('sp', cn[:, 0:64], src, (), ['cn'])
                        DMA('sp', cn[:, 64:128], src, (), ['cn'])
                        TR(PS[3][:, 0:128], cn[:], ident[:], ['cn', 'ident'], [psk[3]])
                        TS('dve', ctr[:], PS[3][:, 0:128], sgn, None, ALU.mult, None, [psk[3]], ['ctr'])
                        for q in range(4):
                            sc = c * 4 + q
                            g0 = 2 * q
                            CP('pool', dstC[0:64, sc, g0 * 16:(g0 + 1) * 16], ctr[0:64, g0 * 16:(g0 + 1) * 16], ['ctr'], [key])
                            CP('pool', dstC[64:128, sc, (g0 + 1) * 16:(g0 + 2) * 16],
                               ctr[64:128, (g0 + 1) * 16:(g0 + 2) * 16], ['ctr'], [key])
                TS('pool', ccn[:].rearrange("p a b -> p (a b)"), ccr[:].rearrange("p a b -> p (a b)"), -1.0, None, ALU.mult, None, ['ccr'], ['ccn'])
                CP('dve', cosT[:, :, 0], cs0[:], ['cs0'], ['tab'])
                CP('dve', sinT[:, :, 0], sn0[:], ['sn0'], ['tab'])
                tw = sb(st2, [128, 32, 64])
                m = 1
                while m < 128:
                    cm = cosT[:, :, m - 1:m].to_broadcast([128, 32, m]); sm_ = sinT[:, :, m - 1:m].to_broadcast([128, 32, m])
                    c_lo, s_lo = cosT[:, :, 0:m], sinT[:, :, 0:m]
                    c_hi, s_hi = cosT[:, :, m:2 * m], sinT[:, :, m:2 * m]
                    TT('dve', c_hi, c_lo, cm, ALU.mult, ['tab'], ['tab'])
                    TT('dve', tw[:, :, 0:m], s_lo, sm_, ALU.mult, ['tab'], ['tw'])
                    TT('dve', c_hi, c_hi, tw[:, :, 0:m], ALU.subtract, ['tab', 'tw'], ['tab'])
                    TT('dve', s_hi, s_lo, cm, ALU.mult, ['tab'], ['tab'])
                    TT('dve', tw[:, :, 0:m], c_lo, sm_, ALU.mult, ['tab'], ['tw'])
                    TT('dve', s_hi, s_hi, tw[:, :, 0:m], ALU.add, ['tab', 'tw'], ['tab'])
                    m *= 2
                sti = sb(st2, [16, 8192])
                DMA('sp', sti[:], I["st5"].ap()[oi], (), ['sti'])
                stv = sti[:].rearrange("s (sc x c) -> s sc x c", sc=32, c=2)
                for sc in range(32):
                    for c2, dstS in ((0, S0r), (1, S0i)):
                        TR(PS[5][:, (sc % 16) * 32 + c2 * 16:(sc % 16) * 32 + c2 * 16 + 16], stv[:, sc, :, c2],
                           ident[0:16, 0:16], ['sti', 'ident'], [psk[5]])
                    if sc % 16 == 15:
                        h0 = sc - 15
                        pv = PS[5][:, 0:512].rearrange("p (sc c s) -> p sc c s", c=2, s=16)
                        CP('dve', S0r[:, h0:h0 + 16, :], pv[:, :, 0, :], [psk[5]], ['S0'])
                        CP('dve', S0i[:, h0:h0 + 16, :], pv[:, :, 1, :], [psk[5]], ['S0'])
                MS('pool', Sre[:], 0.0, ['Sc%d' % i_ for i_ in range(32)])
                MS('pool', Sim[:], 0.0, ['Sc%d' % i_ for i_ in range(32)])
                P.barrier()
            st3 = contextlib.ExitStack()
            xt = sb(st3, [128, 8, T0]); sqb = sb(st3, [128, 8, T0], BF16)
            hb = sb(st3, [128, 8, T0], BF16); hf = sb(st3, [128, 8, T0]); rstd = sb(st3, [128, T0])
            zz = sb(st3, [128, 8, T0], BF16)
            burs = [sb(st3, [128, T0]) for _ in range(4)]; buis = [sb(st3, [128, T0]) for _ in range(4)]
            rin = [[sb(st3, [128, T0]) for _ in range(4)] for _ in range(4)]
            wrs = [sb(st3, [128, T0]) for _ in range(4)]; wis = [sb(st3, [128, T0]) for _ in range(4)]
            prods = [[sb(st3, [128, T0], BF16) for _ in range(4)] for _ in range(4)]
            rhs_ = sb(st3, [128, 128]); tn = sb(st3, [128, 16]); yv = sb(st3, [128, T0]); sgm = sb(st3, [128, T0])
            for (c0, T, sample) in tiles(T0):
                DMA('sp', xt[:, :, 0:T], XFv[:, :, c0:c0 + T], ['XF'], ['xt'])
                norm_mod(xt, sqb, hb, rstd, T, l, 0, sample, hf=hf)
                DMA('sp', xt[:, :, 0:T], XFv[:, :, c0:c0 + T], ['XF'], ['xt'])
                nsub = T // 128

                def tabv(tab, sc):
                    if sample:
                        return tab[:, sc, 0:8].unsqueeze(1).to_broadcast([128, 16, 8])
                    return tab[:, sc, :].unsqueeze(1).to_broadcast([128, nsub, 128])

                def tv(ap):
                    return ap.rearrange("p (s t) -> p s t", t=8) if sample else ap.rearrange("p (k t) -> p k t", t=128)

                def front(sc):
                    c, q = sc // 4, sc % 4
                    pb_ = PS[1 + (q % 2)]; kb_ = psk[1 + (q % 2)]
                    pr, pi_ = pb_[:, 0:T], pb_[:, 256:256 + T]
                    psm = PS[3 + (q % 2)]; ksm = psk[3 + (q % 2)]
                    MM(pr, bbr[:, sc, :], hb[:, c, 0:T], True, True, ['bbr', 'hb'], [kb_])
                    MM(pi_, bbi[:, sc, :], hb[:, c, 0:T], True, True, ['bbi', 'hb'], [kb_])
                    cv, sv = tabv(cosT, sc), tabv(sinT, sc)
                    pp = rin[q]
                    kpp = 'rin%d' % q

                    def adds():
                        MM(psm[:, 0:T], ident[:], pp[0][:, 0:T], True, False, ['ident', kpp], [ksm])
                        MM(psm[:, 0:T], ident[:], pp[1][:, 0:T], False, True, ['ident', kpp], [ksm])
                        MM(psm[:, 256:256 + T], ident[:], pp[2][:, 0:T], True, False, ['ident', kpp], [ksm])
                        MM(psm[:, 256:256 + T], identn[:], pp[3][:, 0:T], False, True, ['identn', kpp], [ksm])

                    if sc % 2 == 1:
                        return [
                            lambda: TT('dve', tv(pp[0][:, 0:T]), tv(pr), cv, ALU.mult, [kb_, 'tab'], [kpp]),
                            lambda: TT('dve', tv(pp[1][:, 0:T]), tv(pi_), sv, ALU.mult, [kb_, 'tab'], [kpp]),
                            lambda: TT('dve', tv(pp[2][:, 0:T]), tv(pi_), cv, ALU.mult, [kb_, 'tab'], [kpp]),
                            lambda: (TT('dve', tv(pp[3][:, 0:T]), tv(pr), sv, ALU.mult, [kb_, 'tab'], [kpp]), adds()),
                        ]
                    bur, bui = burs[q], buis[q]
                    kbur, kbui = 'bur%d' % q, 'bui%d' % q
                    CP('act', bur[:, 0:T], pr, [kb_], [kbur])
                    CP('act', bui[:, 0:T], pi_, [kb_], [kbui])
                    TT('pool', tv(pp[0][:, 0:T]), tv(bur[:, 0:T]), cv, ALU.mult, [kbur, 'tab'], [kpp])
                    TT('pool', tv(pp[1][:, 0:T]), tv(bui[:, 0:T]), sv, ALU.mult, [kbui, 'tab'], [kpp])
                    TT('pool', tv(pp[2][:, 0:T]), tv(bui[:, 0:T]), cv, ALU.mult, [kbui, 'tab'], [kpp])
                    TT('pool', tv(pp[3][:, 0:T]), tv(bur[:, 0:T]), sv, ALU.mult, [kbur, 'tab'], [kpp])
                    adds()
                    return []

                def chain(sc):
                    q = sc % 4
                    psm = PS[3 + (q % 2)]; ksm = psk[3 + (q % 2)]
                    bpr, bpi, wr, wi = psm[:, 0:256], psm[:, 256:512], wrs[q], wis[q]
                    kbpr, kbpi, kwr, kwi = ksm, ksm, 'wr%d' % q, 'wi%d' % q
                    th = []
                    if not sample:
                        rb = rho[:, sc:sc + 1].to_broadcast([128, 128])
                        c1, s1 = cosT[:, sc, 127:128], sinT[:, sc, 127:128]
                        for k in range(nsub):
                            sl = slice(k * 128, (k + 1) * 128)
                            e = k * 128 + 127
                            th.append(lambda sl=sl: SCAN(wr[:, sl], rb, bpr[:, sl], Sre[:, sc:sc + 1], ['rho', kbpr, 'Sc%d' % sc], [kwr]))
                            th.append(lambda sl=sl: SCAN(wi[:, sl], rb, bpi[:, sl], Sim[:, sc:sc + 1], ['rho', kbpi, 'Sc%d' % sc], [kwi]))
                            th.append(lambda e=e: TT('dve', tn[:, 2 * q:2 * q + 1], wi[:, e:e + 1], s1, ALU.mult, [kwi, 'tab'], ['tn%d' % q]))
                            th.append(lambda e=e: TT('dve', tn[:, 2 * q + 1:2 * q + 2], wr[:, e:e + 1], s1, ALU.mult, [kwr, 'tab'], ['tn%d' % q]))
                            th.append(lambda e=e: STT(Sre[:, sc:sc + 1], wr[:, e:e + 1], c1, tn[:, 2 * q:2 * q + 1], ALU.mult, ALU.subtract,
                                                      [kwr, 'tab', 'tn%d' % q], ['Sc%d' % sc]))
                            th.append(lambda e=e: STT(Sim[:, sc:sc + 1], wi[:, e:e + 1], c1, tn[:, 2 * q + 1:2 * q + 2], ALU.mult, ALU.add,
                                                      [kwi, 'tab', 'tn%d' % q], ['Sc%d' % sc]))
                    else:
                        b0r = bpr[:, 0:T].rearrange("p (s t) -> p s t", t=8)[:, :, 0]
                        b0i = bpi[:, 0:T].rearrange("p (s t) -> p s t", t=8)[:, :, 0]
                        w7r = wr[:, 0:T].rearrange("p (s t) -> p s t", t=8)[:, :, 7]
                        w7i = wi[:, 0:T].rearrange("p (s t) -> p s t", t=8)[:, :, 7]
                        c8, s8 = cosT[:, sc, 7:8], sinT[:, sc, 7:8]
                        th.append(lambda: TS('dve', rhs_[:], m01[:], rho[:, sc:sc + 1], None, ALU.mult, None, ['m01', 'rho'], ['rhs']))
                        th.append(lambda: STT(b0r, S0r[:, sc, :], rho[:, sc:sc + 1], b0r, ALU.mult, ALU.add, ['S0', 'rho', kbpr], [kbpr]))
                        th.append(lambda: STT(b0i, S0i[:, sc, :], rho[:, sc:sc + 1], b0i, ALU.mult, ALU.add, ['S0', 'rho', kbpi], [kbpi]))
                        th.append(lambda: SCAN(wr[:, 0:T], rhs_[:], bpr[:, 0:T], 0.0, ['rhs', kbpr], [kwr]))
                        th.append(lambda: SCAN(wi[:, 0:T], rhs_[:], bpi[:, 0:T], 0.0, ['rhs', kbpi], [kwi]))
                        th.append(lambda: TS('dve', tn[:], w7i, s8, None, ALU.mult, None, [kwi, 'tab'], ['tns']))
                        th.append(lambda: STT(Sor[:, sc, :], w7r, c8, tn[:], ALU.mult, ALU.subtract, [kwr, 'tab', 'tns'], ['So']))
                        th.append(lambda: TS('dve', tn[:], w7r, s8, None, ALU.mult, None, [kwr, 'tab'], ['tns']))
                        th.append(lambda: STT(Soi[:, sc, :], w7i, c8, tn[:], ALU.mult, ALU.add, [kwi, 'tab', 'tns'], ['So']))
                    return th

                def filler(sc):
                    q = sc % 4
                    wr, wi = wrs[q], wis[q]
                    kwr, kwi = 'wr%d' % q, 'wi%d' % q
                    cv, sv = tabv(cosT, sc), tabv(sinT, sc)
                    return [
                        lambda: TT('dve', tv(prods[q][0][:, 0:T]), tv(wr[:, 0:T]), cv, ALU.mult, [kwr, 'tab'], ['pr%d' % q]),
                        lambda: TT('dve', tv(prods[q][1][:, 0:T]), tv(wi[:, 0:T]), sv, ALU.mult, [kwi, 'tab'], ['pr%d' % q]),
                        lambda: TT('dve', tv(prods[q][2][:, 0:T]), tv(wi[:, 0:T]), cv, ALU.mult, [kwi, 'tab'], ['pr%d' % q]),
                        lambda: TT('dve', tv(prods[q][3][:, 0:T]), tv(wr[:, 0:T]), sv, ALU.mult, [kwr, 'tab'], ['pr%d' % q]),
                    ]

                def cproj(c):
                    py = PS[5 + c % 2]; ky = psk[5 + c % 2]
                    for q in range(4):
                        sc = c * 4 + q
                        MM(py[:, 0:T], ccr[:, sc, :], prods[q][0][:, 0:T], q == 0, False, ['ccr', 'pr%d' % q], [ky])
                        MM(py[:, 0:T], ccn[:, sc, :], prods[q][1][:, 0:T], False, False, ['ccn', 'pr%d' % q], [ky])
                        MM(py[:, 0:T], cci[:, sc, :], prods[q][2][:, 0:T], False, False, ['cci', 'pr%d' % q], [ky])
                        MM(py[:, 0:T], cci[:, sc, :], prods[q][3][:, 0:T], False, q == 3, ['cci', 'pr%d' % q], [ky])
                    STT(yv[:, 0:T], hf[:, c, 0:T], dsk[:, c:c + 1], py[:, 0:T], ALU.mult, ALU.add, ['hf', 'dsk', ky], ['yv'])
                    ACT(zz[:, c, 0:T], yv[:, 0:T], AF.Gelu_apprx_tanh, ['yv'], ['zz'])

                pend = front(0)
                for t_ in pend:
                    t_()
                for sc in range(33):
                    fl = []
                    if sc + 1 < 32:
                        fl += front(sc + 1)
                    ch = chain(sc) if sc < 32 else []
                    fl += filler(sc - 1) if sc >= 1 else []
                    i_f = 0
                    for i_c, t_ in enumerate(ch):
                        t_()
                        if i_f < len(fl):
                            fl[i_f](); i_f += 1
                    while i_f < len(fl):
                        fl[i_f](); i_f += 1
                    if sc >= 1 and (sc - 1) % 4 == 3:
                        cproj((sc - 1) // 4)
                for m_ in range(8):
                    pa, pb = PS[1 + 2 * (m_ % 2)], PS[2 + 2 * (m_ % 2)]
                    ka, kb = psk[1 + 2 * (m_ % 2)], psk[2 + 2 * (m_ % 2)]
                    for k in range(8):
                        MM(pa[:, 0:T], wgl[:, k, m_ * 128:(m_ + 1) * 128], zz[:, k, 0:T], k == 0, k == 7, ['wgl', 'zz'], [ka])
                    for k in range(8):
                        MM(pb[:, 0:T], wgl[:, k, D + m_ * 128:D + (m_ + 1) * 128], zz[:, k, 0:T], k == 0, k == 7,
                           ['wgl', 'zz'], [kb])
                    ACT(sgm[:, 0:T], pb[:, 0:T], AF.Sigmoid, [kb, 'bgl'], ['sgm'], bias=bgl[:, 8 + m_:9 + m_])
                    STT(yv[:, 0:T], pa[:, 0:T], bgl[:, m_:m_ + 1], sgm[:, 0:T], ALU.add, ALU.mult, [ka, 'bgl', 'sgm'], ['yv'])
                    TT('pool', v2(yv[:, 0:T], sample), v2(yv[:, 0:T], sample), modbc1(l, 2, m_, T, sample), ALU.mult,
                       ['yv', 'MOD'], ['yv'])
                    TT('pool', xt[:, m_, 0:T], xt[:, m_, 0:T], yv[:, 0:T], ALU.add, ['xt', 'yv'], ['xt'])
                DMA('pool', XFv[:, :, c0:c0 + T], xt[:, :, 0:T], ['xt'], ['XF'])
            P.barrier()
            st3.close()
            so = sb(st, [32, 128, 2])
            for c2, srcS in ((0, Sre), (1, Sim)):
                TR(PS[1][0:32, c2 * 128:(c2 + 1) * 128], srcS[:], ident[:], ['Sc%d' % i_ for i_ in range(32)] + ['ident'], [psk[1]])
            CP('dve', so[:].rearrange("p x c -> p c x"), PS[1][0:32, 0:256].rearrange("p (c x) -> p c x", c=2), [psk[1]], ['so'])
            DMA('pool', O["s5_p"].ap()[oi], so[:].rearrange("p x c -> p (x c)"), ['so'], ['o_s5p'])
            sso = sb(st, [16, 8192])
            ssv = sso[:].rearrange("s (sc x c) -> s sc x c", sc=32, c=2)
            for sc in range(32):
                for c2, srcS in ((0, Sor), (1, Soi)):
                    TR(PS[2][0:16, c2 * 128:(c2 + 1) * 128], srcS[:, sc, :], ident[:], ['So', 'ident'], [psk[2]])
                CP('dve', ssv[:, sc].rearrange("s x c -> s c x"), PS[2][0:16, 0:256].rearrange("s (c x) -> s c x", c=2),
                   [psk[2]], ['sso'])
            DMA('pool', O["s5_s"].ap()[oi], sso[:], ['sso'], ['o_s5s'])

        def final(st):
            fg = sb(st, [128, D])
            DMA('sp', fg[:], I["final_g"].ap().to_broadcast([128, D]), (), ['fg'])
            xf = [sb(st, [128, 8, 128]) for _ in range(2)]
            xtm = [sb(st, [128, D]) for _ in range(2)]
            junk = sb(st, [128, D]); ss = sb(st, [128, 1])
            for n in range(NSUB + 1):
                b = n % 2
                DMA('sp', xf[b][:], XFv[:, :, n * 128:(n + 1) * 128], ['XF'], ['xf%d' % b])
                for hh in range(2):
                    pt = PS[1 + 2 * b + hh]; pk = psk[1 + 2 * b + hh]
                    for q in range(4):
                        TR(pt[:, q * 128:(q + 1) * 128], xf[b][:, hh * 4 + q, :], ident[:], ['xf%d' % b, 'ident'], [pk])
                    CP('dve' if hh == 0 else 'act', xtm[b][:, hh * 512:(hh + 1) * 512], pt[:], [pk], ['xtm%d' % b])
                P.op('act', lambda e, b=b: e.activation(junk[:], xtm[b][:], AF.Square, accum_out=ss[:]),
                     ['xtm%d' % b], ['junk', 'ss'])
                ACT(ss[:], ss[:], AF.Sqrt, ['ss', 'epsT'], ['ss'], scale=1.0 / D, bias=epsT[:])
                RCP(ss[:], ss[:], ['ss'], ['ss'])
                STT(xtm[b][:], xtm[b][:], ss[:], fg[:], ALU.mult, ALU.mult, ['xtm%d' % b, 'ss', 'fg'], ['xtm%d' % b])
                dst = O["y_p"].ap()[n * 128:(n + 1) * 128, :] if n < NSUB else O["y_s"].ap()
                DMA('pool', dst, xtm[b][:], ['xtm%d' % b], ['o_y'])

        for l in range(LAYERS):
            if l % 2 == 0:
                ei = l // 2
                stq = contextlib.ExitStack()
                C.qs = sb(stq, [128, 2304])
                with contextlib.ExitStack() as st:
                    even_E1(l, ei, st)
                    P.barrier()
                with contextlib.ExitStack() as st:
                    even_E2(st)
                    P.barrier()
                with contextlib.ExitStack() as st:
                    even_SA(ei, st)
                    P.barrier()
                stq.close()
                with contextlib.ExitStack() as st:
                    even_E3(l, ei, st)
                    P.barrier()
            else:
                with contextlib.ExitStack() as st:
                    odd_O1(l, l // 2, st)
                    P.barrier()
            with contextlib.ExitStack() as st:
                ffn(l, st)
                P.barrier()
        with contextlib.ExitStack() as st:
            final(st)
        P.emit()
    return nc


def host_consts(SEQ):
    NSUB = SEQ // 128
    half = 8
    inv = (np.float32(500000.0) ** (-(2.0 / 16) * np.arange(half, dtype=np.float32))).astype(np.float32)
    pos_p = np.arange(SEQ, dtype=np.float32)
    ang = pos_p[:, None] * inv[None, :]
    rp = np.concatenate([np.cos(ang), np.sin(ang)], axis=1).astype(np.float32)
    ropep = np.ascontiguousarray(rp.reshape(NSUB, 128, 16).transpose(1, 0, 2))
    pos_s = (2048 + np.arange(8)).astype(np.float32)
    angs = pos_s[:, None] * inv[None, :]
    rs = np.concatenate([np.cos(angs), np.sin(angs)], axis=1).astype(np.float32)
    ropes = np.ascontiguousarray(np.tile(rs, (16, 1)))
    k = np.arange(128)[:, None]; q = np.arange(128)[None, :]
    amask = np.where(np.stack([(k >= q), (k <= q)], axis=1), 0.0, -30000.0).astype(np.float32)
    t = (np.arange(128) % 8)[:, None]
    NEG = np.float32(-1e30)
    m = np.arange(128)[None, :]
    sm = np.zeros((128, 280), np.float32)
    sm[:, 0:128] = np.where(m >= t, 0.0, NEG)
    sm[:, 128:256] = np.where((m == 0) & (t >= 4), NEG, 0.0)
    tp = np.arange(8)[None, :]
    sm[:, 256:264] = np.where(tp <= t, 0.0, NEG)
    sm[:, 264:272] = np.where((tp == t) | (tp == t - 4), 0.0, NEG)
    sm[:, 272:280] = np.where(tp == t, 0.0, NEG)
    m01 = np.ones((128, 128), np.float32); m01[:, 0::8] = 0.0
    return dict(c_ident=np.eye(128, dtype=np.float32), c_ropep=ropep, c_ropes=ropes, c_amask=amask,
                c_smask=sm, c_m01=m01)


_NC_CACHE = {}


def kernel(**inp):
    from concourse.bass_utils import run_bass_kernel_spmd
    f = lambda a: np.ascontiguousarray(np.asarray(a, dtype=np.float32))
    SEQ = inp["x_prompt"].shape[1]
    nb = inp["x_prompt"].shape[0]
    ncores = inp["x_sample"].shape[0] // 16
    if SEQ not in _NC_CACHE:
        _NC_CACHE[SEQ] = build(SEQ)
    nc = _NC_CACHE[SEQ]
    consts = host_consts(SEQ)
    wnames = ["norm_g", "w_ada", "b_ada", "w_in", "conv_w", "conv_b", "conv_ln_g", "conv_ln_b", "w_o",
              "s5_lam_re", "s5_lam_im", "s5_log_dt", "s5_b_re", "s5_b_im", "s5_c_re", "s5_c_im", "s5_d",
              "s5_w_glu", "s5_b_glu", "w_ff_gate", "w_ff_up", "w_ff_down"]
    shared = {k: f(inp[k]) for k in wnames}
    shared["final_g"] = f(inp["final_g"]).reshape(1, D)
    shared.update(consts)
    in_maps = []
    for i in range(ncores):
        sl = slice(16 * i, 16 * (i + 1))
        m = dict(shared)
        m["xp"] = f(inp["x_prompt"][i % nb])
        m["xs"] = f(inp["x_sample"][sl]).reshape(NS, D)
        m["call"] = f(np.concatenate([inp["c_prompt"][i % nb][None], inp["c_sample"][sl]], axis=0))
        m["cconv"] = f(inp["cache_conv"][:, sl]).reshape(2, 480, 512)
        m["ckv0"] = f(inp["cache_kv_g0"][:, sl]).reshape(2, 16, -1, 512)
        m["ckv1"] = f(inp["cache_kv_g1"][:, sl]).reshape(2, 16, -1, 512)
        m["ckv2"] = f(inp["cache_kv_g2"][:, sl]).reshape(2, 16, -1, 512)
        m["st5"] = f(inp["state_s5"][:, sl]).reshape(2, 16, 8192)
        in_maps.append(m)
    res = run_bass_kernel_spmd(nc, in_maps, core_ids=list(range(ncores)))
    kernel.last_res = res
    R = res.results
    keep = [min(w, SEQ) for w, _ in DILS]
    y_p = np.stack([R[i]["y_p"] for i in range(nb)]).reshape(nb, SEQ, D)
    y_s = np.concatenate([R[i]["y_s"].reshape(16, 8, D) for i in range(ncores)], axis=0)
    conv_p = np.stack([R[i]["conv_p"] for i in range(nb)], axis=1)
    kvp = [np.stack([R[i]["kv%d_p" % g].reshape(2, keep[g], 2, 4, 64) for i in range(nb)], axis=1) for g in range(3)]
    s5_p = np.stack([R[i]["s5_p"].reshape(2, 64, 64, 2) for i in range(nb)], axis=1)
    conv_s = np.concatenate([R[i]["conv_s"] for i in range(ncores)], axis=1)
    kvs = [np.concatenate([R[i]["kv%d_s" % g].reshape(2, 16, 8, 2, 4, 64) for i in range(ncores)], axis=1)
           for g in range(3)]
    s5_s = np.concatenate([R[i]["s5_s"].reshape(2, 16, 64, 64, 2) for i in range(ncores)], axis=1)
    outs = (y_p, y_s, conv_p, kvp[0], kvp[1], kvp[2], s5_p, conv_s, kvs[0], kvs[1], kvs[2], s5_s)
    return tuple(np.ascontiguousarray(o, dtype=np.float32) for o in outs)
```

```python
import contextlib

import numpy as np
import concourse.bass as bass
import concourse.mybir as mybir

F32 = mybir.dt.float32
BF16 = mybir.dt.bfloat16
AF = mybir.ActivationFunctionType
ALU = mybir.AluOpType
AX = mybir.AxisListType

EPOCH = 12000
NDSEM = 6


class Prog:
    def __init__(self, nc, stack):
        self.nc = nc
        self.stack = stack
        self.names = ['pe', 'dve', 'act', 'pool', 'sp']
        self.ops = {e: [] for e in self.names}
        self.cnt = {e: 0 for e in self.names}
        self.csem = {e: self._newsem() for e in self.names}
        self.dsem = {e: [self._newsem() for _ in range(NDSEM)] for e in ('sp', 'pool', 'act')}
        self.dcnt = {e: 0 for e in ('sp', 'pool', 'act')}
        self.dlast = {e: [0] * NDSEM for e in ('sp', 'pool', 'act')}
        self.last_w = {}
        self.readers = {}
        self.waited = {e: {} for e in self.names}
        self.pending = {e: [] for e in self.names}

    def _newsem(self):
        self.nsem = getattr(self, 'nsem', 0) + 1
        return self.stack.enter_context(self.nc.semaphore(name="sem%d" % self.nsem))

    def _deps(self, eng, r, w, is_dma=False):
        toks = []
        for k in r:
            t = self.last_w.get(k)
            if t is not None:
                toks.append((t, True))
        for k in w:
            t = self.last_w.get(k)
            if t is not None:
                toks.append((t, False))
            for t in self.readers.get(k, {}).values():
                toks.append((t, False))
        need = {}
        for (sem, val, teng, isdma), raw in toks:
            if teng == eng and not isdma and not raw and not is_dma:
                continue
            if self.waited[eng].get(id(sem), 0) >= val:
                continue
            cur = need.get(id(sem))
            if cur is None or cur[1] < val:
                need[id(sem)] = (sem, val)
        for sem, val in need.values():
            self.waited[eng][id(sem)] = val
        out = list(need.values()) + self.pending[eng]
        self.pending[eng] = []
        return out

    def _record(self, tok, r, w):
        for k in w:
            self.last_w[k] = tok
            self.readers[k] = {}
        for k in r:
            d = self.readers.setdefault(k, {})
            d[id(tok[0])] = tok

    def barrier(self):
        toks = []
        for e in self.names:
            if self.cnt[e] > 0:
                toks.append((self.csem[e], self.cnt[e], e))
        for q in self.dsem:
            for i, sem in enumerate(self.dsem[q]):
                if self.dlast[q][i] > 0:
                    toks.append((sem, self.dlast[q][i], None))
        for e in self.names:
            for s, v, te in toks:
                if te == e:
                    continue
                if self.waited[e].get(id(s), 0) < v:
                    self.pending[e].append((s, v))
                    self.waited[e][id(s)] = v

    def op(self, eng, fn, r=(), w=()):
        waits = self._deps(eng, r, w)
        if self.cnt[eng] >= EPOCH:
            self.csem[eng] = self._newsem()
            self.cnt[eng] = 0
        self.cnt[eng] += 1
        sem = self.csem[eng]
        tok = (sem, self.cnt[eng], eng, False)
        self.ops[eng].append((fn, waits, sem, 1))
        self._record(tok, r, w)
        return tok

    def dma(self, q, out, in_, r=(), w=(), **kw):
        waits = self._deps(q, r, w, True)
        n = self.dcnt[q]
        self.dcnt[q] += 1
        i = n % NDSEM
        sem = self.dsem[q][i]
        prev = self.dlast[q][i]
        if prev > 0 and self.waited[q].get(id(sem), 0) < prev:
            waits.append((sem, prev))
            self.waited[q][id(sem)] = prev
        val = prev + 16
        self.dlast[q][i] = val
        tok = (sem, val, q, True)
        self.ops[q].append((lambda e: e.dma_start(out=out, in_=in_, **kw), waits, sem, 16))
        self._record(tok, r, w)
        return tok

    def emit(self):
        nc = self.nc
        fin = {}
        for q in self.dsem:
            for i, sem in enumerate(self.dsem[q]):
                if self.dlast[q][i] > 0:
                    fin[id(sem)] = (sem, self.dlast[q][i])
        with nc.allow_non_contiguous_dma(reason="small strided parameter loads"), nc.Block() as block:
            def run(engname):
                def body(e):
                    for fn, waits, sem, inc in self.ops[engname]:
                        for s, v in waits:
                            e.wait_ge(s, v)
                        fn(e).then_inc(sem, inc)
                    if engname == 'sp':
                        for s, v in fin.values():
                            e.wait_ge(s, v)
                return body
            block.tensor(run('pe'))
            block.vector(run('dve'))
            block.scalar(run('act'))
            block.gpsimd(run('pool'))
            block.sync(run('sp'))


D = 1024
NCH = 8
DFF = 2816
NFF = 22
INC = 3328
EPS = 1e-6
NS = 128
DILS = ((128, 1), (512, 4), (2048, 16))


class Ctx:
    pass


def build(SEQ=4096, LAYERS=4, stop_after=None):
    from concourse.bass_utils import run_bass_kernel_spmd
    nc = bass.Bass("TRN2", target_bir_lowering=False)
    NT = SEQ + NS
    NSUB = SEQ // 128
    NE = 2
    NO = 2
    keep = [min(w, SEQ) for w, _ in DILS]

    def din(name, shape, dt=F32):
        return nc.dram_tensor(name, list(shape), dt, kind="ExternalInput")

    def dout(name, shape):
        return nc.dram_tensor(name, list(shape), F32, kind="ExternalOutput")

    def dscr(name, shape, dt=F32):
        return nc.dram_tensor(name, list(shape), dt, kind="Internal")

    I = {}
    for name, shape in [
        ("xp", (SEQ, D)), ("xs", (NS, D)), ("call", (17, D)),
        ("cconv", (NE, 480, 512)), ("ckv0", (NE, 16, 128, 512)), ("ckv1", (NE, 16, 512, 512)),
        ("ckv2", (NE, 16, 2048, 512)), ("st5", (NO, 16, 8192)),
        ("norm_g", (4, 2, D)), ("final_g", (1, D)), ("w_ada", (4, D, 6 * D)), ("b_ada", (4, 6 * D)),
        ("w_in", (NE, D, INC)), ("conv_w", (NE, 31, 512)), ("conv_b", (NE, 512)),
        ("conv_ln_g", (NE, 512)), ("conv_ln_b", (NE, 512)), ("w_o", (NE, 768, D)),
        ("s5_lam_re", (NO, 64, 64)), ("s5_lam_im", (NO, 64, 64)), ("s5_log_dt", (NO, 64)),
        ("s5_b_re", (NO, 64, 64, 16)), ("s5_b_im", (NO, 64, 64, 16)),
        ("s5_c_re", (NO, 64, 16, 64)), ("s5_c_im", (NO, 64, 16, 64)),
        ("s5_d", (NO, D)), ("s5_w_glu", (NO, D, 2 * D)), ("s5_b_glu", (NO, 2 * D)),
        ("w_ff_gate", (4, D, DFF)), ("w_ff_up", (4, D, DFF)), ("w_ff_down", (4, DFF, D)),
        ("c_ident", (128, 128)), ("c_ropep", (128, NSUB, 16)), ("c_ropes", (128, 16)),
        ("c_amask", (128, 2, 128)), ("c_smask", (128, 128 + 128 + 24)), ("c_m01", (128, 128)),
    ]:
        I[name] = din(name, shape)
    O = {}
    for name, shape in [
        ("y_p", (SEQ, D)), ("y_s", (NS, D)), ("conv_p", (NE, 30, 512)),
        ("kv0_p", (NE, keep[0], 512)), ("kv1_p", (NE, keep[1], 512)), ("kv2_p", (NE, keep[2], 512)),
        ("s5_p", (NO, 32, 256)), ("conv_s", (NE, 16, 30, 512)),
        ("kv0_s", (NE, NS, 512)), ("kv1_s", (NE, NS, 512)), ("kv2_s", (NE, NS, 512)),
        ("s5_s", (NO, 16, 8192)),
    ]:
        O[name] = dout(name, shape)
    XF = dscr("XF", (D, NT))
    YC = dscr("YC", (512, NT), BF16)
    QKV = dscr("QKVs", (SEQ, 2304), BF16)
    ATT = dscr("ATTs", (3, NT, 260))

    with contextlib.ExitStack() as gst:
        P = Prog(nc, gst)
        cnt = [0]

        def sb(st, shape, dt=F32):
            cnt[0] += 1
            return st.enter_context(nc.sbuf_tensor("t%d" % cnt[0], list(shape), dt))

        PS = [gst.enter_context(nc.psum_tensor("ps%d" % i, [128, 512], F32)) for i in range(7)]
        PSB = gst.enter_context(nc.psum_tensor("psb", [128, 1024], BF16))
        psk = ["ps%d" % i for i in range(7)]

        def MM(out, lhsT, rhs, start, stop, r, w):
            return P.op('pe', lambda e: e.matmul(out, lhsT, rhs, start=start, stop=stop), r, w)

        def TR(out, in_, idt, r, w):
            return P.op('pe', lambda e: e.transpose(out, in_, idt), r, w)

        def TT(eng, out, a, b, op, r, w):
            return P.op(eng, lambda e: e.tensor_tensor(out, a, b, op), r, w)

        def TS(eng, out, a, s1, s2, op0, op1, r, w):
            if s2 is None:
                return P.op(eng, lambda e: e.tensor_scalar(out, a, s1, None, op0), r, w)
            return P.op(eng, lambda e: e.tensor_scalar(out, a, s1, s2, op0, op1), r, w)

        def STT(out, a, s, b, op0, op1, r, w):
            return P.op('dve', lambda e: e.scalar_tensor_tensor(out, a, s, b, op0, op1), r, w)

        def ACT(out, in_, func, r, w, scale=1.0, bias=None):
            if bias is None:
                return P.op('act', lambda e: e.activation(out, in_, func, scale=scale), r, w)
            return P.op('act', lambda e: e.activation(out, in_, func, scale=scale, bias=bias), r, w)

        def CP(eng, out, in_, r, w):
            if eng == 'act':
                return P.op('act', lambda e: e.activation(out, in_, AF.Copy), r, w)
            return P.op(eng, lambda e: e.tensor_copy(out, in_), r, w)

        def MS(eng, out, val, w):
            return P.op(eng, lambda e: e.memset(out, val), (), w)

        def RED(out, in_, op, r, w, eng='dve'):
            return P.op(eng, lambda e: e.tensor_reduce(out, in_, AX.X, op), r, w)

        def RCP(out, in_, r, w):
            return P.op('dve', lambda e: e.reciprocal(out, in_), r, w)

        def SCAN(out, d0, d1, init, r, w):
            return P.op('dve', lambda e: e.tensor_tensor_scan(out, d0, d1, init, ALU.mult, ALU.add), r, w)

        def DMA(q, out, in_, r, w):
            return P.dma(q, out, in_, r, w)

        ident = sb(gst, [128, 128]); identb = sb(gst, [128, 128], BF16)
        onesb = sb(gst, [128, 128], BF16); onesf = sb(gst, [128, 128])
        epsT = sb(gst, [128, 1])
        MOD = sb(gst, [128, 4, 48, 17])
        m01 = sb(gst, [128, 128])
        DMA('sp', ident[:], I["c_ident"].ap(), (), ['ident'])
        DMA('sp', m01[:], I["c_m01"].ap(), (), ['m01'])
        CP('dve', identb[:], ident[:], ['ident'], ['identb'])
        MS('pool', onesb[:], 1.0, ['onesb'])
        MS('pool', onesf[:], 1.0 / 512.0, ['onesf'])
        MS('pool', epsT[:], EPS, ['epsT'])

        XFv = XF.ap().rearrange("(c p) t -> p c t", p=128)

        def modbc(l, i, T, sample):
            a = MOD[:, l, i * 8:(i + 1) * 8, :]
            if not sample:
                return a[:, :, 0:1].to_broadcast([128, 8, T])
            return a[:, :, 1:17].unsqueeze(3).to_broadcast([128, 8, 16, 8])

        def modbc1(l, i, m, T, sample):
            a = MOD[:, l, i * 8 + m, :]
            if not sample:
                return a[:, 0:1].to_broadcast([128, T])
            return a[:, 1:17].unsqueeze(2).to_broadcast([128, 16, 8])

        def v3(ap, sample):
            return ap.rearrange("p c (s t) -> p c s t", t=8) if sample else ap

        def v2(ap, sample):
            return ap.rearrange("p (s t) -> p s t", t=8) if sample else ap

        with contextlib.ExitStack() as st:
            ct = sb(st, [17, D]); sct = sb(st, [17, D]); scT = sb(st, [128, 8, 17])
            DMA('sp', ct[:], I["call"].ap(), (), ['ct'])
            ACT(sct[:], ct[:], AF.Silu, ['ct'], ['sct'])
            for c in range(8):
                TR(PS[0][:, c * 17:(c + 1) * 17], sct[:, c * 128:(c + 1) * 128], ident[0:17, 0:17],
                   ['sct', 'ident'], [psk[0]])
            CP('dve', scT[:].rearrange("p c s -> p (c s)"), PS[0][:, 0:136], [psk[0]], ['scT'])
            slabs = [sb(st, [128, 8, 512]) for _ in range(2)]
            bts = [sb(st, [17, 512]) for _ in range(2)]
            mts = [sb(st, [17, 512]) for _ in range(2)]
            n = 0
            for l in range(4):
                for j in range(12):
                    b = n % 2
                    DMA('sp' if n % 2 == 0 else 'act', slabs[b][:],
                        I["w_ada"].ap()[l].rearrange("(k p) n -> p k n", p=128)[:, :, j * 512:(j + 1) * 512],
                        (), ['slab%d' % b])
                    DMA('pool', bts[b][:], I["b_ada"].ap()[l:l + 1, j * 512:(j + 1) * 512].to_broadcast([17, 512]),
                        (), ['bt%d' % b])
                    pm = PS[1 + b]
                    for k in range(8):
                        MM(pm[0:17, :], scT[:, k, :], slabs[b][:, k, :], k == 0, k == 7,
                           ['scT', 'slab%d' % b], [psk[1 + b]])
                    TT('dve', mts[b][:], pm[0:17, :], bts[b][:], ALU.add, [psk[1 + b], 'bt%d' % b], ['mt%d' % b])
                    pt = PS[3 + b]
                    for q in range(4):
                        TR(pt[:, q * 17:(q + 1) * 17], mts[b][:, q * 128:(q + 1) * 128], ident[0:17, 0:17],
                           ['mt%d' % b, 'ident'], [psk[3 + b]])
                    CP('act', MOD[:, l, 4 * j:4 * j + 4, :].rearrange("p c s -> p (c s)"), pt[:, 0:68],
                       [psk[3 + b]], ['MOD'])
                    n += 1
            ng = sb(st, [128, 4, 2, 8])
            for l in range(4):
                for i2 in range(2):
                    DMA('sp', ng[:, l, i2, :], I["norm_g"].ap()[l, i2].rearrange("(c p) -> p c", p=128),
                        (), ['ng'])
            for l in range(4):
                for i2, idx in ((0, 1), (1, 4)):
                    a = MOD[:, l, idx * 8:(idx + 1) * 8, :]
                    TS('dve', a, a, 1.0, None, ALU.add, None, ['MOD'], ['MOD'])
                    TT('dve', a, a, ng[:, l, i2, :].unsqueeze(2).to_broadcast([128, 8, 17]), ALU.mult,
                       ['MOD', 'ng'], ['MOD'])
        P.barrier()

        with contextlib.ExitStack() as st:
            xin = [sb(st, [128, D]) for _ in range(2)]
            xo = [sb(st, [128, 8, 128]) for _ in range(2)]
            for n in range(NSUB + 1):
                b = n % 2
                src = I["xp"].ap()[n * 128:(n + 1) * 128, :] if n < NSUB else I["xs"].ap()
                DMA('sp', xin[b][:], src, (), ['xin%d' % b])
                for hh in range(2):
                    pt = PS[2 * b + hh]
                    for q in range(4):
                        c = hh * 4 + q
                        TR(pt[:, q * 128:(q + 1) * 128], xin[b][:, c * 128:(c + 1) * 128], ident[:],
                           ['xin%d' % b, 'ident'], [psk[2 * b + hh]])
                    CP('dve' if hh == 0 else 'act', xo[b][:, hh * 4:hh * 4 + 4, :].rearrange("p c t -> p (c t)"),
                       pt[:], [psk[2 * b + hh]], ['xo%d' % b])
                DMA('pool', XFv[:, :, n * 128:(n + 1) * 128], xo[b][:], ['xo%d' % b], ['XF'])
        P.barrier()

        wl = [0]

        def load_w(dst, w_ap, K, N, key, stg):
            for k in range(K):
                b = wl[0] % 2
                eng = ('dve', 'act', 'pool')[wl[0] % 3]
                wl[0] += 1
                DMA('sp' if b == 0 else 'act', stg[b][:, 0:N], w_ap[k * 128:(k + 1) * 128, :], (), ['stg%d' % b])
                CP(eng, dst[:, k, :], stg[b][:, 0:N], ['stg%d' % b], [key])

        def norm_mod(xt, sqb, hb, rstd, T, l, i_sh, sample, hf=None, kx='xt', kh='hb'):
            ACT(sqb[:, :, 0:T], xt[:, :, 0:T], AF.Square, [kx], ['sqb'])
            for c in range(8):
                MM(PS[0][:, 0:T], onesb[:], sqb[:, c, 0:T], c == 0, c == 7, ['onesb', 'sqb'], [psk[0]])
            ACT(rstd[:, 0:T], PS[0][:, 0:T], AF.Sqrt, [psk[0], 'epsT'], ['rstd'], scale=1.0 / D, bias=epsT[:])
            RCP(rstd[:, 0:T], rstd[:, 0:T], ['rstd'], ['rstd'])
            x3 = xt[:, :, 0:T]
            TT('dve', x3, x3, rstd[:, 0:T].unsqueeze(1).to_broadcast([128, 8, T]), ALU.mult, [kx, 'rstd'], [kx])
            TT('pool', v3(x3, sample), v3(x3, sample), modbc(l, i_sh + 1, T, sample), ALU.mult, [kx, 'MOD'], [kx])
            if hf is not None:
                TT('dve', v3(hf[:, :, 0:T], sample), v3(x3, sample), modbc(l, i_sh, T, sample), ALU.add,
                   [kx, 'MOD'], ['hf'])
                CP('act', hb[:, :, 0:T], hf[:, :, 0:T], ['hf'], [kh])
            else:
                TT('dve', v3(hb[:, :, 0:T], sample), v3(x3, sample), modbc(l, i_sh, T, sample), ALU.add,
                   [kx, 'MOD'], [kh])

        def tiles(T):
            out = [(c0, T, False) for c0 in range(0, SEQ, T)]
            out.append((SEQ, NS, True))
            return out

        def even_E1(l, ei, st):
            T0 = 512
            win = sb(st, [128, 8, INC], BF16)
            with contextlib.ExitStack() as st2:
                stg = [sb(st2, [128, INC]) for _ in range(2)]
                load_w(win, I["w_in"].ap()[ei], 8, INC, 'win', stg)
                P.barrier()
            cw = sb(st, [128, 4, 31]); cb = sb(st, [128, 4]); lg = sb(st, [128, 4]); lb = sb(st, [128, 4])
            for ch in range(4):
                DMA('pool', cw[:, ch, :], I["conv_w"].ap()[ei].rearrange("j c -> c j")[ch * 128:(ch + 1) * 128, :],
                    (), ['cw'])
            for t_, nm in ((cb, "conv_b"), (lg, "conv_ln_g"), (lb, "conv_ln_b")):
                DMA('pool', t_[:], I[nm].ap()[ei].rearrange("(c p) -> p c", p=128), (), ['cw'])
            ropep = sb(st, [128, NSUB, 16]); ropes = sb(st, [128, 16])
            DMA('pool', ropep[:], I["c_ropep"].ap(), (), ['rope'])
            DMA('pool', ropes[:], I["c_ropes"].ap(), (), ['rope'])
            xt = sb(st, [128, 8, T0]); sqb = sb(st, [128, 8, T0], BF16); hb = sb(st, [128, 8, T0], BF16)
            rstd = sb(st, [128, T0])
            uext = sb(st, [128, 4, 30 + T0]); uxs = sb(st, [128, 4, 16, 38])
            sg = sb(st, [128, T0]); yc = sb(st, [128, 4, T0]); ycb = sb(st, [128, 4, T0], BF16)
            lnr = sb(st, [128, T0]); sq4 = sb(st, [128, 4, T0])
            qkv = sb(st, [128, 2304]); qkvb = sb(st, [128, 2304], BF16); rt = sb(st, [128, 4, 24, 8])
            cvt = sb(st, [128, 512]); cin = sb(st, [120, 512]); cmp_ = sb(st, [128, 4, 16, 30])
            MS('pool', uext[:, :, 0:30], 0.0, ['uext'])
            for q4 in range(4):
                DMA('sp', cin[:], I["cconv"].ap()[ei, q4 * 120:(q4 + 1) * 120, :], (), ['cin'])
                for ch in range(4):
                    TR(PS[1][:, ch * 120:(ch + 1) * 120], cin[:, ch * 128:(ch + 1) * 128], ident[0:120, 0:120],
                       ['cin', 'ident'], [psk[1]])
                CP('dve', uxs[:, :, q4 * 4:(q4 + 1) * 4, 0:30],
                   PS[1][:, 0:480].rearrange("p (c s j) -> p c s j", c=4, s=4), [psk[1]], ['uxs'])
            for (c0, T, sample) in tiles(T0):
                DMA('sp', xt[:, :, 0:T], XFv[:, :, c0:c0 + T], ['XF'], ['xt'])
                norm_mod(xt, sqb, hb, rstd, T, l, 0, sample)
                for ch in range(4):
                    pv, pg = PS[1 + (ch % 2) * 2], PS[2 + (ch % 2) * 2]
                    kv_, kg_ = psk[1 + (ch % 2) * 2], psk[2 + (ch % 2) * 2]
                    for k in range(8):
                        MM(pv[:, 0:T], win[:, k, ch * 128:(ch + 1) * 128], hb[:, k, 0:T], k == 0, k == 7,
                           ['win', 'hb'], [kv_])
                    for k in range(8):
                        MM(pg[:, 0:T], win[:, k, 512 + ch * 128:512 + (ch + 1) * 128], hb[:, k, 0:T], k == 0, k == 7,
                           ['win', 'hb'], [kg_])
                    ACT(sg[:, 0:T], pg[:, 0:T], AF.Sigmoid, [kg_], ['sg'])
                    if not sample:
                        TT('dve', uext[:, ch, 30:30 + T], pv[:, 0:T], sg[:, 0:T], ALU.mult, [kv_, 'sg'], ['uext'])
                    else:
                        TT('dve', uxs[:, ch, :, 30:38], v2(pv[:, 0:T], True), v2(sg[:, 0:T], True), ALU.mult,
                           [kv_, 'sg'], ['uxs'])
                for s4 in range(T // 128):
                    n = (c0 // 128) + s4
                    for nb in range(5):
                        wd = 512 if nb < 4 else 256
                        pq = PS[1 + (nb % 4)]
                        for k in range(8):
                            MM(pq[:, 0:wd], hb[:, k, s4 * 128:(s4 + 1) * 128],
                               win[:, k, 1024 + nb * 512:1024 + nb * 512 + wd], k == 0, k == 7,
                               ['hb', 'win'], [psk[1 + (nb % 4)]])
                        CP('act', qkv[:, nb * 512:nb * 512 + wd], pq[:, 0:wd],
                           [psk[1 + (nb % 4)]], ['qkv'])
                    qk = qkv[:, 0:1536].rearrange("p (h d) -> p h d", d=64)
                    x1, x2 = qk[:, :, 0:8], qk[:, :, 8:16]
                    rp = ropes[:] if sample else ropep[:, n, :]
                    cs = rp[:, 0:8].unsqueeze(1).to_broadcast([128, 24, 8])
                    sn = rp[:, 8:16].unsqueeze(1).to_broadcast([128, 24, 8])
                    TT('pool', rt[:, 0], x1, cs, ALU.mult, ['qkv', 'rope'], ['rt'])
                    TT('pool', rt[:, 1], x2, sn, ALU.mult, ['qkv', 'rope'], ['rt'])
                    TT('pool', rt[:, 2], x2, cs, ALU.mult, ['qkv', 'rope'], ['rt'])
                    TT('pool', rt[:, 3], x1, sn, ALU.mult, ['qkv', 'rope'], ['rt'])
                    TT('pool', x1, rt[:, 0], rt[:, 1], ALU.subtract, ['rt'], ['qkv'])
                    TT('pool', x2, rt[:, 2], rt[:, 3], ALU.add, ['rt'], ['qkv'])
                    for g in range(3):
                        kcols = qkv[:, 768 + 256 * g:768 + 256 * (g + 1)]
                        vcols = qkv[:, 1536 + 256 * g:1536 + 256 * (g + 1)]
                        if sample:
                            dsto = O["kv%d_s" % g].ap()[ei]
                        else:
                            r0 = n * 128 - (SEQ - keep[g])
                            if r0 < 0:
                                continue
                            dsto = O["kv%d_p" % g].ap()[ei, r0:r0 + 128, :]
                        DMA('pool', dsto[:, 0:256], kcols, ['qkv'], ['o_kv'])
                        DMA('pool', dsto[:, 256:512], vcols, ['qkv'], ['o_kv'])
                    if not sample:
                        CP('act', qkvb[:], qkv[:], ['qkv'], ['qkvb'])
                        DMA('sp', QKV.ap()[n * 128:(n + 1) * 128, :], qkvb[:], ['qkvb'], ['QKV'])
                    else:
                        CP('act', C.qs[:], qkv[:], ['qkv'], ['qs'])
                for ch in range(4):
                    if not sample:
                        src = lambda j: uext[:, ch, j:j + T]
                        dst = yc[:, ch, 0:T]
                        uk = 'uext'
                    else:
                        src = lambda j: uxs[:, ch, :, j:j + 8]
                        dst = v2(yc[:, ch, 0:T], True)
                        uk = 'uxs'
                    TS('dve', dst, src(0), cw[:, ch, 0:1], cb[:, ch:ch + 1], ALU.mult, ALU.add, [uk, 'cw'], ['yc'])
                    for j in range(1, 31):
                        STT(dst, src(j), cw[:, ch, j:j + 1], dst, ALU.mult, ALU.add, [uk, 'cw', 'yc'], ['yc'])
                for ch in range(4):
                    MM(PS[5][:, 0:T], onesf[:], yc[:, ch, 0:T], ch == 0, ch == 3, ['onesf', 'yc'], [psk[5]])
                TT('dve', yc[:, :, 0:T], yc[:, :, 0:T], PS[5][:, 0:T].unsqueeze(1).to_broadcast([128, 4, T]),
                   ALU.subtract, ['yc', psk[5]], ['yc'])
                ACT(sq4[:, :, 0:T], yc[:, :, 0:T], AF.Square, ['yc'], ['sq4'])
                for ch in range(4):
                    MM(PS[5][:, 0:T], onesf[:], sq4[:, ch, 0:T], ch == 0, ch == 3, ['onesf', 'sq4'], [psk[5]])
                ACT(lnr[:, 0:T], PS[5][:, 0:T], AF.Sqrt, [psk[5], 'epsT'], ['lnr'], bias=epsT[:])
                RCP(lnr[:, 0:T], lnr[:, 0:T], ['lnr'], ['lnr'])
                TT('dve', yc[:, :, 0:T], yc[:, :, 0:T], lnr[:, 0:T].unsqueeze(1).to_broadcast([128, 4, T]), ALU.mult,
                   ['yc', 'lnr'], ['yc'])
                TT('pool', yc[:, :, 0:T], yc[:, :, 0:T], lg[:].unsqueeze(2).to_broadcast([128, 4, T]), ALU.mult,
                   ['yc', 'cw'], ['yc'])
                TT('pool', yc[:, :, 0:T], yc[:, :, 0:T], lb[:].unsqueeze(2).to_broadcast([128, 4, T]), ALU.add,
                   ['yc', 'cw'], ['yc'])
                ACT(ycb[:, :, 0:T], yc[:, :, 0:T], AF.Silu, ['yc'], ['ycb'])
                DMA('pool', YC.ap().rearrange("(c p) t -> p c t", p=128)[:, :, c0:c0 + T], ycb[:, :, 0:T],
                    ['ycb'], ['YC'])
                if not sample:
                    if c0 + T == SEQ:
                        for ch in range(4):
                            TR(PS[6][0:30, ch * 128:(ch + 1) * 128], uext[:, ch, T:T + 30], ident[:],
                               ['uext', 'ident'], [psk[6]])
                        CP('dve', cvt[0:30, :], PS[6][0:30, :], [psk[6]], ['cvt'])
                        DMA('pool', O["conv_p"].ap()[ei], cvt[0:30, :], ['cvt'], ['o_convp'])
                    CP('pool', uext[:, :, 0:30], uext[:, :, T:T + 30], ['uext'], ['uext'])
                else:
                    CP('dve', cmp_[:], uxs[:, :, :, 8:38], ['uxs'], ['cmp'])
                    for q4 in range(4):
                        for ch in range(4):
                            TR(PS[6][0:120, ch * 128:(ch + 1) * 128],
                               cmp_[:, ch, q4 * 4:(q4 + 1) * 4, :].rearrange("p s j -> p (s j)"), ident[:],
                               ['cmp', 'ident'], [psk[6]])
                        CP('dve', cvt[0:120, :], PS[6][0:120, :], [psk[6]], ['cvt'])
                        DMA('pool', O["conv_s"].ap()[ei, q4 * 4:(q4 + 1) * 4].rearrange("s j c -> (s j) c"),
                            cvt[0:120, :], ['cvt'], ['o_convs'])

        C = Ctx()

        def even_E2(st):
            am = sb(st, [128, 2, 128]); amb = sb(st, [128, 2, 128], BF16)
            DMA('sp', am[:], I["c_amask"].ap(), (), ['am'])
            CP('dve', amb[:], am[:], ['am'], ['amb'])
            qk_in = [sb(st, [128, 2, 256], BF16) for _ in range(2)]
            vx = [sb(st, [128, 4, 65], BF16) for _ in range(3)]
            qkT = [sb(st, [128, 4, 128], BF16) for _ in range(3)]
            pT = [sb(st, [128, 2, 128], BF16) for _ in range(4)]
            ob = [sb(st, [128, 260]) for _ in range(2)]
            for i in range(3):
                MS('pool', vx[i][:, :, 64:65], 1.0, ['vx%d' % i])
            blk = 0
            for g, (win_, dil) in enumerate(DILS):
                nb = SEQ // dil // 128
                qv = QKV.ap().rearrange("(b i d) c -> d b i c", i=128, d=dil)
                av = ATT.ap()[g, 0:SEQ, :].rearrange("(b i d) c -> d b i c", i=128, d=dil)
                for r in range(dil):
                    for b in range(nb):
                        ib = blk % 2
                        cur = blk % 3
                        prv = (blk - 1) % 3
                        rows = qv[r, b]
                        DMA('sp', qk_in[ib][:, 0, :], rows[:, 256 * g:256 * (g + 1)], ['QKV'], ['qkin%d' % ib])
                        DMA('sp', qk_in[ib][:, 1, :], rows[:, 768 + 256 * g:768 + 256 * (g + 1)], ['QKV'],
                            ['qkin%d' % ib])
                        DMA('sp', vx[cur][:, :, 0:64],
                            rows[:, 1536 + 256 * g:1536 + 256 * (g + 1)].rearrange("p (h d) -> p h d", d=64),
                            ['QKV'], ['vx%d' % cur])
                        for c in range(4):
                            TR(PSB[:, c * 128:(c + 1) * 128], qk_in[ib][:, c // 2, (c % 2) * 128:(c % 2 + 1) * 128],
                               identb[:], ['qkin%d' % ib, 'identb'], ['psb'])
                        CP('dve', qkT[cur][:].rearrange("p c t -> p (c t)"), PSB[:, 0:512], ['psb'], ['qkT%d' % cur])
                        po = PS[5 + ib]
                        lo = 0 if b > 0 else 1
                        for h in range(4):
                            c, hp = h // 2, (h % 2) * 64
                            pss = PS[1 + h]
                            ksk = psk[1 + h]
                            MM(pss[:, 128:256], qkT[cur][hp:hp + 64, 2 + c, :], qkT[cur][hp:hp + 64, c, :], True, False,
                               ['qkT%d' % cur], [ksk])
                            MM(pss[:, 128:256], identb[:], amb[:, 1, :], False, True, ['identb', 'amb'], [ksk])
                            if b > 0:
                                MM(pss[:, 0:128], qkT[prv][hp:hp + 64, 2 + c, :], qkT[cur][hp:hp + 64, c, :], True, False,
                                   ['qkT%d' % cur, 'qkT%d' % prv], [ksk])
                                MM(pss[:, 0:128], identb[:], amb[:, 0, :], False, True, ['identb', 'amb'], [ksk])
                            ACT(pT[h][:, lo:2, :].rearrange("p a t -> p (a t)"), pss[:, lo * 128:256], AF.Exp,
                                [ksk], ['pT%d' % h], scale=0.125)
                        for h in range(4):
                            pt_ = pT[h]
                            ptk = 'pT%d' % h
                            if b > 0:
                                MM(po[:, h * 65:(h + 1) * 65], pt_[:, 0, :], vx[prv][:, h, :], True, False,
                                   [ptk, 'vx%d' % prv], [psk[5 + ib]])
                            MM(po[:, h * 65:(h + 1) * 65], pt_[:, 1, :], vx[cur][:, h, :], b == 0, True,
                               [ptk, 'vx%d' % cur], [psk[5 + ib]])
                        CP('dve', ob[ib][:], po[:, 0:260], [psk[5 + ib]], ['ob%d' % ib])
                        DMA('pool', av[r, b], ob[ib][:], ['ob%d' % ib], ['ATT'])
                        blk += 1
                    blk += 1

        def even_SA(ei, st):
            KC = 16
            sm = sb(st, [128, 280]); DMA('sp', sm[:], I["c_smask"].ap(), (), ['sm'])
            kt = [sb(st, [128, KC, 256]) for _ in range(4)]
            prod = sb(st, [128, KC, 256])
            S = sb(st, [128, 136, 4]); Pm = sb(st, [128, 136, 4])
            kn = sb(st, [128, 8, 512])
            mx = sb(st, [128, 4]); ls = sb(st, [128, 4]); lse = sb(st, [128, 3, 4]); og = sb(st, [128, 3, 256])
            part = sb(st, [128, 256]); wts = sb(st, [128, 3, 4]); mm_ = sb(st, [128, 4]); den = sb(st, [128, 4])
            att = sb(st, [128, 260]); zer = sb(st, [128, 260])
            caches = [I["ckv0"], I["ckv1"], I["ckv2"]]
            wbs = [128, 512, 2048]
            n = 0
            for g in range(3):
                wb, dil = wbs[g], DILS[g][1]
                base = ei * 16 * wb * 512

                def cache_load(q, dst, kc, off, key):
                    m0 = kc * KC
                    for t in range(8):
                        if g == 0:
                            row0, rs = m0, 1
                        elif g == 1:
                            row0, rs = (t % 4) + 4 * m0, 4
                        else:
                            row0, rs = t + 16 * m0, 16
                        src = bass.AP(caches[g], base + row0 * 512 + off, [[wb * 512, 16], [rs * 512, KC], [1, 256]])
                        DMA(q, dst[t::8], src, (), [key])

                for t in range(8):
                    DMA('sp', kn[t::8], bass.AP(O["kv%d_s" % g], ei * NS * 512, [[8 * 512, 16], [512, 8], [1, 512]]),
                        ['o_kv'], ['kn'])
                qg = C.qs[:, 256 * g:256 * (g + 1)]
                for kc in range(8):
                    b = n % 4; n += 1
                    cache_load('sp' if b % 2 == 0 else 'act', kt[b], kc, 0, 'kt%d' % b)
                    TT('pool', prod[:], kt[b][:], qg.unsqueeze(1).to_broadcast([128, KC, 256]), ALU.mult,
                       ['kt%d' % b, 'qs'], ['prod'])
                    RED(S[:, kc * KC:(kc + 1) * KC, :].rearrange("p k h -> p (k h)"),
                        prod[:].rearrange("p k (h d) -> p (k h) d", d=64), ALU.add, ['prod'], ['S'])
                TT('pool', prod[:, 0:8, :], kn[:, :, 0:256], qg.unsqueeze(1).to_broadcast([128, 8, 256]), ALU.mult,
                   ['kn', 'qs'], ['prod'])
                RED(S[:, 128:136, :].rearrange("p k h -> p (k h)"),
                    prod[:, 0:8, :].rearrange("p k (h d) -> p (k h) d", d=64), ALU.add, ['prod'], ['S'])
                if g < 2:
                    TT('dve', S[:, 0:128, :], S[:, 0:128, :],
                       sm[:, 128 * g:128 * (g + 1)].unsqueeze(2).to_broadcast([128, 128, 4]), ALU.add, ['S', 'sm'], ['S'])
                TT('dve', S[:, 128:136, :], S[:, 128:136, :],
                   sm[:, 256 + 8 * g:256 + 8 * (g + 1)].unsqueeze(2).to_broadcast([128, 8, 4]), ALU.add, ['S', 'sm'], ['S'])
                Sv = S[:].rearrange("p k h -> p h k")
                RED(mx[:], Sv, ALU.max, ['S'], ['mx'])
                TT('dve', S[:], S[:], mx[:].unsqueeze(1).to_broadcast([128, 136, 4]), ALU.subtract, ['S', 'mx'], ['S'])
                ACT(Pm[:], S[:], AF.Exp, ['S'], ['Pm'], scale=0.125)
                RED(ls[:], Pm[:].rearrange("p k h -> p h k"), ALU.add, ['Pm'], ['ls'])
                ACT(lse[:, g, :], ls[:], AF.Ln, ['ls'], ['lse'])
                STT(lse[:, g, :], mx[:], 0.125, lse[:, g, :], ALU.mult, ALU.add, ['mx', 'lse'], ['lse'])
                RCP(ls[:], ls[:], ['ls'], ['ls'])
                TT('dve', Pm[:], Pm[:], ls[:].unsqueeze(1).to_broadcast([128, 136, 4]), ALU.mult, ['Pm', 'ls'], ['Pm'])
                for kc in range(9):
                    if kc < 8:
                        b = n % 4; n += 1
                        cache_load('sp' if b % 2 == 0 else 'act', kt[b], kc, 256, 'kt%d' % b)
                        vsrc, nk, vk = kt[b][:], KC, 'kt%d' % b
                        pp = Pm[:, kc * KC:(kc + 1) * KC, :]
                    else:
                        vsrc, nk, vk = kn[:, :, 256:512], 8, 'kn'
                        pp = Pm[:, 128:136, :]
                    TT('pool', prod[:, 0:nk, :].rearrange("p k (h d) -> p k h d", d=64),
                       vsrc.rearrange("p k (h d) -> p k h d", d=64),
                       pp.unsqueeze(3).to_broadcast([128, nk, 4, 64]), ALU.mult, [vk, 'Pm'], ['prod'])
                    dst = og[:, g, :] if kc == 0 else part[:]
                    RED(dst, prod[:, 0:nk, :].rearrange("p k c -> p c k"), ALU.add, ['prod'], ['og' if kc == 0 else 'part'])
                    if kc > 0:
                        TT('dve', og[:, g, :], og[:, g, :], part[:], ALU.add, ['og', 'part'], ['og'])
            RED(mm_[:], lse[:].rearrange("p g h -> p h g"), ALU.max, ['lse'], ['mm'])
            TT('dve', wts[:], lse[:], mm_[:].unsqueeze(1).to_broadcast([128, 3, 4]), ALU.subtract, ['lse', 'mm'], ['wts'])
            ACT(wts[:], wts[:], AF.Exp, ['wts'], ['wts'])
            RED(den[:], wts[:].rearrange("p g h -> p h g"), ALU.add, ['wts'], ['den'])
            RCP(den[:], den[:], ['den'], ['den'])
            TT('dve', wts[:], wts[:], den[:].unsqueeze(1).to_broadcast([128, 3, 4]), ALU.mult, ['wts', 'den'], ['wts'])
            TT('dve', og[:].rearrange("p g (h d) -> p g h d", d=64), og[:].rearrange("p g (h d) -> p g h d", d=64),
               wts[:].unsqueeze(3).to_broadcast([128, 3, 4, 64]), ALU.mult, ['og', 'wts'], ['og'])
            RED(att[:, 0:256], og[:].rearrange("p g c -> p c g"), ALU.add, ['og'], ['att'])
            MS('pool', att[:, 256:260], 1.0, ['att'])
            MS('pool', zer[:], 0.0, ['zer'])
            a65 = sb(st, [128, 260])
            CP('dve', a65[:].rearrange("p (h e) -> p h e", e=65)[:, :, 0:64],
               att[:, 0:256].rearrange("p (h d) -> p h d", d=64), ['att'], ['a65'])
            MS('pool', a65[:].rearrange("p (h e) -> p h e", e=65)[:, :, 64:65], 1.0, ['a65'])
            DMA('pool', ATT.ap()[0, SEQ:NT, :], a65[:], ['a65'], ['ATT'])
            DMA('pool', ATT.ap()[1, SEQ:NT, :], zer[:], ['zer'], ['ATT'])
            DMA('pool', ATT.ap()[2, SEQ:NT, :], zer[:], ['zer'], ['ATT'])

        def even_E3(l, ei, st):
            T0 = 512
            wo = sb(st, [128, 6, D], BF16)
            stg = [sb(st, [128, D]) for _ in range(2)]
            load_w(wo, I["w_o"].ap()[ei], 6, D, 'wo', stg)
            xt = sb(st, [128, 8, T0]); ycb = sb(st, [128, 4, T0], BF16); afm = sb(st, [128, 2, T0], BF16)
            a3 = [sb(st, [128, 3, 260]) for _ in range(2)]; atm = sb(st, [128, 256]); rc = sb(st, [128, 4])
            tmp = sb(st, [128, T0])
            YCv = YC.ap().rearrange("(c p) t -> p c t", p=128)
            n = 0
            for (c0, T, sample) in tiles(T0):
                DMA('sp', xt[:, :, 0:T], XFv[:, :, c0:c0 + T], ['XF'], ['xt'])
                DMA('act', ycb[:, :, 0:T], YCv[:, :, c0:c0 + T], ['YC'], ['ycb'])
                for s4 in range(T // 128):
                    b = n % 2; n += 1
                    t0 = c0 + s4 * 128
                    DMA('sp', a3[b][:], ATT.ap()[:, t0:t0 + 128, :].rearrange("g t c -> t g c"), ['ATT'], ['a3%d' % b])
                    TT('pool', a3[b][:, 0, :], a3[b][:, 0, :], a3[b][:, 1, :], ALU.add, ['a3%d' % b], ['a3%d' % b])
                    TT('pool', a3[b][:, 0, :], a3[b][:, 0, :], a3[b][:, 2, :], ALU.add, ['a3%d' % b], ['a3%d' % b])
                    a65 = a3[b][:, 0, :].rearrange("p (h e) -> p h e", e=65)
                    RCP(rc[:], a65[:, :, 64], ['a3%d' % b], ['rc'])
                    TT('dve', atm[:].rearrange("p (h d) -> p h d", d=64), a65[:, :, 0:64],
                       rc[:].unsqueeze(2).to_broadcast([128, 4, 64]), ALU.mult, ['a3%d' % b, 'rc'], ['atm'])
                    for c in range(2):
                        TR(PS[6][:, c * 128:(c + 1) * 128], atm[:, c * 128:(c + 1) * 128], ident[:], ['atm', 'ident'],
                           [psk[6]])
                    CP('act', afm[:, :, s4 * 128:(s4 + 1) * 128], PS[6][:, 0:256].rearrange("p (c t) -> p c t", c=2),
                       [psk[6]], ['afm'])
                for m in range(8):
                    pm = PS[1 + (m % 4)]; pk = psk[1 + (m % 4)]
                    for kk in range(6):
                        rhs = ycb[:, kk, 0:T] if kk < 4 else afm[:, kk - 4, 0:T]
                        MM(pm[:, 0:T], wo[:, kk, m * 128:(m + 1) * 128], rhs, kk == 0, kk == 5,
                           ['wo', 'ycb', 'afm'], [pk])
                    TT('dve', v2(tmp[:, 0:T], sample), v2(pm[:, 0:T], sample), modbc1(l, 2, m, T, sample), ALU.mult,
                       [pk, 'MOD'], ['tmp'])
                    TT('pool', xt[:, m, 0:T], xt[:, m, 0:T], tmp[:, 0:T], ALU.add, ['xt', 'tmp'], ['xt'])
                DMA('pool', XFv[:, :, c0:c0 + T], xt[:, :, 0:T], ['xt'], ['XF'])

        def ffn(l, st):
            T0 = 256
            wg = sb(st, [128, 8, DFF], BF16); wu = sb(st, [128, 8, DFF], BF16); wd = sb(st, [128, NFF, D], BF16)
            with contextlib.ExitStack() as st2:
                stg = [sb(st2, [128, DFF]) for _ in range(2)]
                load_w(wg, I["w_ff_gate"].ap()[l], 8, DFF, 'wg', stg)
                load_w(wu, I["w_ff_up"].ap()[l], 8, DFF, 'wu', stg)
                load_w(wd, I["w_ff_down"].ap()[l], NFF, D, 'wd', stg)
                P.barrier()
            xts = [sb(st, [128, 8, T0]) for _ in range(2)]; sqb = sb(st, [128, 8, T0], BF16)
            hbs = [sb(st, [128, 8, T0], BF16) for _ in range(2)]; rstd = sb(st, [128, T0]); act = sb(st, [128, NFF, T0], BF16)
            sg = [sb(st, [128, T0]) for _ in range(2)]; tmp = sb(st, [128, T0])
            tl = tiles(T0)

            def prep(it):
                c0, T, sample = tl[it]
                bb = it % 2
                kx, kh = 'xt%d' % bb, 'hb%d' % bb
                DMA('sp', xts[bb][:, :, 0:T], XFv[:, :, c0:c0 + T], ['XF'], [kx])
                norm_mod(xts[bb], sqb, hbs[bb], rstd, T, l, 3, sample, kx=kx, kh=kh)
                DMA('sp', xts[bb][:, :, 0:T], XFv[:, :, c0:c0 + T], ['XF'], [kx])

            prep(0)
            for it, (c0, T, sample) in enumerate(tl):
                bb = it % 2
                xt, hb = xts[bb], hbs[bb]
                kx, kh = 'xt%d' % bb, 'hb%d' % bb
                for j in range(NFF):
                    if j == 12 and it + 1 < len(tl):
                        prep(it + 1)
                    b = j % 2
                    pg, pu = PS[1 + 2 * b], PS[2 + 2 * b]
                    kg, ku = psk[1 + 2 * b], psk[2 + 2 * b]
                    for k in range(8):
                        MM(pg[:, 0:T], wg[:, k, j * 128:(j + 1) * 128], hb[:, k, 0:T], k == 0, k == 7, ['wg', kh], [kg])
                    for k in range(8):
                        MM(pu[:, 0:T], wu[:, k, j * 128:(j + 1) * 128], hb[:, k, 0:T], k == 0, k == 7, ['wu', kh], [ku])
                    ACT(sg[b][:, 0:T], pg[:, 0:T], AF.Silu, [kg], ['sg%d' % b])
                    TT('dve', act[:, j, 0:T], pu[:, 0:T], sg[b][:, 0:T], ALU.mult, [ku, 'sg%d' % b], ['act'])
                for m in range(8):
                    pm = PS[5 + (m % 2)]; pk = psk[5 + (m % 2)]
                    for j in range(NFF):
                        MM(pm[:, 0:T], wd[:, j, m * 128:(m + 1) * 128], act[:, j, 0:T], j == 0, j == NFF - 1,
                           ['wd', 'act'], [pk])
                    TT('dve', v2(tmp[:, 0:T], sample), v2(pm[:, 0:T], sample), modbc1(l, 5, m, T, sample), ALU.mult,
                       [pk, 'MOD'], ['tmp'])
                    TT('pool', xt[:, m, 0:T], xt[:, m, 0:T], tmp[:, 0:T], ALU.add, [kx, 'tmp'], [kx])
                DMA('pool', XFv[:, :, c0:c0 + T], xt[:, :, 0:T], [kx], ['XF'])

        def odd_O1(l, oi, st):
            T0 = 256
            PI = float(np.pi)
            bbr = sb(st, [128, 32, 128], BF16); bbi = sb(st, [128, 32, 128], BF16)
            ccr = sb(st, [128, 32, 128], BF16); cci = sb(st, [128, 32, 128], BF16); ccn = sb(st, [128, 32, 128], BF16)
            cosT = sb(st, [128, 32, 128]); sinT = sb(st, [128, 32, 128])
            rho = sb(st, [128, 32]); dsk = sb(st, [128, 8]); bgl = sb(st, [128, 16])
            wgl = sb(st, [128, 8, 2 * D], BF16)
            Sre = sb(st, [128, 32]); Sim = sb(st, [128, 32])
            S0r = sb(st, [128, 32, 16]); S0i = sb(st, [128, 32, 16]); Sor = sb(st, [128, 32, 16]); Soi = sb(st, [128, 32, 16])
            with contextlib.ExitStack() as st2:
                stg = [sb(st2, [128, 2 * D]) for _ in range(2)]
                load_w(wgl, I["s5_w_glu"].ap()[oi], 8, 2 * D, 'wgl', stg)
                DMA('pool', dsk[:], I["s5_d"].ap()[oi].rearrange("(c p) -> p c", p=128), (), ['dsk'])
                DMA('pool', bgl[:], I["s5_b_glu"].ap()[oi].rearrange("(c p) -> p c", p=128), (), ['bgl'])
                lr = sb(st2, [128, 32]); li = sb(st2, [128, 32]); dt = sb(st2, [128, 32])
                DMA('sp', lr[:], I["s5_lam_re"].ap()[oi].rearrange("(sc gl) p -> (gl p) sc", gl=2), (), ['lr'])
                DMA('sp', li[:], I["s5_lam_im"].ap()[oi].rearrange("(sc gl) p -> (gl p) sc", gl=2), (), ['li'])
                for gl in range(2):
                    DMA('sp', dt[gl * 64:(gl + 1) * 64, :],
                        bass.AP(I["s5_log_dt"], oi * 64 + gl, [[0, 64], [2, 32]]), (), ['dt'])
                br = sb(st2, [128, 32, 16]); bi = sb(st2, [128, 32, 16])
                DMA('sp', br[:], I["s5_b_re"].ap()[oi].rearrange("(sc gl) p h -> (gl p) sc h", gl=2), (), ['br'])
                DMA('sp', bi[:], I["s5_b_im"].ap()[oi].rearrange("(sc gl) p h -> (gl p) sc h", gl=2), (), ['bi'])
                ACT(dt[:], dt[:], AF.Exp, ['dt'], ['dt'])
                mag = sb(st2, [128, 32]); th = sb(st2, [128, 32]); ta = sb(st2, [128, 32]); tb = sb(st2, [128, 32])
                TT('dve', mag[:], lr[:], dt[:], ALU.mult, ['lr', 'dt'], ['mag'])
                ACT(mag[:], mag[:], AF.Exp, ['mag'], ['mag'])
                TT('dve', th[:], li[:], dt[:], ALU.mult, ['li', 'dt'], ['th'])
                cs0 = sb(st2, [128, 32]); sn0 = sb(st2, [128, 32])

                def sin_of(dst, shift, key):
                    TS('dve', ta[:], th[:], shift + 2 * PI, None, ALU.add, None, ['th'], ['ta'])
                    CP('dve', tb[:], ta[:], ['ta'], ['tb'])
                    for j in range(7):
                        thr = (2 * j + 1) * PI
                        TS('dve', dst[:], ta[:], thr, -2 * PI, ALU.is_ge, ALU.mult, ['ta'], [key])
                        TT('dve', tb[:], tb[:], dst[:], ALU.add, ['tb', key], ['tb'])
                    ACT(dst[:], tb[:], AF.Sin, ['tb'], [key])

                sin_of(sn0, 0.0, 'sn0')
                sin_of(cs0, PI / 2, 'cs0')
                abr = sb(st2, [128, 32]); abi = sb(st2, [128, 32]); fre = sb(st2, [128, 32]); fim = sb(st2, [128, 32])
                den = sb(st2, [128, 32])
                TT('dve', abr[:], mag[:], cs0[:], ALU.mult, ['mag', 'cs0'], ['abr'])
                TT('dve', abi[:], mag[:], sn0[:], ALU.mult, ['mag', 'sn0'], ['abi'])
                CP('dve', rho[:], mag[:], ['mag'], ['rho'])
                TT('dve', den[:], lr[:], lr[:], ALU.mult, ['lr'], ['den'])
                TT('dve', ta[:], li[:], li[:], ALU.mult, ['li'], ['ta'])
                TT('dve', den[:], den[:], ta[:], ALU.add, ['den', 'ta'], ['den'])
                RCP(den[:], den[:], ['den'], ['den'])
                TS('dve', tb[:], abr[:], -1.0, None, ALU.add, None, ['abr'], ['tb'])
                TT('dve', fre[:], tb[:], lr[:], ALU.mult, ['tb', 'lr'], ['fre'])
                TT('dve', ta[:], abi[:], li[:], ALU.mult, ['abi', 'li'], ['ta'])
                TT('dve', fre[:], fre[:], ta[:], ALU.add, ['fre', 'ta'], ['fre'])
                TT('dve', fre[:], fre[:], den[:], ALU.mult, ['fre', 'den'], ['fre'])
                TT('dve', fim[:], abi[:], lr[:], ALU.mult, ['abi', 'lr'], ['fim'])
                TT('dve', ta[:], tb[:], li[:], ALU.mult, ['tb', 'li'], ['ta'])
                TT('dve', fim[:], fim[:], ta[:], ALU.subtract, ['fim', 'ta'], ['fim'])
                TT('dve', fim[:], fim[:], den[:], ALU.mult, ['fim', 'den'], ['fim'])
                Br = sb(st2, [128, 32, 16]); Bi = sb(st2, [128, 32, 16]); t16 = sb(st2, [128, 32, 16])
                fr_b = fre[:].unsqueeze(2).to_broadcast([128, 32, 16]); fi_b = fim[:].unsqueeze(2).to_broadcast([128, 32, 16])
                TT('dve', Br[:], br[:], fr_b, ALU.mult, ['br', 'fre'], ['Br'])
                TT('dve', t16[:], bi[:], fi_b, ALU.mult, ['bi', 'fim'], ['t16'])
                TT('dve', Br[:], Br[:], t16[:], ALU.subtract, ['Br', 't16'], ['Br'])
                TT('dve', Bi[:], bi[:], fr_b, ALU.mult, ['bi', 'fre'], ['Bi'])
                TT('dve', t16[:], br[:], fi_b, ALU.mult, ['br', 'fim'], ['t16'])
                TT('dve', Bi[:], Bi[:], t16[:], ALU.add, ['Bi', 't16'], ['Bi'])
                Z = sb(st2, [128, 128])
                for which, (srcB, dstB, key) in enumerate(((Br, bbr, 'bbr'), (Bi, bbi, 'bbi'))):
                    for sc in range(32):
                        g0 = (2 * sc) % 8
                        MS('pool', Z[:], 0.0, ['Z'])
                        CP('pool', Z[0:64, g0 * 16:(g0 + 1) * 16], srcB[0:64, sc, :], [key[0:1] + 'B' + key[2:], 'Br', 'Bi'], ['Z'])
                        CP('pool', Z[64:128, (g0 + 1) * 16:(g0 + 2) * 16], srcB[64:128, sc, :], ['Br', 'Bi'], ['Z'])
                        TR(PS[1 + sc % 2][:, 0:128], Z[:], ident[:], ['Z', 'ident'], [psk[1 + sc % 2]])
                        CP('act', dstB[:, sc, :], PS[1 + sc % 2][:, 0:128], [psk[1 + sc % 2]], [key])
                cn = sb(st2, [128, 128]); ctr = sb(st2, [128, 128])
                MS('pool', ccr[:], 0.0, ['ccr'])
                MS('pool', cci[:], 0.0, ['cci'])
                for which, (nm, dstC, key, sgn) in enumerate((("s5_c_re", ccr, 'ccr', 1.0), ("s5_c_im", cci, 'cci', -1.0))):
                    for c in range(8):
                        src = I[nm].ap()[oi, c * 8:(c + 1) * 8].rearrange("g h p -> (g h) p")
                        DMA('sp', cn[:, 0:64], src, (), ['cn'])
                        DMA('sp', cn[:, 64:128], src, (), ['cn'])
                        TR(PS[3][:, 0:128], cn[:], ident[:], ['cn', 'ident'], [psk[3]])
                        TS('dve', ctr[:], PS[3][:, 0:128], sgn, None, ALU.mult, None, [psk[3]], ['ctr'])
                        for q in range(4):
                            sc = c * 4 + q
                            g0 = 2 * q
                            CP('pool', dstC[0:64, sc, g0 * 16:(g0 + 1) * 16], ctr[0:64, g0 * 16:(g0 + 1) * 16], ['ctr'], [key])
                            CP('pool', dstC[64:128, sc, (g0 + 1) * 16:(g0 + 2) * 16],
                               ctr[64:128, (g0 + 1) * 16:(g0 + 2) * 16], ['ctr'], [key])
                TS('pool', ccn[:].rearrange("p a b -> p (a b)"), ccr[:].rearrange("p a b -> p (a b)"), -1.0, None, ALU.mult, None, ['ccr'], ['ccn'])
                CP('dve', cosT[:, :, 0], cs0[:], ['cs0'], ['tab'])
                CP('dve', sinT[:, :, 0], sn0[:], ['sn0'], ['tab'])
                tw = sb(st2, [128, 32, 64])
                m = 1
                while m < 128:
                    cm = cosT[:, :, m - 1:m].to_broadcast([128, 32, m]); sm_ = sinT[:, :, m - 1:m].to_broadcast([128, 32, m])
                    c_lo, s_lo = cosT[:, :, 0:m], sinT[:, :, 0:m]
                    c_hi, s_hi = cosT[:, :, m:2 * m], sinT[:, :, m:2 * m]
                    TT('dve', c_hi, c_lo, cm, ALU.mult, ['tab'], ['tab'])
                    TT('dve', tw[:, :, 0:m], s_lo, sm_, ALU.mult, ['tab'], ['tw'])
                    TT('dve', c_hi, c_hi, tw[:, :, 0:m], ALU.subtract, ['tab', 'tw'], ['tab'])
                    TT('dve', s_hi, s_lo, cm, ALU.mult, ['tab'], ['tab'])
                    TT('dve', tw[:, :, 0:m], c_lo, sm_, ALU.mult, ['tab'], ['tw'])
                    TT('dve', s_hi, s_hi, tw[:, :, 0:m], ALU.add, ['tab', 'tw'], ['tab'])
                    m *= 2
                sti = sb(st2, [16, 8192])
                DMA('sp', sti[:], I["st5"].ap()[oi], (), ['sti'])
                stv = sti[:].rearrange("s (sc x c) -> s sc x c", sc=32, c=2)
                for sc in range(32):
                    for c2, dstS in ((0, S0r), (1, S0i)):
                        TR(PS[5][:, (sc % 16) * 32 + c2 * 16:(sc % 16) * 32 + c2 * 16 + 16], stv[:, sc, :, c2],
                           ident[0:16, 0:16], ['sti', 'ident'], [psk[5]])
                    if sc % 16 == 15:
                        h0 = sc - 15
                        pv = PS[5][:, 0:512].rearrange("p (sc c s) -> p sc c s", c=2, s=16)
                        CP('dve', S0r[:, h0:h0 + 16, :], pv[:, :, 0, :], [psk[5]], ['S0'])
                        CP('dve', S0i[:, h0:h0 + 16, :], pv[:, :, 1, :], [psk[5]], ['S0'])
                MS('pool', Sre[:], 0.0, ['Sc%d' % i_ for i_ in range(32)])
                MS('pool', Sim[:], 0.0, ['Sc%d' % i_ for i_ in range(32)])
                P.barrier()
            st3 = contextlib.ExitStack()
            xt = sb(st3, [128, 8, T0]); sqb = sb(st3, [128, 8, T0], BF16)
            hb = sb(st3, [128, 8, T0], BF16); hf = sb(st3, [128, 8, T0]); rstd = sb(st3, [128, T0])
            zz = sb(st3, [128, 8, T0], BF16)
            burs = [sb(st3, [128, T0]) for _ in range(4)]; buis = [sb(st3, [128, T0]) for _ in range(4)]
            p1 = sb(st3, [128, T0]); p2 = sb(st3, [128, T0]); d1 = sb(st3, [128, T0]); d2 = sb(st3, [128, T0])
            bprs = [sb(st3, [128, T0]) for _ in range(4)]; bpis = [sb(st3, [128, T0]) for _ in range(4)]
            wrs = [sb(st3, [128, T0]) for _ in range(4)]; wis = [sb(st3, [128, T0]) for _ in range(4)]
            prods = [[sb(st3, [128, T0], BF16) for _ in range(4)] for _ in range(4)]
            rhs_ = sb(st3, [128, 128]); tn = sb(st3, [128, 16]); yv = sb(st3, [128, T0]); sgm = sb(st3, [128, T0])
            for (c0, T, sample) in tiles(T0):
                DMA('sp', xt[:, :, 0:T], XFv[:, :, c0:c0 + T], ['XF'], ['xt'])
                norm_mod(xt, sqb, hb, rstd, T, l, 0, sample, hf=hf)
                DMA('sp', xt[:, :, 0:T], XFv[:, :, c0:c0 + T], ['XF'], ['xt'])
                nsub = T // 128

                def tabv(tab, sc):
                    if sample:
                        return tab[:, sc, 0:8].unsqueeze(1).to_broadcast([128, 16, 8])
                    return tab[:, sc, :].unsqueeze(1).to_broadcast([128, nsub, 128])

                def tv(ap):
                    return ap.rearrange("p (s t) -> p s t", t=8) if sample else ap.rearrange("p (k t) -> p k t", t=128)

                def front(sc):
                    c, q = sc // 4, sc % 4
                    pr, pi_ = PS[1 + 2 * (q % 2)], PS[2 + 2 * (q % 2)]
                    kr, ki = psk[1 + 2 * (q % 2)], psk[2 + 2 * (q % 2)]
                    bur, bui, bpr, bpi = burs[q], buis[q], bprs[q], bpis[q]
                    kbur, kbui, kbpr, kbpi = ['%s%d' % (n_, q) for n_ in ('bur', 'bui', 'bpr', 'bpi')]
                    MM(pr[:, 0:T], bbr[:, sc, :], hb[:, c, 0:T], True, True, ['bbr', 'hb'], [kr])
                    MM(pi_[:, 0:T], bbi[:, sc, :], hb[:, c, 0:T], True, True, ['bbi', 'hb'], [ki])
                    cv, sv = tabv(cosT, sc), tabv(sinT, sc)
                    if sc % 2 == 1:
                        return [
                            lambda: TT('dve', tv(d1[:, 0:T]), tv(pr[:, 0:T]), cv, ALU.mult, [kr, 'tab'], ['d1']),
                            lambda: TT('dve', tv(d2[:, 0:T]), tv(pi_[:, 0:T]), sv, ALU.mult, [ki, 'tab'], ['d2']),
                            lambda: TT('dve', bpr[:, 0:T], d1[:, 0:T], d2[:, 0:T], ALU.add, ['d1', 'd2'], [kbpr]),
                            lambda: TT('dve', tv(d1[:, 0:T]), tv(pi_[:, 0:T]), cv, ALU.mult, [ki, 'tab'], ['d1']),
                            lambda: TT('dve', tv(d2[:, 0:T]), tv(pr[:, 0:T]), sv, ALU.mult, [kr, 'tab'], ['d2']),
                            lambda: TT('dve', bpi[:, 0:T], d1[:, 0:T], d2[:, 0:T], ALU.subtract, ['d1', 'd2'], [kbpi]),
                        ]
                    CP('act', bur[:, 0:T], pr[:, 0:T], [kr], [kbur])
                    CP('act', bui[:, 0:T], pi_[:, 0:T], [ki], [kbui])
                    TT('pool', tv(p1[:, 0:T]), tv(bur[:, 0:T]), cv, ALU.mult, [kbur, 'tab'], ['p1'])
                    TT('pool', tv(p2[:, 0:T]), tv(bui[:, 0:T]), sv, ALU.mult, [kbui, 'tab'], ['p2'])
                    TT('pool', bpr[:, 0:T], p1[:, 0:T], p2[:, 0:T], ALU.add, ['p1', 'p2'], [kbpr])
                    TT('pool', tv(p1[:, 0:T]), tv(bui[:, 0:T]), cv, ALU.mult, [kbui, 'tab'], ['p1'])
                    TT('pool', tv(p2[:, 0:T]), tv(bur[:, 0:T]), sv, ALU.mult, [kbur, 'tab'], ['p2'])
                    TT('pool', bpi[:, 0:T], p1[:, 0:T], p2[:, 0:T], ALU.subtract, ['p1', 'p2'], [kbpi])
                    return []

                def chain(sc):
                    q = sc % 4
                    bpr, bpi, wr, wi = bprs[q], bpis[q], wrs[q], wis[q]
                    kbpr, kbpi, kwr, kwi = ['%s%d' % (n_, q) for n_ in ('bpr', 'bpi', 'wr', 'wi')]
                    th = []
                    if not sample:
                        rb = rho[:, sc:sc + 1].to_broadcast([128, 128])
                        c1, s1 = cosT[:, sc, 127:128], sinT[:, sc, 127:128]
                        for k in range(nsub):
                            sl = slice(k * 128, (k + 1) * 128)
                            e = k * 128 + 127
                            th.append(lambda sl=sl: SCAN(wr[:, sl], rb, bpr[:, sl], Sre[:, sc:sc + 1], ['rho', kbpr, 'Sc%d' % sc], [kwr]))
                            th.append(lambda sl=sl: SCAN(wi[:, sl], rb, bpi[:, sl], Sim[:, sc:sc + 1], ['rho', kbpi, 'Sc%d' % sc], [kwi]))
                            th.append(lambda e=e: TT('dve', tn[:, 2 * q:2 * q + 1], wi[:, e:e + 1], s1, ALU.mult, [kwi, 'tab'], ['tn%d' % q]))
                            th.append(lambda e=e: TT('dve', tn[:, 2 * q + 1:2 * q + 2], wr[:, e:e + 1], s1, ALU.mult, [kwr, 'tab'], ['tn%d' % q]))
                            th.append(lambda e=e: STT(Sre[:, sc:sc + 1], wr[:, e:e + 1], c1, tn[:, 2 * q:2 * q + 1], ALU.mult, ALU.subtract,
                                                      [kwr, 'tab', 'tn%d' % q], ['Sc%d' % sc]))
                            th.append(lambda e=e: STT(Sim[:, sc:sc + 1], wi[:, e:e + 1], c1, tn[:, 2 * q + 1:2 * q + 2], ALU.mult, ALU.add,
                                                      [kwi, 'tab', 'tn%d' % q], ['Sc%d' % sc]))
                    else:
                        b0r = bpr[:, 0:T].rearrange("p (s t) -> p s t", t=8)[:, :, 0]
                        b0i = bpi[:, 0:T].rearrange("p (s t) -> p s t", t=8)[:, :, 0]
                        w7r = wr[:, 0:T].rearrange("p (s t) -> p s t", t=8)[:, :, 7]
                        w7i = wi[:, 0:T].rearrange("p (s t) -> p s t", t=8)[:, :, 7]
                        c8, s8 = cosT[:, sc, 7:8], sinT[:, sc, 7:8]
                        th.append(lambda: TS('dve', rhs_[:], m01[:], rho[:, sc:sc + 1], None, ALU.mult, None, ['m01', 'rho'], ['rhs']))
                        th.append(lambda: STT(b0r, S0r[:, sc, :], rho[:, sc:sc + 1], b0r, ALU.mult, ALU.add, ['S0', 'rho', kbpr], [kbpr]))
                        th.append(lambda: STT(b0i, S0i[:, sc, :], rho[:, sc:sc + 1], b0i, ALU.mult, ALU.add, ['S0', 'rho', kbpi], [kbpi]))
                        th.append(lambda: SCAN(wr[:, 0:T], rhs_[:], bpr[:, 0:T], 0.0, ['rhs', kbpr], [kwr]))
                        th.append(lambda: SCAN(wi[:, 0:T], rhs_[:], bpi[:, 0:T], 0.0, ['rhs', kbpi], [kwi]))
                        th.append(lambda: TS('dve', tn[:], w7i, s8, None, ALU.mult, None, [kwi, 'tab'], ['tns']))
                        th.append(lambda: STT(Sor[:, sc, :], w7r, c8, tn[:], ALU.mult, ALU.subtract, [kwr, 'tab', 'tns'], ['So']))
                        th.append(lambda: TS('dve', tn[:], w7r, s8, None, ALU.mult, None, [kwr, 'tab'], ['tns']))
                        th.append(lambda: STT(Soi[:, sc, :], w7i, c8, tn[:], ALU.mult, ALU.add, [kwi, 'tab', 'tns'], ['So']))
                    return th

                def filler(sc):
                    q = sc % 4
                    wr, wi = wrs[q], wis[q]
                    kwr, kwi = 'wr%d' % q, 'wi%d' % q
                    cv, sv = tabv(cosT, sc), tabv(sinT, sc)
                    return [
                        lambda: TT('dve', tv(prods[q][0][:, 0:T]), tv(wr[:, 0:T]), cv, ALU.mult, [kwr, 'tab'], ['pr%d' % q]),
                        lambda: TT('dve', tv(prods[q][1][:, 0:T]), tv(wi[:, 0:T]), sv, ALU.mult, [kwi, 'tab'], ['pr%d' % q]),
                        lambda: TT('dve', tv(prods[q][2][:, 0:T]), tv(wi[:, 0:T]), cv, ALU.mult, [kwi, 'tab'], ['pr%d' % q]),
                        lambda: TT('dve', tv(prods[q][3][:, 0:T]), tv(wr[:, 0:T]), sv, ALU.mult, [kwr, 'tab'], ['pr%d' % q]),
                    ]

                def cproj(c):
                    py = PS[5 + c % 2]; ky = psk[5 + c % 2]
                    for q in range(4):
                        sc = c * 4 + q
                        MM(py[:, 0:T], ccr[:, sc, :], prods[q][0][:, 0:T], q == 0, False, ['ccr', 'pr%d' % q], [ky])
                        MM(py[:, 0:T], ccn[:, sc, :], prods[q][1][:, 0:T], False, False, ['ccn', 'pr%d' % q], [ky])
                        MM(py[:, 0:T], cci[:, sc, :], prods[q][2][:, 0:T], False, False, ['cci', 'pr%d' % q], [ky])
                        MM(py[:, 0:T], cci[:, sc, :], prods[q][3][:, 0:T], False, q == 3, ['cci', 'pr%d' % q], [ky])
                    STT(yv[:, 0:T], hf[:, c, 0:T], dsk[:, c:c + 1], py[:, 0:T], ALU.mult, ALU.add, ['hf', 'dsk', ky], ['yv'])
                    ACT(zz[:, c, 0:T], yv[:, 0:T], AF.Gelu_apprx_tanh, ['yv'], ['zz'])

                pend = front(0)
                for t_ in pend:
                    t_()
                for sc in range(33):
                    fl = []
                    if sc + 1 < 32:
                        fl += front(sc + 1)
                    ch = chain(sc) if sc < 32 else []
                    fl += filler(sc - 1) if sc >= 1 else []
                    i_f = 0
                    for i_c, t_ in enumerate(ch):
                        t_()
                        if i_f < len(fl):
                            fl[i_f](); i_f += 1
                    while i_f < len(fl):
                        fl[i_f](); i_f += 1
                    if sc >= 1 and (sc - 1) % 4 == 3:
                        cproj((sc - 1) // 4)
                for m_ in range(8):
                    pa, pb = PS[1 + 2 * (m_ % 2)], PS[2 + 2 * (m_ % 2)]
                    ka, kb = psk[1 + 2 * (m_ % 2)], psk[2 + 2 * (m_ % 2)]
                    for k in range(8):
                        MM(pa[:, 0:T], wgl[:, k, m_ * 128:(m_ + 1) * 128], zz[:, k, 0:T], k == 0, k == 7, ['wgl', 'zz'], [ka])
                    for k in range(8):
                        MM(pb[:, 0:T], wgl[:, k, D + m_ * 128:D + (m_ + 1) * 128], zz[:, k, 0:T], k == 0, k == 7,
                           ['wgl', 'zz'], [kb])
                    ACT(sgm[:, 0:T], pb[:, 0:T], AF.Sigmoid, [kb, 'bgl'], ['sgm'], bias=bgl[:, 8 + m_:9 + m_])
                    STT(yv[:, 0:T], pa[:, 0:T], bgl[:, m_:m_ + 1], sgm[:, 0:T], ALU.add, ALU.mult, [ka, 'bgl', 'sgm'], ['yv'])
                    TT('pool', v2(yv[:, 0:T], sample), v2(yv[:, 0:T], sample), modbc1(l, 2, m_, T, sample), ALU.mult,
                       ['yv', 'MOD'], ['yv'])
                    TT('pool', xt[:, m_, 0:T], xt[:, m_, 0:T], yv[:, 0:T], ALU.add, ['xt', 'yv'], ['xt'])
                DMA('pool', XFv[:, :, c0:c0 + T], xt[:, :, 0:T], ['xt'], ['XF'])
            P.barrier()
            st3.close()
            so = sb(st, [32, 128, 2])
            for c2, srcS in ((0, Sre), (1, Sim)):
                TR(PS[1][0:32, c2 * 128:(c2 + 1) * 128], srcS[:], ident[:], ['Sc%d' % i_ for i_ in range(32)] + ['ident'], [psk[1]])
            CP('dve', so[:].rearrange("p x c -> p c x"), PS[1][0:32, 0:256].rearrange("p (c x) -> p c x", c=2), [psk[1]], ['so'])
            DMA('pool', O["s5_p"].ap()[oi], so[:].rearrange("p x c -> p (x c)"), ['so'], ['o_s5p'])
            sso = sb(st, [16, 8192])
            ssv = sso[:].rearrange("s (sc x c) -> s sc x c", sc=32, c=2)
            for sc in range(32):
                for c2, srcS in ((0, Sor), (1, Soi)):
                    TR(PS[2][0:16, c2 * 128:(c2 + 1) * 128], srcS[:, sc, :], ident[:], ['So', 'ident'], [psk[2]])
                CP('dve', ssv[:, sc].rearrange("s x c -> s c x"), PS[2][0:16, 0:256].rearrange("s (c x) -> s c x", c=2),
                   [psk[2]], ['sso'])
            DMA('pool', O["s5_s"].ap()[oi], sso[:], ['sso'], ['o_s5s'])

        def final(st):
            fg = sb(st, [128, D])
            DMA('sp', fg[:], I["final_g"].ap().to_broadcast([128, D]), (), ['fg'])
            xf = [sb(st, [128, 8, 128]) for _ in range(2)]
            xtm = [sb(st, [128, D]) for _ in range(2)]
            junk = sb(st, [128, D]); ss = sb(st, [128, 1])
            for n in range(NSUB + 1):
                b = n % 2
                DMA('sp', xf[b][:], XFv[:, :, n * 128:(n + 1) * 128], ['XF'], ['xf%d' % b])
                for hh in range(2):
                    pt = PS[1 + 2 * b + hh]; pk = psk[1 + 2 * b + hh]
                    for q in range(4):
                        TR(pt[:, q * 128:(q + 1) * 128], xf[b][:, hh * 4 + q, :], ident[:], ['xf%d' % b, 'ident'], [pk])
                    CP('dve' if hh == 0 else 'act', xtm[b][:, hh * 512:(hh + 1) * 512], pt[:], [pk], ['xtm%d' % b])
                P.op('act', lambda e, b=b: e.activation(junk[:], xtm[b][:], AF.Square, accum_out=ss[:]),
                     ['xtm%d' % b], ['junk', 'ss'])
                ACT(ss[:], ss[:], AF.Sqrt, ['ss', 'epsT'], ['ss'], scale=1.0 / D, bias=epsT[:])
                RCP(ss[:], ss[:], ['ss'], ['ss'])
                STT(xtm[b][:], xtm[b][:], ss[:], fg[:], ALU.mult, ALU.mult, ['xtm%d' % b, 'ss', 'fg'], ['xtm%d' % b])
                dst = O["y_p"].ap()[n * 128:(n + 1) * 128, :] if n < NSUB else O["y_s"].ap()
                DMA('pool', dst, xtm[b][:], ['xtm%d' % b], ['o_y'])

        for l in range(LAYERS):
            if l % 2 == 0:
                ei = l // 2
                stq = contextlib.ExitStack()
                C.qs = sb(stq, [128, 2304])
                with contextlib.ExitStack() as st:
                    even_E1(l, ei, st)
                    P.barrier()
                with contextlib.ExitStack() as st:
                    even_E2(st)
                    P.barrier()
                with contextlib.ExitStack() as st:
                    even_SA(ei, st)
                    P.barrier()
                stq.close()
                with contextlib.ExitStack() as st:
                    even_E3(l, ei, st)
                    P.barrier()
            else:
                with contextlib.ExitStack() as st:
                    odd_O1(l, l // 2, st)
                    P.barrier()
            with contextlib.ExitStack() as st:
                ffn(l, st)
                P.barrier()
        with contextlib.ExitStack() as st:
            final(st)
        P.emit()
    return nc


def host_consts(SEQ):
    NSUB = SEQ // 128
    half = 8
    inv = (np.float32(500000.0) ** (-(2.0 / 16) * np.arange(half, dtype=np.float32))).astype(np.float32)
    pos_p = np.arange(SEQ, dtype=np.float32)
    ang = pos_p[:, None] * inv[None, :]
    rp = np.concatenate([np.cos(ang), np.sin(ang)], axis=1).astype(np.float32)
    ropep = np.ascontiguousarray(rp.reshape(NSUB, 128, 16).transpose(1, 0, 2))
    pos_s = (2048 + np.arange(8)).astype(np.float32)
    angs = pos_s[:, None] * inv[None, :]
    rs = np.concatenate([np.cos(angs), np.sin(angs)], axis=1).astype(np.float32)
    ropes = np.ascontiguousarray(np.tile(rs, (16, 1)))
    k = np.arange(128)[:, None]; q = np.arange(128)[None, :]
    amask = np.where(np.stack([(k >= q), (k <= q)], axis=1), 0.0, -30000.0).astype(np.float32)
    t = (np.arange(128) % 8)[:, None]
    NEG = np.float32(-1e30)
    m = np.arange(128)[None, :]
    sm = np.zeros((128, 280), np.float32)
    sm[:, 0:128] = np.where(m >= t, 0.0, NEG)
    sm[:, 128:256] = np.where((m == 0) & (t >= 4), NEG, 0.0)
    tp = np.arange(8)[None, :]
    sm[:, 256:264] = np.where(tp <= t, 0.0, NEG)
    sm[:, 264:272] = np.where((tp == t) | (tp == t - 4), 0.0, NEG)
    sm[:, 272:280] = np.where(tp == t, 0.0, NEG)
    m01 = np.ones((128, 128), np.float32); m01[:, 0::8] = 0.0
    return dict(c_ident=np.eye(128, dtype=np.float32), c_ropep=ropep, c_ropes=ropes, c_amask=amask,
                c_smask=sm, c_m01=m01)


_NC_CACHE = {}


def kernel(**inp):
    from concourse.bass_utils import run_bass_kernel_spmd
    f = lambda a: np.ascontiguousarray(np.asarray(a, dtype=np.float32))
    SEQ = inp["x_prompt"].shape[1]
    nb = inp["x_prompt"].shape[0]
    ncores = inp["x_sample"].shape[0] // 16
    if SEQ not in _NC_CACHE:
        _NC_CACHE[SEQ] = build(SEQ)
    nc = _NC_CACHE[SEQ]
    consts = host_consts(SEQ)
    wnames = ["norm_g", "w_ada", "b_ada", "w_in", "conv_w", "conv_b", "conv_ln_g", "conv_ln_b", "w_o",
              "s5_lam_re", "s5_lam_im", "s5_log_dt", "s5_b_re", "s5_b_im", "s5_c_re", "s5_c_im", "s5_d",
              "s5_w_glu", "s5_b_glu", "w_ff_gate", "w_ff_up", "w_ff_down"]
    shared = {k: f(inp[k]) for k in wnames}
    shared["final_g"] = f(inp["final_g"]).reshape(1, D)
    shared.update(consts)
    in_maps = []
    for i in range(ncores):
        sl = slice(16 * i, 16 * (i + 1))
        m = dict(shared)
        m["xp"] = f(inp["x_prompt"][i % nb])
        m["xs"] = f(inp["x_sample"][sl]).reshape(NS, D)
        m["call"] = f(np.concatenate([inp["c_prompt"][i % nb][None], inp["c_sample"][sl]], axis=0))
        m["cconv"] = f(inp["cache_conv"][:, sl]).reshape(2, 480, 512)
        m["ckv0"] = f(inp["cache_kv_g0"][:, sl]).reshape(2, 16, -1, 512)
        m["ckv1"] = f(inp["cache_kv_g1"][:, sl]).reshape(2, 16, -1, 512)
        m["ckv2"] = f(inp["cache_kv_g2"][:, sl]).reshape(2, 16, -1, 512)
        m["st5"] = f(inp["state_s5"][:, sl]).reshape(2, 16, 8192)
        in_maps.append(m)
    res = run_bass_kernel_spmd(nc, in_maps, core_ids=list(range(ncores)))
    kernel.last_res = res
    R = res.results
    keep = [min(w, SEQ) for w, _ in DILS]
    y_p = np.stack([R[i]["y_p"] for i in range(nb)]).reshape(nb, SEQ, D)
    y_s = np.concatenate([R[i]["y_s"].reshape(16, 8, D) for i in range(ncores)], axis=0)
    conv_p = np.stack([R[i]["conv_p"] for i in range(nb)], axis=1)
    kvp = [np.stack([R[i]["kv%d_p" % g].reshape(2, keep[g], 2, 4, 64) for i in range(nb)], axis=1) for g in range(3)]
    s5_p = np.stack([R[i]["s5_p"].reshape(2, 64, 64, 2) for i in range(nb)], axis=1)
    conv_s = np.concatenate([R[i]["conv_s"] for i in range(ncores)], axis=1)
    kvs = [np.concatenate([R[i]["kv%d_s" % g].reshape(2, 16, 8, 2, 4, 64) for i in range(ncores)], axis=1)
           for g in range(3)]
    s5_s = np.concatenate([R[i]["s5_s"].reshape(2, 16, 64, 64, 2) for i in range(ncores)], axis=1)
    outs = (y_p, y_s, conv_p, kvp[0], kvp[1], kvp[2], s5_p, conv_s, kvs[0], kvs[1], kvs[2], s5_s)
    return tuple(np.ascontiguousarray(o, dtype=np.float32) for o in outs)
```

```python
import contextlib

import numpy as np
import concourse.bass as bass
import concourse.mybir as mybir

F32 = mybir.dt.float32
BF16 = mybir.dt.bfloat16
AF = mybir.ActivationFunctionType
ALU = mybir.AluOpType
AX = mybir.AxisListType

EPOCH = 12000
NDSEM = 6


class Prog:
    def __init__(self, nc, stack):
        self.nc = nc
        self.stack = stack
        self.names = ['pe', 'dve', 'act', 'pool', 'sp']
        self.ops = {e: [] for e in self.names}
        self.cnt = {e: 0 for e in self.names}
        self.csem = {e: self._newsem() for e in self.names}
        self.dsem = {e: [self._newsem() for _ in range(NDSEM)] for e in ('sp', 'pool', 'act')}
        self.dcnt = {e: 0 for e in ('sp', 'pool', 'act')}
        self.dlast = {e: [0] * NDSEM for e in ('sp', 'pool', 'act')}
        self.last_w = {}
        self.readers = {}
        self.waited = {e: {} for e in self.names}
        self.pending = {e: [] for e in self.names}

    def _newsem(self):
        self.nsem = getattr(self, 'nsem', 0) + 1
        return self.stack.enter_context(self.nc.semaphore(name="sem%d" % self.nsem))

    def _deps(self, eng, r, w, is_dma=False):
        toks = []
        for k in r:
            t = self.last_w.get(k)
            if t is not None:
                toks.append((t, True))
        for k in w:
            t = self.last_w.get(k)
            if t is not None:
                toks.append((t, False))
            for t in self.readers.get(k, {}).values():
                toks.append((t, False))
        need = {}
        for (sem, val, teng, isdma), raw in toks:
            if teng == eng and not isdma and not raw and not is_dma:
                continue
            if self.waited[eng].get(id(sem), 0) >= val:
                continue
            cur = need.get(id(sem))
            if cur is None or cur[1] < val:
                need[id(sem)] = (sem, val)
        for sem, val in need.values():
            self.waited[eng][id(sem)] = val
        out = list(need.values()) + self.pending[eng]
        self.pending[eng] = []
        return out

    def _record(self, tok, r, w):
        for k in w:
            self.last_w[k] = tok
            self.readers[k] = {}
        for k in r:
            d = self.readers.setdefault(k, {})
            d[id(tok[0])] = tok

    def barrier(self):
        toks = []
        for e in self.names:
            if self.cnt[e] > 0:
                toks.append((self.csem[e], self.cnt[e], e))
        for q in self.dsem:
            for i, sem in enumerate(self.dsem[q]):
                if self.dlast[q][i] > 0:
                    toks.append((sem, self.dlast[q][i], None))
        for e in self.names:
            for s, v, te in toks:
                if te == e:
                    continue
                if self.waited[e].get(id(s), 0) < v:
                    self.pending[e].append((s, v))
                    self.waited[e][id(s)] = v

    def op(self, eng, fn, r=(), w=()):
        waits = self._deps(eng, r, w)
        if self.cnt[eng] >= EPOCH:
            self.csem[eng] = self._newsem()
            self.cnt[eng] = 0
        self.cnt[eng] += 1
        sem = self.csem[eng]
        tok = (sem, self.cnt[eng], eng, False)
        self.ops[eng].append((fn, waits, sem, 1))
        self._record(tok, r, w)
        return tok

    def dma(self, q, out, in_, r=(), w=(), **kw):
        waits = self._deps(q, r, w, True)
        n = self.dcnt[q]
        self.dcnt[q] += 1
        i = n % NDSEM
        sem = self.dsem[q][i]
        prev = self.dlast[q][i]
        if prev > 0 and self.waited[q].get(id(sem), 0) < prev:
            waits.append((sem, prev))
            self.waited[q][id(sem)] = prev
        val = prev + 16
        self.dlast[q][i] = val
        tok = (sem, val, q, True)
        self.ops[q].append((lambda e: e.dma_start(out=out, in_=in_, **kw), waits, sem, 16))
        self._record(tok, r, w)
        return tok

    def emit(self):
        nc = self.nc
        fin = {}
        for q in self.dsem:
            for i, sem in enumerate(self.dsem[q]):
                if self.dlast[q][i] > 0:
                    fin[id(sem)] = (sem, self.dlast[q][i])
        with nc.allow_non_contiguous_dma(reason="small strided parameter loads"), nc.Block() as block:
            def run(engname):
                def body(e):
                    for fn, waits, sem, inc in self.ops[engname]:
                        for s, v in waits:
                            e.wait_ge(s, v)
                        fn(e).then_inc(sem, inc)
                    if engname == 'sp':
                        for s, v in fin.values():
                            e.wait_ge(s, v)
                return body
            block.tensor(run('pe'))
            block.vector(run('dve'))
            block.scalar(run('act'))
            block.gpsimd(run('pool'))
            block.sync(run('sp'))


D = 1024
NCH = 8
DFF = 2816
NFF = 22
INC = 3328
EPS = 1e-6
NS = 128
DILS = ((128, 1), (512, 4), (2048, 16))


class Ctx:
    pass


def build(SEQ=4096, LAYERS=4, stop_after=None):
    from concourse.bass_utils import run_bass_kernel_spmd
    nc = bass.Bass("TRN2", target_bir_lowering=False)
    NT = SEQ + NS
    NSUB = SEQ // 128
    NE = 2
    NO = 2
    keep = [min(w, SEQ) for w, _ in DILS]

    def din(name, shape, dt=F32):
        return nc.dram_tensor(name, list(shape), dt, kind="ExternalInput")

    def dout(name, shape):
        return nc.dram_tensor(name, list(shape), F32, kind="ExternalOutput")

    def dscr(name, shape, dt=F32):
        return nc.dram_tensor(name, list(shape), dt, kind="Internal")

    I = {}
    for name, shape in [
        ("xp", (SEQ, D)), ("xs", (NS, D)), ("call", (17, D)),
        ("cconv", (NE, 480, 512)), ("ckv0", (NE, 16, 128, 512)), ("ckv1", (NE, 16, 512, 512)),
        ("ckv2", (NE, 16, 2048, 512)), ("st5", (NO, 16, 8192)),
        ("norm_g", (4, 2, D)), ("final_g", (1, D)), ("w_ada", (4, D, 6 * D)), ("b_ada", (4, 6 * D)),
        ("w_in", (NE, D, INC)), ("conv_w", (NE, 31, 512)), ("conv_b", (NE, 512)),
        ("conv_ln_g", (NE, 512)), ("conv_ln_b", (NE, 512)), ("w_o", (NE, 768, D)),
        ("s5_lam_re", (NO, 64, 64)), ("s5_lam_im", (NO, 64, 64)), ("s5_log_dt", (NO, 64)),
        ("s5_b_re", (NO, 64, 64, 16)), ("s5_b_im", (NO, 64, 64, 16)),
        ("s5_c_re", (NO, 64, 16, 64)), ("s5_c_im", (NO, 64, 16, 64)),
        ("s5_d", (NO, D)), ("s5_w_glu", (NO, D, 2 * D)), ("s5_b_glu", (NO, 2 * D)),
        ("w_ff_gate", (4, D, DFF)), ("w_ff_up", (4, D, DFF)), ("w_ff_down", (4, DFF, D)),
        ("c_ident", (128, 128)), ("c_ropep", (128, NSUB, 16)), ("c_ropes", (128, 16)),
        ("c_amask", (128, 2, 128)), ("c_smask", (128, 128 + 128 + 24)), ("c_m01", (128, 128)),
    ]:
        I[name] = din(name, shape)
    O = {}
    for name, shape in [
        ("y_p", (SEQ, D)), ("y_s", (NS, D)), ("conv_p", (NE, 30, 512)),
        ("kv0_p", (NE, keep[0], 512)), ("kv1_p", (NE, keep[1], 512)), ("kv2_p", (NE, keep[2], 512)),
        ("s5_p", (NO, 32, 256)), ("conv_s", (NE, 16, 30, 512)),
        ("kv0_s", (NE, NS, 512)), ("kv1_s", (NE, NS, 512)), ("kv2_s", (NE, NS, 512)),
        ("s5_s", (NO, 16, 8192)),
    ]:
        O[name] = dout(name, shape)
    XF = dscr("XF", (D, NT))
    YC = dscr("YC", (512, NT), BF16)
    QKV = dscr("QKVs", (SEQ, 2304), BF16)
    ATT = dscr("ATTs", (3, NT, 260))

    with contextlib.ExitStack() as gst:
        P = Prog(nc, gst)
        cnt = [0]

        def sb(st, shape, dt=F32):
            cnt[0] += 1
            return st.enter_context(nc.sbuf_tensor("t%d" % cnt[0], list(shape), dt))

        PS = [gst.enter_context(nc.psum_tensor("ps%d" % i, [128, 512], F32)) for i in range(7)]
        PSB = gst.enter_context(nc.psum_tensor("psb", [128, 1024], BF16))
        psk = ["ps%d" % i for i in range(7)]

        def MM(out, lhsT, rhs, start, stop, r, w):
            return P.op('pe', lambda e: e.matmul(out, lhsT, rhs, start=start, stop=stop), r, w)

        def TR(out, in_, idt, r, w):
            return P.op('pe', lambda e: e.transpose(out, in_, idt), r, w)

        def TT(eng, out, a, b, op, r, w):
            return P.op(eng, lambda e: e.tensor_tensor(out, a, b, op), r, w)

        def TS(eng, out, a, s1, s2, op0, op1, r, w):
            if s2 is None:
                return P.op(eng, lambda e: e.tensor_scalar(out, a, s1, None, op0), r, w)
            return P.op(eng, lambda e: e.tensor_scalar(out, a, s1, s2, op0, op1), r, w)

        def STT(out, a, s, b, op0, op1, r, w):
            return P.op('dve', lambda e: e.scalar_tensor_tensor(out, a, s, b, op0, op1), r, w)

        def ACT(out, in_, func, r, w, scale=1.0, bias=None):
            if bias is None:
                return P.op('act', lambda e: e.activation(out, in_, func, scale=scale), r, w)
            return P.op('act', lambda e: e.activation(out, in_, func, scale=scale, bias=bias), r, w)

        def CP(eng, out, in_, r, w):
            if eng == 'act':
                return P.op('act', lambda e: e.activation(out, in_, AF.Copy), r, w)
            return P.op(eng, lambda e: e.tensor_copy(out, in_), r, w)

        def MS(eng, out, val, w):
            return P.op(eng, lambda e: e.memset(out, val), (), w)

        def RED(out, in_, op, r, w, eng='dve'):
            return P.op(eng, lambda e: e.tensor_reduce(out, in_, AX.X, op), r, w)

        def RCP(out, in_, r, w):
            return P.op('dve', lambda e: e.reciprocal(out, in_), r, w)

        def SCAN(out, d0, d1, init, r, w):
            return P.op('dve', lambda e: e.tensor_tensor_scan(out, d0, d1, init, ALU.mult, ALU.add), r, w)

        def DMA(q, out, in_, r, w):
            return P.dma(q, out, in_, r, w)

        ident = sb(gst, [128, 128]); identb = sb(gst, [128, 128], BF16)
        onesb = sb(gst, [128, 128], BF16); onesf = sb(gst, [128, 128])
        epsT = sb(gst, [128, 1])
        MOD = sb(gst, [128, 4, 48, 17])
        m01 = sb(gst, [128, 128])
        DMA('sp', ident[:], I["c_ident"].ap(), (), ['ident'])
        DMA('sp', m01[:], I["c_m01"].ap(), (), ['m01'])
        CP('dve', identb[:], ident[:], ['ident'], ['identb'])
        MS('pool', onesb[:], 1.0, ['onesb'])
        MS('pool', onesf[:], 1.0 / 512.0, ['onesf'])
        MS('pool', epsT[:], EPS, ['epsT'])

        XFv = XF.ap().rearrange("(c p) t -> p c t", p=128)

        def modbc(l, i, T, sample):
            a = MOD[:, l, i * 8:(i + 1) * 8, :]
            if not sample:
                return a[:, :, 0:1].to_broadcast([128, 8, T])
            return a[:, :, 1:17].unsqueeze(3).to_broadcast([128, 8, 16, 8])

        def modbc1(l, i, m, T, sample):
            a = MOD[:, l, i * 8 + m, :]
            if not sample:
                return a[:, 0:1].to_broadcast([128, T])
            return a[:, 1:17].unsqueeze(2).to_broadcast([128, 16, 8])

        def v3(ap, sample):
            return ap.rearrange("p c (s t) -> p c s t", t=8) if sample else ap

        def v2(ap, sample):
            return ap.rearrange("p (s t) -> p s t", t=8) if sample else ap

        with contextlib.ExitStack() as st:
            ct = sb(st, [17, D]); sct = sb(st, [17, D]); scT = sb(st, [128, 8, 17])
            DMA('sp', ct[:], I["call"].ap(), (), ['ct'])
            ACT(sct[:], ct[:], AF.Silu, ['ct'], ['sct'])
            for c in range(8):
                TR(PS[0][:, c * 17:(c + 1) * 17], sct[:, c * 128:(c + 1) * 128], ident[0:17, 0:17],
                   ['sct', 'ident'], [psk[0]])
            CP('dve', scT[:].rearrange("p c s -> p (c s)"), PS[0][:, 0:136], [psk[0]], ['scT'])
            slabs = [sb(st, [128, 8, 512]) for _ in range(2)]
            bts = [sb(st, [17, 512]) for _ in range(2)]
            mts = [sb(st, [17, 512]) for _ in range(2)]
            n = 0
            for l in range(4):
                for j in range(12):
                    b = n % 2
                    DMA('sp' if n % 2 == 0 else 'act', slabs[b][:],
                        I["w_ada"].ap()[l].rearrange("(k p) n -> p k n", p=128)[:, :, j * 512:(j + 1) * 512],
                        (), ['slab%d' % b])
                    DMA('pool', bts[b][:], I["b_ada"].ap()[l:l + 1, j * 512:(j + 1) * 512].to_broadcast([17, 512]),
                        (), ['bt%d' % b])
                    pm = PS[1 + b]
                    for k in range(8):
                        MM(pm[0:17, :], scT[:, k, :], slabs[b][:, k, :], k == 0, k == 7,
                           ['scT', 'slab%d' % b], [psk[1 + b]])
                    TT('dve', mts[b][:], pm[0:17, :], bts[b][:], ALU.add, [psk[1 + b], 'bt%d' % b], ['mt%d' % b])
                    pt = PS[3 + b]
                    for q in range(4):
                        TR(pt[:, q * 17:(q + 1) * 17], mts[b][:, q * 128:(q + 1) * 128], ident[0:17, 0:17],
                           ['mt%d' % b, 'ident'], [psk[3 + b]])
                    CP('act', MOD[:, l, 4 * j:4 * j + 4, :].rearrange("p c s -> p (c s)"), pt[:, 0:68],
                       [psk[3 + b]], ['MOD'])
                    n += 1
            ng = sb(st, [128, 4, 2, 8])
            for l in range(4):
                for i2 in range(2):
                    DMA('sp', ng[:, l, i2, :], I["norm_g"].ap()[l, i2].rearrange("(c p) -> p c", p=128),
                        (), ['ng'])
            for l in range(4):
                for i2, idx in ((0, 1), (1, 4)):
                    a = MOD[:, l, idx * 8:(idx + 1) * 8, :]
                    TS('dve', a, a, 1.0, None, ALU.add, None, ['MOD'], ['MOD'])
                    TT('dve', a, a, ng[:, l, i2, :].unsqueeze(2).to_broadcast([128, 8, 17]), ALU.mult,
                       ['MOD', 'ng'], ['MOD'])
        P.barrier()

        with contextlib.ExitStack() as st:
            xin = [sb(st, [128, D]) for _ in range(2)]
            xo = [sb(st, [128, 8, 128]) for _ in range(2)]
            for n in range(NSUB + 1):
                b = n % 2
                src = I["xp"].ap()[n * 128:(n + 1) * 128, :] if n < NSUB else I["xs"].ap()
                DMA('sp', xin[b][:], src, (), ['xin%d' % b])
                for hh in range(2):
                    pt = PS[2 * b + hh]
                    for q in range(4):
                        c = hh * 4 + q
                        TR(pt[:, q * 128:(q + 1) * 128], xin[b][:, c * 128:(c + 1) * 128], ident[:],
                           ['xin%d' % b, 'ident'], [psk[2 * b + hh]])
                    CP('dve' if hh == 0 else 'act', xo[b][:, hh * 4:hh * 4 + 4, :].rearrange("p c t -> p (c t)"),
                       pt[:], [psk[2 * b + hh]], ['xo%d' % b])
                DMA('pool', XFv[:, :, n * 128:(n + 1) * 128], xo[b][:], ['xo%d' % b], ['XF'])
        P.barrier()

        wl = [0]

        def load_w(dst, w_ap, K, N, key, stg):
            for k in range(K):
                b = wl[0] % 2
                eng = ('dve', 'act', 'pool')[wl[0] % 3]
                wl[0] += 1
                DMA('sp' if b == 0 else 'act', stg[b][:, 0:N], w_ap[k * 128:(k + 1) * 128, :], (), ['stg%d' % b])
                CP(eng, dst[:, k, :], stg[b][:, 0:N], ['stg%d' % b], [key])

        def norm_mod(xt, sqb, hb, rstd, T, l, i_sh, sample, hf=None, kx='xt', kh='hb'):
            ACT(sqb[:, :, 0:T], xt[:, :, 0:T], AF.Square, [kx], ['sqb'])
            for c in range(8):
                MM(PS[0][:, 0:T], onesb[:], sqb[:, c, 0:T], c == 0, c == 7, ['onesb', 'sqb'], [psk[0]])
            ACT(rstd[:, 0:T], PS[0][:, 0:T], AF.Sqrt, [psk[0], 'epsT'], ['rstd'], scale=1.0 / D, bias=epsT[:])
            RCP(rstd[:, 0:T], rstd[:, 0:T], ['rstd'], ['rstd'])
            x3 = xt[:, :, 0:T]
            TT('dve', x3, x3, rstd[:, 0:T].unsqueeze(1).to_broadcast([128, 8, T]), ALU.mult, [kx, 'rstd'], [kx])
            TT('pool', v3(x3, sample), v3(x3, sample), modbc(l, i_sh + 1, T, sample), ALU.mult, [kx, 'MOD'], [kx])
            if hf is not None:
                TT('dve', v3(hf[:, :, 0:T], sample), v3(x3, sample), modbc(l, i_sh, T, sample), ALU.add,
                   [kx, 'MOD'], ['hf'])
                CP('act', hb[:, :, 0:T], hf[:, :, 0:T], ['hf'], [kh])
            else:
                TT('dve', v3(hb[:, :, 0:T], sample), v3(x3, sample), modbc(l, i_sh, T, sample), ALU.add,
                   [kx, 'MOD'], [kh])

        def tiles(T):
            out = [(c0, T, False) for c0 in range(0, SEQ, T)]
            out.append((SEQ, NS, True))
            return out

        def even_E1(l, ei, st):
            T0 = 512
            win = sb(st, [128, 8, INC], BF16)
            with contextlib.ExitStack() as st2:
                stg = [sb(st2, [128, INC]) for _ in range(2)]
                load_w(win, I["w_in"].ap()[ei], 8, INC, 'win', stg)
                P.barrier()
            cw = sb(st, [128, 4, 31]); cb = sb(st, [128, 4]); lg = sb(st, [128, 4]); lb = sb(st, [128, 4])
            for ch in range(4):
                DMA('pool', cw[:, ch, :], I["conv_w"].ap()[ei].rearrange("j c -> c j")[ch * 128:(ch + 1) * 128, :],
                    (), ['cw'])
            for t_, nm in ((cb, "conv_b"), (lg, "conv_ln_g"), (lb, "conv_ln_b")):
                DMA('pool', t_[:], I[nm].ap()[ei].rearrange("(c p) -> p c", p=128), (), ['cw'])
            ropep = sb(st, [128, NSUB, 16]); ropes = sb(st, [128, 16])
            DMA('pool', ropep[:], I["c_ropep"].ap(), (), ['rope'])
            DMA('pool', ropes[:], I["c_ropes"].ap(), (), ['rope'])
            xt = sb(st, [128, 8, T0]); sqb = sb(st, [128, 8, T0], BF16); hb = sb(st, [128, 8, T0], BF16)
            rstd = sb(st, [128, T0])
            uext = sb(st, [128, 4, 30 + T0]); uxs = sb(st, [128, 4, 16, 38])
            sg = sb(st, [128, T0]); yc = sb(st, [128, 4, T0]); ycb = sb(st, [128, 4, T0], BF16)
            lnr = sb(st, [128, T0]); sq4 = sb(st, [128, 4, T0])
            qkv = sb(st, [128, 2304]); qkvb = sb(st, [128, 2304], BF16); rt = sb(st, [128, 4, 24, 8])
            cvt = sb(st, [128, 512]); cin = sb(st, [120, 512]); cmp_ = sb(st, [128, 4, 16, 30])
            MS('pool', uext[:, :, 0:30], 0.0, ['uext'])
            for q4 in range(4):
                DMA('sp', cin[:], I["cconv"].ap()[ei, q4 * 120:(q4 + 1) * 120, :], (), ['cin'])
                for ch in range(4):
                    TR(PS[1][:, ch * 120:(ch + 1) * 120], cin[:, ch * 128:(ch + 1) * 128], ident[0:120, 0:120],
                       ['cin', 'ident'], [psk[1]])
                CP('dve', uxs[:, :, q4 * 4:(q4 + 1) * 4, 0:30],
                   PS[1][:, 0:480].rearrange("p (c s j) -> p c s j", c=4, s=4), [psk[1]], ['uxs'])
            for (c0, T, sample) in tiles(T0):
                DMA('sp', xt[:, :, 0:T], XFv[:, :, c0:c0 + T], ['XF'], ['xt'])
                norm_mod(xt, sqb, hb, rstd, T, l, 0, sample)
                for ch in range(4):
                    pv, pg = PS[1 + (ch % 2) * 2], PS[2 + (ch % 2) * 2]
                    kv_, kg_ = psk[1 + (ch % 2) * 2], psk[2 + (ch % 2) * 2]
                    for k in range(8):
                        MM(pv[:, 0:T], win[:, k, ch * 128:(ch + 1) * 128], hb[:, k, 0:T], k == 0, k == 7,
                           ['win', 'hb'], [kv_])
                    for k in range(8):
                        MM(pg[:, 0:T], win[:, k, 512 + ch * 128:512 + (ch + 1) * 128], hb[:, k, 0:T], k == 0, k == 7,
                           ['win', 'hb'], [kg_])
                    ACT(sg[:, 0:T], pg[:, 0:T], AF.Sigmoid, [kg_], ['sg'])
                    if not sample:
                        TT('dve', uext[:, ch, 30:30 + T], pv[:, 0:T], sg[:, 0:T], ALU.mult, [kv_, 'sg'], ['uext'])
                    else:
                        TT('dve', uxs[:, ch, :, 30:38], v2(pv[:, 0:T], True), v2(sg[:, 0:T], True), ALU.mult,
                           [kv_, 'sg'], ['uxs'])
                for s4 in range(T // 128):
                    n = (c0 // 128) + s4
                    for nb in range(5):
                        wd = 512 if nb < 4 else 256
                        pq = PS[1 + (nb % 4)]
                        for k in range(8):
                            MM(pq[:, 0:wd], hb[:, k, s4 * 128:(s4 + 1) * 128],
                               win[:, k, 1024 + nb * 512:1024 + nb * 512 + wd], k == 0, k == 7,
                               ['hb', 'win'], [psk[1 + (nb % 4)]])
                        CP('act', qkv[:, nb * 512:nb * 512 + wd], pq[:, 0:wd],
                           [psk[1 + (nb % 4)]], ['qkv'])
                    qk = qkv[:, 0:1536].rearrange("p (h d) -> p h d", d=64)
                    x1, x2 = qk[:, :, 0:8], qk[:, :, 8:16]
                    rp = ropes[:] if sample else ropep[:, n, :]
                    cs = rp[:, 0:8].unsqueeze(1).to_broadcast([128, 24, 8])
                    sn = rp[:, 8:16].unsqueeze(1).to_broadcast([128, 24, 8])
                    TT('pool', rt[:, 0], x1, cs, ALU.mult, ['qkv', 'rope'], ['rt'])
                    TT('pool', rt[:, 1], x2, sn, ALU.mult, ['qkv', 'rope'], ['rt'])
                    TT('pool', rt[:, 2], x2, cs, ALU.mult, ['qkv', 'rope'], ['rt'])
                    TT('pool', rt[:, 3], x1, sn, ALU.mult, ['qkv', 'rope'], ['rt'])
                    TT('pool', x1, rt[:, 0], rt[:, 1], ALU.subtract, ['rt'], ['qkv'])
                    TT('pool', x2, rt[:, 2], rt[:, 3], ALU.add, ['rt'], ['qkv'])
                    for g in range(3):
                        kcols = qkv[:, 768 + 256 * g:768 + 256 * (g + 1)]
                        vcols = qkv[:, 1536 + 256 * g:1536 + 256 * (g + 1)]
                        if sample:
                            dsto = O["kv%d_s" % g].ap()[ei]
                        else:
                            r0 = n * 128 - (SEQ - keep[g])
                            if r0 < 0:
                                continue
                            dsto = O["kv%d_p" % g].ap()[ei, r0:r0 + 128, :]
                        DMA('pool', dsto[:, 0:256], kcols, ['qkv'], ['o_kv'])
                        DMA('pool', dsto[:, 256:512], vcols, ['qkv'], ['o_kv'])
                    if not sample:
                        CP('act', qkvb[:], qkv[:], ['qkv'], ['qkvb'])
                        DMA('sp', QKV.ap()[n * 128:(n + 1) * 128, :], qkvb[:], ['qkvb'], ['QKV'])
                    else:
                        CP('act', C.qs[:], qkv[:], ['qkv'], ['qs'])
                for ch in range(4):
                    if not sample:
                        src = lambda j: uext[:, ch, j:j + T]
                        dst = yc[:, ch, 0:T]
                        uk = 'uext'
                    else:
                        src = lambda j: uxs[:, ch, :, j:j + 8]
                        dst = v2(yc[:, ch, 0:T], True)
                        uk = 'uxs'
                    TS('dve', dst, src(0), cw[:, ch, 0:1], cb[:, ch:ch + 1], ALU.mult, ALU.add, [uk, 'cw'], ['yc'])
                    for j in range(1, 31):
                        STT(dst, src(j), cw[:, ch, j:j + 1], dst, ALU.mult, ALU.add, [uk, 'cw', 'yc'], ['yc'])
                for ch in range(4):
                    MM(PS[5][:, 0:T], onesf[:], yc[:, ch, 0:T], ch == 0, ch == 3, ['onesf', 'yc'], [psk[5]])
                TT('dve', yc[:, :, 0:T], yc[:, :, 0:T], PS[5][:, 0:T].unsqueeze(1).to_broadcast([128, 4, T]),
                   ALU.subtract, ['yc', psk[5]], ['yc'])
                ACT(sq4[:, :, 0:T], yc[:, :, 0:T], AF.Square, ['yc'], ['sq4'])
                for ch in range(4):
                    MM(PS[5][:, 0:T], onesf[:], sq4[:, ch, 0:T], ch == 0, ch == 3, ['onesf', 'sq4'], [psk[5]])
                ACT(lnr[:, 0:T], PS[5][:, 0:T], AF.Sqrt, [psk[5], 'epsT'], ['lnr'], bias=epsT[:])
                RCP(lnr[:, 0:T], lnr[:, 0:T], ['lnr'], ['lnr'])
                TT('dve', yc[:, :, 0:T], yc[:, :, 0:T], lnr[:, 0:T].unsqueeze(1).to_broadcast([128, 4, T]), ALU.mult,
                   ['yc', 'lnr'], ['yc'])
                TT('pool', yc[:, :, 0:T], yc[:, :, 0:T], lg[:].unsqueeze(2).to_broadcast([128, 4, T]), ALU.mult,
                   ['yc', 'cw'], ['yc'])
                TT('pool', yc[:, :, 0:T], yc[:, :, 0:T], lb[:].unsqueeze(2).to_broadcast([128, 4, T]), ALU.add,
                   ['yc', 'cw'], ['yc'])
                ACT(ycb[:, :, 0:T], yc[:, :, 0:T], AF.Silu, ['yc'], ['ycb'])
                DMA('pool', YC.ap().rearrange("(c p) t -> p c t", p=128)[:, :, c0:c0 + T], ycb[:, :, 0:T],
                    ['ycb'], ['YC'])
                if not sample:
                    if c0 + T == SEQ:
                        for ch in range(4):
                            TR(PS[6][0:30, ch * 128:(ch + 1) * 128], uext[:, ch, T:T + 30], ident[:],
                               ['uext', 'ident'], [psk[6]])
                        CP('dve', cvt[0:30, :], PS[6][0:30, :], [psk[6]], ['cvt'])
                        DMA('pool', O["conv_p"].ap()[ei], cvt[0:30, :], ['cvt'], ['o_convp'])
                    CP('pool', uext[:, :, 0:30], uext[:, :, T:T + 30], ['uext'], ['uext'])
                else:
                    CP('dve', cmp_[:], uxs[:, :, :, 8:38], ['uxs'], ['cmp'])
                    for q4 in range(4):
                        for ch in range(4):
                            TR(PS[6][0:120, ch * 128:(ch + 1) * 128],
                               cmp_[:, ch, q4 * 4:(q4 + 1) * 4, :].rearrange("p s j -> p (s j)"), ident[:],
                               ['cmp', 'ident'], [psk[6]])
                        CP('dve', cvt[0:120, :], PS[6][0:120, :], [psk[6]], ['cvt'])
                        DMA('pool', O["conv_s"].ap()[ei, q4 * 4:(q4 + 1) * 4].rearrange("s j c -> (s j) c"),
                            cvt[0:120, :], ['cvt'], ['o_convs'])

        C = Ctx()

        def even_E2(st):
            am = sb(st, [128, 2, 128]); amb = sb(st, [128, 2, 128], BF16)
            DMA('sp', am[:], I["c_amask"].ap(), (), ['am'])
            CP('dve', amb[:], am[:], ['am'], ['amb'])
            qk_in = [sb(st, [128, 2, 256], BF16) for _ in range(2)]
            vx = [sb(st, [128, 4, 65], BF16) for _ in range(3)]
            qkT = [sb(st, [128, 4, 128], BF16) for _ in range(3)]
            pT = [sb(st, [128, 2, 128], BF16) for _ in range(4)]
            ob = [sb(st, [128, 260]) for _ in range(2)]
            for i in range(3):
                MS('pool', vx[i][:, :, 64:65], 1.0, ['vx%d' % i])
            blk = 0
            for g, (win_, dil) in enumerate(DILS):
                nb = SEQ // dil // 128
                qv = QKV.ap().rearrange("(b i d) c -> d b i c", i=128, d=dil)
                av = ATT.ap()[g, 0:SEQ, :].rearrange("(b i d) c -> d b i c", i=128, d=dil)
                for r in range(dil):
                    for b in range(nb):
                        ib = blk % 2
                        cur = blk % 3
                        prv = (blk - 1) % 3
                        rows = qv[r, b]
                        DMA('sp', qk_in[ib][:, 0, :], rows[:, 256 * g:256 * (g + 1)], ['QKV'], ['qkin%d' % ib])
                        DMA('sp', qk_in[ib][:, 1, :], rows[:, 768 + 256 * g:768 + 256 * (g + 1)], ['QKV'],
                            ['qkin%d' % ib])
                        DMA('sp', vx[cur][:, :, 0:64],
                            rows[:, 1536 + 256 * g:1536 + 256 * (g + 1)].rearrange("p (h d) -> p h d", d=64),
                            ['QKV'], ['vx%d' % cur])
                        for c in range(4):
                            TR(PSB[:, c * 128:(c + 1) * 128], qk_in[ib][:, c // 2, (c % 2) * 128:(c % 2 + 1) * 128],
                               identb[:], ['qkin%d' % ib, 'identb'], ['psb'])
                        CP('dve', qkT[cur][:].rearrange("p c t -> p (c t)"), PSB[:, 0:512], ['psb'], ['qkT%d' % cur])
                        po = PS[5 + ib]
                        lo = 0 if b > 0 else 1
                        for h in range(4):
                            c, hp = h // 2, (h % 2) * 64
                            pss = PS[1 + h]
                            ksk = psk[1 + h]
                            MM(pss[:, 128:256], qkT[cur][hp:hp + 64, 2 + c, :], qkT[cur][hp:hp + 64, c, :], True, False,
                               ['qkT%d' % cur], [ksk])
                            MM(pss[:, 128:256], identb[:], amb[:, 1, :], False, True, ['identb', 'amb'], [ksk])
                            if b > 0:
                                MM(pss[:, 0:128], qkT[prv][hp:hp + 64, 2 + c, :], qkT[cur][hp:hp + 64, c, :], True, False,
                                   ['qkT%d' % cur, 'qkT%d' % prv], [ksk])
                                MM(pss[:, 0:128], identb[:], amb[:, 0, :], False, True, ['identb', 'amb'], [ksk])
                            ACT(pT[h][:, lo:2, :].rearrange("p a t -> p (a t)"), pss[:, lo * 128:256], AF.Exp,
                                [ksk], ['pT%d' % h], scale=0.125)
                        for h in range(4):
                            pt_ = pT[h]
                            ptk = 'pT%d' % h
                            if b > 0:
                                MM(po[:, h * 65:(h + 1) * 65], pt_[:, 0, :], vx[prv][:, h, :], True, False,
                                   [ptk, 'vx%d' % prv], [psk[5 + ib]])
                            MM(po[:, h * 65:(h + 1) * 65], pt_[:, 1, :], vx[cur][:, h, :], b == 0, True,
                               [ptk, 'vx%d' % cur], [psk[5 + ib]])
                        CP('dve', ob[ib][:], po[:, 0:260], [psk[5 + ib]], ['ob%d' % ib])
                        DMA('pool', av[r, b], ob[ib][:], ['ob%d' % ib], ['ATT'])
                        blk += 1
                    blk += 1

        def even_SA(ei, st):
            KC = 16
            sm = sb(st, [128, 280]); DMA('sp', sm[:], I["c_smask"].ap(), (), ['sm'])
            kt = [sb(st, [128, KC, 256]) for _ in range(4)]
            prods_ = [sb(st, [128, KC, 256]) for _ in range(2)]
            npd = [0]

            def nprod():
                npd[0] += 1
                return prods_[npd[0] % 2], 'prod%d' % (npd[0] % 2)

            S = sb(st, [128, 136, 4]); Pm = sb(st, [128, 136, 4])
            kn = sb(st, [128, 8, 512])
            mx = sb(st, [128, 4]); ls = sb(st, [128, 4]); lse = sb(st, [128, 3, 4]); og = sb(st, [128, 3, 256])
            part = sb(st, [128, 256]); wts = sb(st, [128, 3, 4]); mm_ = sb(st, [128, 4]); den = sb(st, [128, 4])
            att = sb(st, [128, 260]); zer = sb(st, [128, 260])
            caches = [I["ckv0"], I["ckv1"], I["ckv2"]]
            wbs = [128, 512, 2048]
            n = 0
            for g in range(3):
                wb, dil = wbs[g], DILS[g][1]
                base = ei * 16 * wb * 512

                def cache_load(q, dst, kc, off, key):
                    m0 = kc * KC
                    for t in range(8):
                        if g == 0:
                            row0, rs = m0, 1
                        elif g == 1:
                            row0, rs = (t % 4) + 4 * m0, 4
                        else:
                            row0, rs = t + 16 * m0, 16
                        src = bass.AP(caches[g], base + row0 * 512 + off, [[wb * 512, 16], [rs * 512, KC], [1, 256]])
                        DMA(q, dst[t::8], src, (), [key])

                for t in range(8):
                    DMA('sp', kn[t::8], bass.AP(O["kv%d_s" % g], ei * NS * 512, [[8 * 512, 16], [512, 8], [1, 512]]),
                        ['o_kv'], ['kn'])
                qg = C.qs[:, 256 * g:256 * (g + 1)]
                for kc in range(8):
                    b = n % 4; n += 1
                    cache_load('sp' if b % 2 == 0 else 'act', kt[b], kc, 0, 'kt%d' % b)
                    prod, kpr = nprod()
                    TT('pool', prod[:], kt[b][:], qg.unsqueeze(1).to_broadcast([128, KC, 256]), ALU.mult,
                       ['kt%d' % b, 'qs'], [kpr])
                    RED(S[:, kc * KC:(kc + 1) * KC, :].rearrange("p k h -> p (k h)"),
                        prod[:].rearrange("p k (h d) -> p (k h) d", d=64), ALU.add, [kpr], ['S'])
                prod, kpr = nprod()
                TT('pool', prod[:, 0:8, :], kn[:, :, 0:256], qg.unsqueeze(1).to_broadcast([128, 8, 256]), ALU.mult,
                   ['kn', 'qs'], [kpr])
                RED(S[:, 128:136, :].rearrange("p k h -> p (k h)"),
                    prod[:, 0:8, :].rearrange("p k (h d) -> p (k h) d", d=64), ALU.add, [kpr], ['S'])
                if g < 2:
                    TT('dve', S[:, 0:128, :], S[:, 0:128, :],
                       sm[:, 128 * g:128 * (g + 1)].unsqueeze(2).to_broadcast([128, 128, 4]), ALU.add, ['S', 'sm'], ['S'])
                TT('dve', S[:, 128:136, :], S[:, 128:136, :],
                   sm[:, 256 + 8 * g:256 + 8 * (g + 1)].unsqueeze(2).to_broadcast([128, 8, 4]), ALU.add, ['S', 'sm'], ['S'])
                Sv = S[:].rearrange("p k h -> p h k")
                RED(mx[:], Sv, ALU.max, ['S'], ['mx'])
                TT('dve', S[:], S[:], mx[:].unsqueeze(1).to_broadcast([128, 136, 4]), ALU.subtract, ['S', 'mx'], ['S'])
                ACT(Pm[:], S[:], AF.Exp, ['S'], ['Pm'], scale=0.125)
                RED(ls[:], Pm[:].rearrange("p k h -> p h k"), ALU.add, ['Pm'], ['ls'])
                ACT(lse[:, g, :], ls[:], AF.Ln, ['ls'], ['lse'])
                STT(lse[:, g, :], mx[:], 0.125, lse[:, g, :], ALU.mult, ALU.add, ['mx', 'lse'], ['lse'])
                RCP(ls[:], ls[:], ['ls'], ['ls'])
                TT('dve', Pm[:], Pm[:], ls[:].unsqueeze(1).to_broadcast([128, 136, 4]), ALU.mult, ['Pm', 'ls'], ['Pm'])
                for kc in range(9):
                    if kc < 8:
                        b = n % 4; n += 1
                        cache_load('sp' if b % 2 == 0 else 'act', kt[b], kc, 256, 'kt%d' % b)
                        vsrc, nk, vk = kt[b][:], KC, 'kt%d' % b
                        pp = Pm[:, kc * KC:(kc + 1) * KC, :]
                    else:
                        vsrc, nk, vk = kn[:, :, 256:512], 8, 'kn'
                        pp = Pm[:, 128:136, :]
                    prod, kpr = nprod()
                    TT('pool', prod[:, 0:nk, :].rearrange("p k (h d) -> p k h d", d=64),
                       vsrc.rearrange("p k (h d) -> p k h d", d=64),
                       pp.unsqueeze(3).to_broadcast([128, nk, 4, 64]), ALU.mult, [vk, 'Pm'], [kpr])
                    dst = og[:, g, :] if kc == 0 else part[:]
                    RED(dst, prod[:, 0:nk, :].rearrange("p k c -> p c k"), ALU.add, [kpr], ['og' if kc == 0 else 'part'])
                    if kc > 0:
                        TT('dve', og[:, g, :], og[:, g, :], part[:], ALU.add, ['og', 'part'], ['og'])
            RED(mm_[:], lse[:].rearrange("p g h -> p h g"), ALU.max, ['lse'], ['mm'])
            TT('dve', wts[:], lse[:], mm_[:].unsqueeze(1).to_broadcast([128, 3, 4]), ALU.subtract, ['lse', 'mm'], ['wts'])
            ACT(wts[:], wts[:], AF.Exp, ['wts'], ['wts'])
            RED(den[:], wts[:].rearrange("p g h -> p h g"), ALU.add, ['wts'], ['den'])
            RCP(den[:], den[:], ['den'], ['den'])
            TT('dve', wts[:], wts[:], den[:].unsqueeze(1).to_broadcast([128, 3, 4]), ALU.mult, ['wts', 'den'], ['wts'])
            TT('dve', og[:].rearrange("p g (h d) -> p g h d", d=64), og[:].rearrange("p g (h d) -> p g h d", d=64),
               wts[:].unsqueeze(3).to_broadcast([128, 3, 4, 64]), ALU.mult, ['og', 'wts'], ['og'])
            RED(att[:, 0:256], og[:].rearrange("p g c -> p c g"), ALU.add, ['og'], ['att'])
            MS('pool', att[:, 256:260], 1.0, ['att'])
            MS('pool', zer[:], 0.0, ['zer'])
            a65 = sb(st, [128, 260])
            CP('dve', a65[:].rearrange("p (h e) -> p h e", e=65)[:, :, 0:64],
               att[:, 0:256].rearrange("p (h d) -> p h d", d=64), ['att'], ['a65'])
            MS('pool', a65[:].rearrange("p (h e) -> p h e", e=65)[:, :, 64:65], 1.0, ['a65'])
            DMA('pool', ATT.ap()[0, SEQ:NT, :], a65[:], ['a65'], ['ATT'])
            DMA('pool', ATT.ap()[1, SEQ:NT, :], zer[:], ['zer'], ['ATT'])
            DMA('pool', ATT.ap()[2, SEQ:NT, :], zer[:], ['zer'], ['ATT'])

        def even_E3(l, ei, st):
            T0 = 512
            wo = sb(st, [128, 6, D], BF16)
            stg = [sb(st, [128, D]) for _ in range(2)]
            load_w(wo, I["w_o"].ap()[ei], 6, D, 'wo', stg)
            xts = [sb(st, [128, 8, T0]) for _ in range(2)]; ycbs = [sb(st, [128, 4, T0], BF16) for _ in range(2)]
            afms = [sb(st, [128, 2, T0], BF16) for _ in range(2)]
            a3 = [sb(st, [128, 3, 260]) for _ in range(2)]; atm = sb(st, [128, 256]); rc = sb(st, [128, 4])
            tmp = sb(st, [128, T0])
            YCv = YC.ap().rearrange("(c p) t -> p c t", p=128)
            n = 0
            for it, (c0, T, sample) in enumerate(tiles(T0)):
                pb_ = it % 2
                xt, ycb, afm = xts[pb_], ycbs[pb_], afms[pb_]
                kx, kyc, kaf = 'xt%d' % pb_, 'ycb%d' % pb_, 'afm%d' % pb_
                DMA('sp', xt[:, :, 0:T], XFv[:, :, c0:c0 + T], ['XF'], [kx])
                DMA('act', ycb[:, :, 0:T], YCv[:, :, c0:c0 + T], ['YC'], [kyc])
                for s4 in range(T // 128):
                    b = n % 2; n += 1
                    t0 = c0 + s4 * 128
                    DMA('sp', a3[b][:], ATT.ap()[:, t0:t0 + 128, :].rearrange("g t c -> t g c"), ['ATT'], ['a3%d' % b])
                    TT('pool', a3[b][:, 0, :], a3[b][:, 0, :], a3[b][:, 1, :], ALU.add, ['a3%d' % b], ['a3%d' % b])
                    TT('pool', a3[b][:, 0, :], a3[b][:, 0, :], a3[b][:, 2, :], ALU.add, ['a3%d' % b], ['a3%d' % b])
                    a65 = a3[b][:, 0, :].rearrange("p (h e) -> p h e", e=65)
                    RCP(rc[:], a65[:, :, 64], ['a3%d' % b], ['rc'])
                    TT('dve', atm[:].rearrange("p (h d) -> p h d", d=64), a65[:, :, 0:64],
                       rc[:].unsqueeze(2).to_broadcast([128, 4, 64]), ALU.mult, ['a3%d' % b, 'rc'], ['atm'])
                    for c in range(2):
                        TR(PS[6][:, c * 128:(c + 1) * 128], atm[:, c * 128:(c + 1) * 128], ident[:], ['atm', 'ident'],
                           [psk[6]])
                    CP('act', afm[:, :, s4 * 128:(s4 + 1) * 128], PS[6][:, 0:256].rearrange("p (c t) -> p c t", c=2),
                       [psk[6]], [kaf])
                for m in range(8):
                    pm = PS[1 + (m % 4)]; pk = psk[1 + (m % 4)]
                    for kk in range(6):
                        rhs = ycb[:, kk, 0:T] if kk < 4 else afm[:, kk - 4, 0:T]
                        MM(pm[:, 0:T], wo[:, kk, m * 128:(m + 1) * 128], rhs, kk == 0, kk == 5,
                           ['wo', kyc, kaf], [pk])
                    TT('dve', v2(tmp[:, 0:T], sample), v2(pm[:, 0:T], sample), modbc1(l, 2, m, T, sample), ALU.mult,
                       [pk, 'MOD'], ['tmp'])
                    TT('pool', xt[:, m, 0:T], xt[:, m, 0:T], tmp[:, 0:T], ALU.add, [kx, 'tmp'], [kx])
                DMA('pool', XFv[:, :, c0:c0 + T], xt[:, :, 0:T], [kx], ['XF'])

        def ffn(l, st):
            T0 = 256
            wg = sb(st, [128, 8, DFF], BF16); wu = sb(st, [128, 8, DFF], BF16); wd = sb(st, [128, NFF, D], BF16)
            with contextlib.ExitStack() as st2:
                stg = [sb(st2, [128, DFF]) for _ in range(2)]
                load_w(wg, I["w_ff_gate"].ap()[l], 8, DFF, 'wg', stg)
                load_w(wu, I["w_ff_up"].ap()[l], 8, DFF, 'wu', stg)
                load_w(wd, I["w_ff_down"].ap()[l], NFF, D, 'wd', stg)
                P.barrier()
            xts = [sb(st, [128, 8, T0]) for _ in range(2)]; sqb = sb(st, [128, 8, T0], BF16)
            hbs = [sb(st, [128, 8, T0], BF16) for _ in range(2)]; rstd = sb(st, [128, T0]); act = sb(st, [128, NFF, T0], BF16)
            sg = [sb(st, [128, T0]) for _ in range(2)]; tmp = sb(st, [128, T0])
            tl = tiles(T0)

            def prep(it):
                c0, T, sample = tl[it]
                bb = it % 2
                kx, kh = 'xt%d' % bb, 'hb%d' % bb
                DMA('sp', xts[bb][:, :, 0:T], XFv[:, :, c0:c0 + T], ['XF'], [kx])
                norm_mod(xts[bb], sqb, hbs[bb], rstd, T, l, 3, sample, kx=kx, kh=kh)
                DMA('sp', xts[bb][:, :, 0:T], XFv[:, :, c0:c0 + T], ['XF'], [kx])

            prep(0)
            for it, (c0, T, sample) in enumerate(tl):
                bb = it % 2
                xt, hb = xts[bb], hbs[bb]
                kx, kh = 'xt%d' % bb, 'hb%d' % bb
                for j in range(NFF):
                    if j == 12 and it + 1 < len(tl):
                        prep(it + 1)
                    b = j % 2
                    pg, pu = PS[1 + 2 * b], PS[2 + 2 * b]
                    kg, ku = psk[1 + 2 * b], psk[2 + 2 * b]
                    for k in range(8):
                        MM(pg[:, 0:T], wg[:, k, j * 128:(j + 1) * 128], hb[:, k, 0:T], k == 0, k == 7, ['wg', kh], [kg])
                    for k in range(8):
                        MM(pu[:, 0:T], wu[:, k, j * 128:(j + 1) * 128], hb[:, k, 0:T], k == 0, k == 7, ['wu', kh], [ku])
                    ACT(sg[b][:, 0:T], pg[:, 0:T], AF.Silu, [kg], ['sg%d' % b])
                    TT('dve', act[:, j, 0:T], pu[:, 0:T], sg[b][:, 0:T], ALU.mult, [ku, 'sg%d' % b], ['act'])
                for m in range(8):
                    pm = PS[5 + (m % 2)]; pk = psk[5 + (m % 2)]
                    for j in range(NFF):
                        MM(pm[:, 0:T], wd[:, j, m * 128:(m + 1) * 128], act[:, j, 0:T], j == 0, j == NFF - 1,
                           ['wd', 'act'], [pk])
                    TT('dve', v2(tmp[:, 0:T], sample), v2(pm[:, 0:T], sample), modbc1(l, 5, m, T, sample), ALU.mult,
                       [pk, 'MOD'], ['tmp'])
                    TT('pool', xt[:, m, 0:T], xt[:, m, 0:T], tmp[:, 0:T], ALU.add, [kx, 'tmp'], [kx])
                DMA('pool', XFv[:, :, c0:c0 + T], xt[:, :, 0:T], [kx], ['XF'])

        def odd_O1(l, oi, st):
            T0 = 256
            PI = float(np.pi)
            bbr = sb(st, [128, 32, 128], BF16); bbi = sb(st, [128, 32, 128], BF16)
            ccr = sb(st, [128, 32, 128], BF16); cci = sb(st, [128, 32, 128], BF16); ccn = sb(st, [128, 32, 128], BF16)
            cosT = sb(st, [128, 32, 128]); sinT = sb(st, [128, 32, 128])
            rho = sb(st, [128, 32]); dsk = sb(st, [128, 8]); bgl = sb(st, [128, 16])
            wgl = sb(st, [128, 8, 2 * D], BF16)
            Sre = sb(st, [128, 32]); Sim = sb(st, [128, 32])
            S0r = sb(st, [128, 32, 16]); S0i = sb(st, [128, 32, 16]); Sor = sb(st, [128, 32, 16]); Soi = sb(st, [128, 32, 16])
            with contextlib.ExitStack() as st2:
                stg = [sb(st2, [128, 2 * D]) for _ in range(2)]
                load_w(wgl, I["s5_w_glu"].ap()[oi], 8, 2 * D, 'wgl', stg)
                DMA('pool', dsk[:], I["s5_d"].ap()[oi].rearrange("(c p) -> p c", p=128), (), ['dsk'])
                DMA('pool', bgl[:], I["s5_b_glu"].ap()[oi].rearrange("(c p) -> p c", p=128), (), ['bgl'])
                lr = sb(st2, [128, 32]); li = sb(st2, [128, 32]); dt = sb(st2, [128, 32])
                DMA('sp', lr[:], I["s5_lam_re"].ap()[oi].rearrange("(sc gl) p -> (gl p) sc", gl=2), (), ['lr'])
                DMA('sp', li[:], I["s5_lam_im"].ap()[oi].rearrange("(sc gl) p -> (gl p) sc", gl=2), (), ['li'])
                for gl in range(2):
                    DMA('sp', dt[gl * 64:(gl + 1) * 64, :],
                        bass.AP(I["s5_log_dt"], oi * 64 + gl, [[0, 64], [2, 32]]), (), ['dt'])
                br = sb(st2, [128, 32, 16]); bi = sb(st2, [128, 32, 16])
                DMA('sp', br[:], I["s5_b_re"].ap()[oi].rearrange("(sc gl) p h -> (gl p) sc h", gl=2), (), ['br'])
                DMA('sp', bi[:], I["s5_b_im"].ap()[oi].rearrange("(sc gl) p h -> (gl p) sc h", gl=2), (), ['bi'])
                ACT(dt[:], dt[:], AF.Exp, ['dt'], ['dt'])
                mag = sb(st2, [128, 32]); th = sb(st2, [128, 32]); ta = sb(st2, [128, 32]); tb = sb(st2, [128, 32])
                TT('dve', mag[:], lr[:], dt[:], ALU.mult, ['lr', 'dt'], ['mag'])
                ACT(mag[:], mag[:], AF.Exp, ['mag'], ['mag'])
                TT('dve', th[:], li[:], dt[:], ALU.mult, ['li', 'dt'], ['th'])
                cs0 = sb(st2, [128, 32]); sn0 = sb(st2, [128, 32])

                def sin_of(dst, shift, key):
                    TS('dve', ta[:], th[:], shift + 2 * PI, None, ALU.add, None, ['th'], ['ta'])
                    CP('dve', tb[:], ta[:], ['ta'], ['tb'])
                    for j in range(7):
                        thr = (2 * j + 1) * PI
                        TS('dve', dst[:], ta[:], thr, -2 * PI, ALU.is_ge, ALU.mult, ['ta'], [key])
                        TT('dve', tb[:], tb[:], dst[:], ALU.add, ['tb', key], ['tb'])
                    ACT(dst[:], tb[:], AF.Sin, ['tb'], [key])

                sin_of(sn0, 0.0, 'sn0')
                sin_of(cs0, PI / 2, 'cs0')
                abr = sb(st2, [128, 32]); abi = sb(st2, [128, 32]); fre = sb(st2, [128, 32]); fim = sb(st2, [128, 32])
                den = sb(st2, [128, 32])
                TT('dve', abr[:], mag[:], cs0[:], ALU.mult, ['mag', 'cs0'], ['abr'])
                TT('dve', abi[:], mag[:], sn0[:], ALU.mult, ['mag', 'sn0'], ['abi'])
                CP('dve', rho[:], mag[:], ['mag'], ['rho'])
                TT('dve', den[:], lr[:], lr[:], ALU.mult, ['lr'], ['den'])
                TT('dve', ta[:], li[:], li[:], ALU.mult, ['li'], ['ta'])
                TT('dve', den[:], den[:], ta[:], ALU.add, ['den', 'ta'], ['den'])
                RCP(den[:], den[:], ['den'], ['den'])
                TS('dve', tb[:], abr[:], -1.0, None, ALU.add, None, ['abr'], ['tb'])
                TT('dve', fre[:], tb[:], lr[:], ALU.mult, ['tb', 'lr'], ['fre'])
                TT('dve', ta[:], abi[:], li[:], ALU.mult, ['abi', 'li'], ['ta'])
                TT('dve', fre[:], fre[:], ta[:], ALU.add, ['fre', 'ta'], ['fre'])
                TT('dve', fre[:], fre[:], den[:], ALU.mult, ['fre', 'den'], ['fre'])
                TT('dve', fim[:], abi[:], lr[:], ALU.mult, ['abi', 'lr'], ['fim'])
                TT('dve', ta[:], tb[:], li[:], ALU.mult, ['tb', 'li'], ['ta'])
                TT('dve', fim[:], fim[:], ta[:], ALU.subtract, ['fim', 'ta'], ['fim'])
                TT('dve', fim[:], fim[:], den[:], ALU.mult, ['fim', 'den'], ['fim'])
                Br = sb(st2, [128, 32, 16]); Bi = sb(st2, [128, 32, 16]); t16 = sb(st2, [128, 32, 16])
                fr_b = fre[:].unsqueeze(2).to_broadcast([128, 32, 16]); fi_b = fim[:].unsqueeze(2).to_broadcast([128, 32, 16])
                TT('dve', Br[:], br[:], fr_b, ALU.mult, ['br', 'fre'], ['Br'])
                TT('dve', t16[:], bi[:], fi_b, ALU.mult, ['bi', 'fim'], ['t16'])
                TT('dve', Br[:], Br[:], t16[:], ALU.subtract, ['Br', 't16'], ['Br'])
                TT('dve', Bi[:], bi[:], fr_b, ALU.mult, ['bi', 'fre'], ['Bi'])
                TT('dve', t16[:], br[:], fi_b, ALU.mult, ['br', 'fim'], ['t16'])
                TT('dve', Bi[:], Bi[:], t16[:], ALU.add, ['Bi', 't16'], ['Bi'])
                Z = sb(st2, [128, 128])
                for which, (srcB, dstB, key) in enumerate(((Br, bbr, 'bbr'), (Bi, bbi, 'bbi'))):
                    for sc in range(32):
                        g0 = (2 * sc) % 8
                        MS('pool', Z[:], 0.0, ['Z'])
                        CP('pool', Z[0:64, g0 * 16:(g0 + 1) * 16], srcB[0:64, sc, :], [key[0:1] + 'B' + key[2:], 'Br', 'Bi'], ['Z'])
                        CP('pool', Z[64:128, (g0 + 1) * 16:(g0 + 2) * 16], srcB[64:128, sc, :], ['Br', 'Bi'], ['Z'])
                        TR(PS[1 + sc % 2][:, 0:128], Z[:], ident[:], ['Z', 'ident'], [psk[1 + sc % 2]])
                        CP('act', dstB[:, sc, :], PS[1 + sc % 2][:, 0:128], [psk[1 + sc % 2]], [key])
                cn = sb(st2, [128, 128]); ctr = sb(st2, [128, 128])
                MS('pool', ccr[:], 0.0, ['ccr'])
                MS('pool', cci[:], 0.0, ['cci'])
                for which, (nm, dstC, key, sgn) in enumerate((("s5_c_re", ccr, 'ccr', 1.0), ("s5_c_im", cci, 'cci', -1.0))):
                    for c in range(8):
                        src = I[nm].ap()[oi, c * 8:(c + 1) * 8].rearrange("g h p -> (g h) p")
                        DMA('sp', cn[:, 0:64], src, (), ['cn'])
                        DMA('sp', cn[:, 64:128], src, (), ['cn'])
                        TR(PS[3][:, 0:128], cn[:], ident[:], ['cn', 'ident'], [psk[3]])
                        TS('dve', ctr[:], PS[3][:, 0:128], sgn, None, ALU.mult, None, [psk[3]], ['ctr'])
                        for q in range(4):
                            sc = c * 4 + q
                            g0 = 2 * q
                            CP('pool', dstC[0:64, sc, g0 * 16:(g0 + 1) * 16], ctr[0:64, g0 * 16:(g0 + 1) * 16], ['ctr'], [key])
                            CP('pool', dstC[64:128, sc, (g0 + 1) * 16:(g0 + 2) * 16],
                               ctr[64:128, (g0 + 1) * 16:(g0 + 2) * 16], ['ctr'], [key])
                TS('pool', ccn[:].rearrange("p a b -> p (a b)"), ccr[:].rearrange("p a b -> p (a b)"), -1.0, None, ALU.mult, None, ['ccr'], ['ccn'])
                CP('dve', cosT[:, :, 0], cs0[:], ['cs0'], ['tab'])
                CP('dve', sinT[:, :, 0], sn0[:], ['sn0'], ['tab'])
                tw = sb(st2, [128, 32, 64])
                m = 1
                while m < 128:
                    cm = cosT[:, :, m - 1:m].to_broadcast([128, 32, m]); sm_ = sinT[:, :, m - 1:m].to_broadcast([128, 32, m])
                    c_lo, s_lo = cosT[:, :, 0:m], sinT[:, :, 0:m]
                    c_hi, s_hi = cosT[:, :, m:2 * m], sinT[:, :, m:2 * m]
                    TT('dve', c_hi, c_lo, cm, ALU.mult, ['tab'], ['tab'])
                    TT('dve', tw[:, :, 0:m], s_lo, sm_, ALU.mult, ['tab'], ['tw'])
                    TT('dve', c_hi, c_hi, tw[:, :, 0:m], ALU.subtract, ['tab', 'tw'], ['tab'])
                    TT('dve', s_hi, s_lo, cm, ALU.mult, ['tab'], ['tab'])
                    TT('dve', tw[:, :, 0:m], c_lo, sm_, ALU.mult, ['tab'], ['tw'])
                    TT('dve', s_hi, s_hi, tw[:, :, 0:m], ALU.add, ['tab', 'tw'], ['tab'])
                    m *= 2
                sti = sb(st2, [16, 8192])
                DMA('sp', sti[:], I["st5"].ap()[oi], (), ['sti'])
                stv = sti[:].rearrange("s (sc x c) -> s sc x c", sc=32, c=2)
                for sc in range(32):
                    for c2, dstS in ((0, S0r), (1, S0i)):
                        TR(PS[5][:, (sc % 16) * 32 + c2 * 16:(sc % 16) * 32 + c2 * 16 + 16], stv[:, sc, :, c2],
                           ident[0:16, 0:16], ['sti', 'ident'], [psk[5]])
                    if sc % 16 == 15:
                        h0 = sc - 15
                        pv = PS[5][:, 0:512].rearrange("p (sc c s) -> p sc c s", c=2, s=16)
                        CP('dve', S0r[:, h0:h0 + 16, :], pv[:, :, 0, :], [psk[5]], ['S0'])
                        CP('dve', S0i[:, h0:h0 + 16, :], pv[:, :, 1, :], [psk[5]], ['S0'])
                MS('pool', Sre[:], 0.0, ['Sc%d' % i_ for i_ in range(32)])
                MS('pool', Sim[:], 0.0, ['Sc%d' % i_ for i_ in range(32)])
                P.barrier()
            st3 = contextlib.ExitStack()
            xt = sb(st3, [128, 8, T0]); sqb = sb(st3, [128, 8, T0], BF16)
            hb = sb(st3, [128, 8, T0], BF16); hf = sb(st3, [128, 8, T0]); rstd = sb(st3, [128, T0])
            zz = sb(st3, [128, 8, T0], BF16)
            burs = [sb(st3, [128, T0]) for _ in range(4)]; buis = [sb(st3, [128, T0]) for _ in range(4)]
            p1 = sb(st3, [128, T0]); p2 = sb(st3, [128, T0]); d1 = sb(st3, [128, T0]); d2 = sb(st3, [128, T0])
            bprs = [sb(st3, [128, T0]) for _ in range(4)]; bpis = [sb(st3, [128, T0]) for _ in range(4)]
            wrs = [sb(st3, [128, T0]) for _ in range(4)]; wis = [sb(st3, [128, T0]) for _ in range(4)]
            prods = [[sb(st3, [128, T0], BF16) for _ in range(4)] for _ in range(4)]
            rhs_ = sb(st3, [128, 128]); tn = sb(st3, [128, 16]); yv = sb(st3, [128, T0]); sgm = sb(st3, [128, T0])
            for (c0, T, sample) in tiles(T0):
                DMA('sp', xt[:, :, 0:T], XFv[:, :, c0:c0 + T], ['XF'], ['xt'])
                norm_mod(xt, sqb, hb, rstd, T, l, 0, sample, hf=hf)
                DMA('sp', xt[:, :, 0:T], XFv[:, :, c0:c0 + T], ['XF'], ['xt'])
                nsub = T // 128

                def tabv(tab, sc):
                    if sample:
                        return tab[:, sc, 0:8].unsqueeze(1).to_broadcast([128, 16, 8])
                    return tab[:, sc, :].unsqueeze(1).to_broadcast([128, nsub, 128])

                def tv(ap):
                    return ap.rearrange("p (s t) -> p s t", t=8) if sample else ap.rearrange("p (k t) -> p k t", t=128)

                def front(sc):
                    c, q = sc // 4, sc % 4
                    pr, pi_ = PS[1 + 2 * (q % 2)], PS[2 + 2 * (q % 2)]
                    kr, ki = psk[1 + 2 * (q % 2)], psk[2 + 2 * (q % 2)]
                    bur, bui, bpr, bpi = burs[q], buis[q], bprs[q], bpis[q]
                    kbur, kbui, kbpr, kbpi = ['%s%d' % (n_, q) for n_ in ('bur', 'bui', 'bpr', 'bpi')]
                    MM(pr[:, 0:T], bbr[:, sc, :], hb[:, c, 0:T], True, True, ['bbr', 'hb'], [kr])
                    MM(pi_[:, 0:T], bbi[:, sc, :], hb[:, c, 0:T], True, True, ['bbi', 'hb'], [ki])
                    cv, sv = tabv(cosT, sc), tabv(sinT, sc)
                    if sc % 2 == 1:
                        return [
                            lambda: TT('dve', tv(d1[:, 0:T]), tv(pr[:, 0:T]), cv, ALU.mult, [kr, 'tab'], ['d1']),
                            lambda: TT('dve', tv(d2[:, 0:T]), tv(pi_[:, 0:T]), sv, ALU.mult, [ki, 'tab'], ['d2']),
                            lambda: TT('dve', bpr[:, 0:T], d1[:, 0:T], d2[:, 0:T], ALU.add, ['d1', 'd2'], [kbpr]),
                            lambda: TT('dve', tv(d1[:, 0:T]), tv(pi_[:, 0:T]), cv, ALU.mult, [ki, 'tab'], ['d1']),
                            lambda: TT('dve', tv(d2[:, 0:T]), tv(pr[:, 0:T]), sv, ALU.mult, [kr, 'tab'], ['d2']),
                            lambda: TT('dve', bpi[:, 0:T], d1[:, 0:T], d2[:, 0:T], ALU.subtract, ['d1', 'd2'], [kbpi]),
                        ]
                    CP('act', bur[:, 0:T], pr[:, 0:T], [kr], [kbur])
                    CP('act', bui[:, 0:T], pi_[:, 0:T], [ki], [kbui])
                    TT('pool', tv(p1[:, 0:T]), tv(bur[:, 0:T]), cv, ALU.mult, [kbur, 'tab'], ['p1'])
                    TT('pool', tv(p2[:, 0:T]), tv(bui[:, 0:T]), sv, ALU.mult, [kbui, 'tab'], ['p2'])
                    TT('pool', bpr[:, 0:T], p1[:, 0:T], p2[:, 0:T], ALU.add, ['p1', 'p2'], [kbpr])
                    TT('pool', tv(p1[:, 0:T]), tv(bui[:, 0:T]), cv, ALU.mult, [kbui, 'tab'], ['p1'])
                    TT('pool', tv(p2[:, 0:T]), tv(bur[:, 0:T]), sv, ALU.mult, [kbur, 'tab'], ['p2'])
                    TT('pool', bpi[:, 0:T], p1[:, 0:T], p2[:, 0:T], ALU.subtract, ['p1', 'p2'], [kbpi])
                    return []

                def chain(sc):
                    q = sc % 4
                    bpr, bpi, wr, wi = bprs[q], bpis[q], wrs[q], wis[q]
                    kbpr, kbpi, kwr, kwi = ['%s%d' % (n_, q) for n_ in ('bpr', 'bpi', 'wr', 'wi')]
                    th = []
                    if not sample:
                        rb = rho[:, sc:sc + 1].to_broadcast([128, 128])
                        c1, s1 = cosT[:, sc, 127:128], sinT[:, sc, 127:128]
                        for k in range(nsub):
                            sl = slice(k * 128, (k + 1) * 128)
                            e = k * 128 + 127
                            th.append(lambda sl=sl: SCAN(wr[:, sl], rb, bpr[:, sl], Sre[:, sc:sc + 1], ['rho', kbpr, 'Sc%d' % sc], [kwr]))
                            th.append(lambda sl=sl: SCAN(wi[:, sl], rb, bpi[:, sl], Sim[:, sc:sc + 1], ['rho', kbpi, 'Sc%d' % sc], [kwi]))
                            th.append(lambda e=e: TT('dve', tn[:, 2 * q:2 * q + 1], wi[:, e:e + 1], s1, ALU.mult, [kwi, 'tab'], ['tn%d' % q]))
                            th.append(lambda e=e: TT('dve', tn[:, 2 * q + 1:2 * q + 2], wr[:, e:e + 1], s1, ALU.mult, [kwr, 'tab'], ['tn%d' % q]))
                            th.append(lambda e=e: STT(Sre[:, sc:sc + 1], wr[:, e:e + 1], c1, tn[:, 2 * q:2 * q + 1], ALU.mult, ALU.subtract,
                                                      [kwr, 'tab', 'tn%d' % q], ['Sc%d' % sc]))
                            th.append(lambda e=e: STT(Sim[:, sc:sc + 1], wi[:, e:e + 1], c1, tn[:, 2 * q + 1:2 * q + 2], ALU.mult, ALU.add,
                                                      [kwi, 'tab', 'tn%d' % q], ['Sc%d' % sc]))
                    else:
                        b0r = bpr[:, 0:T].rearrange("p (s t) -> p s t", t=8)[:, :, 0]
                        b0i = bpi[:, 0:T].rearrange("p (s t) -> p s t", t=8)[:, :, 0]
                        w7r = wr[:, 0:T].rearrange("p (s t) -> p s t", t=8)[:, :, 7]
                        w7i = wi[:, 0:T].rearrange("p (s t) -> p s t", t=8)[:, :, 7]
                        c8, s8 = cosT[:, sc, 7:8], sinT[:, sc, 7:8]
                        th.append(lambda: TS('dve', rhs_[:], m01[:], rho[:, sc:sc + 1], None, ALU.mult, None, ['m01', 'rho'], ['rhs']))
                        th.append(lambda: STT(b0r, S0r[:, sc, :], rho[:, sc:sc + 1], b0r, ALU.mult, ALU.add, ['S0', 'rho', kbpr], [kbpr]))
                        th.append(lambda: STT(b0i, S0i[:, sc, :], rho[:, sc:sc + 1], b0i, ALU.mult, ALU.add, ['S0', 'rho', kbpi], [kbpi]))
                        th.append(lambda: SCAN(wr[:, 0:T], rhs_[:], bpr[:, 0:T], 0.0, ['rhs', kbpr], [kwr]))
                        th.append(lambda: SCAN(wi[:, 0:T], rhs_[:], bpi[:, 0:T], 0.0, ['rhs', kbpi], [kwi]))
                        th.append(lambda: TS('dve', tn[:], w7i, s8, None, ALU.mult, None, [kwi, 'tab'], ['tns']))
                        th.append(lambda: STT(Sor[:, sc, :], w7r, c8, tn[:], ALU.mult, ALU.subtract, [kwr, 'tab', 'tns'], ['So']))
                        th.append(lambda: TS('dve', tn[:], w7r, s8, None, ALU.mult, None, [kwr, 'tab'], ['tns']))
                        th.append(lambda: STT(Soi[:, sc, :], w7i, c8, tn[:], ALU.mult, ALU.add, [kwi, 'tab', 'tns'], ['So']))
                    return th

                def filler(sc):
                    q = sc % 4
                    wr, wi = wrs[q], wis[q]
                    kwr, kwi = 'wr%d' % q, 'wi%d' % q
                    cv, sv = tabv(cosT, sc), tabv(sinT, sc)
                    return [
                        lambda: TT('dve', tv(prods[q][0][:, 0:T]), tv(wr[:, 0:T]), cv, ALU.mult, [kwr, 'tab'], ['pr%d' % q]),
                        lambda: TT('dve', tv(prods[q][1][:, 0:T]), tv(wi[:, 0:T]), sv, ALU.mult, [kwi, 'tab'], ['pr%d' % q]),
                        lambda: TT('dve', tv(prods[q][2][:, 0:T]), tv(wi[:, 0:T]), cv, ALU.mult, [kwi, 'tab'], ['pr%d' % q]),
                        lambda: TT('dve', tv(prods[q][3][:, 0:T]), tv(wr[:, 0:T]), sv, ALU.mult, [kwr, 'tab'], ['pr%d' % q]),
                    ]

                def cproj(c):
                    py = PS[5 + c % 2]; ky = psk[5 + c % 2]
                    for q in range(4):
                        sc = c * 4 + q
                        MM(py[:, 0:T], ccr[:, sc, :], prods[q][0][:, 0:T], q == 0, False, ['ccr', 'pr%d' % q], [ky])
                        MM(py[:, 0:T], ccn[:, sc, :], prods[q][1][:, 0:T], False, False, ['ccn', 'pr%d' % q], [ky])
                        MM(py[:, 0:T], cci[:, sc, :], prods[q][2][:, 0:T], False, False, ['cci', 'pr%d' % q], [ky])
                        MM(py[:, 0:T], cci[:, sc, :], prods[q][3][:, 0:T], False, q == 3, ['cci', 'pr%d' % q], [ky])
                    STT(yv[:, 0:T], hf[:, c, 0:T], dsk[:, c:c + 1], py[:, 0:T], ALU.mult, ALU.add, ['hf', 'dsk', ky], ['yv'])
                    ACT(zz[:, c, 0:T], yv[:, 0:T], AF.Gelu_apprx_tanh, ['yv'], ['zz'])

                pend = front(0)
                for t_ in pend:
                    t_()
                for sc in range(33):
                    fl = []
                    if sc + 1 < 32:
                        fl += front(sc + 1)
                    ch = chain(sc) if sc < 32 else []
                    fl += filler(sc - 1) if sc >= 1 else []
                    i_f = 0
                    for i_c, t_ in enumerate(ch):
                        t_()
                        if i_f < len(fl):
                            fl[i_f](); i_f += 1
                    while i_f < len(fl):
                        fl[i_f](); i_f += 1
                    if sc >= 1 and (sc - 1) % 4 == 3:
                        cproj((sc - 1) // 4)
                for m_ in range(8):
                    pa, pb = PS[1 + 2 * (m_ % 2)], PS[2 + 2 * (m_ % 2)]
                    ka, kb = psk[1 + 2 * (m_ % 2)], psk[2 + 2 * (m_ % 2)]
                    for k in range(8):
                        MM(pa[:, 0:T], wgl[:, k, m_ * 128:(m_ + 1) * 128], zz[:, k, 0:T], k == 0, k == 7, ['wgl', 'zz'], [ka])
                    for k in range(8):
                        MM(pb[:, 0:T], wgl[:, k, D + m_ * 128:D + (m_ + 1) * 128], zz[:, k, 0:T], k == 0, k == 7,
                           ['wgl', 'zz'], [kb])
                    ACT(sgm[:, 0:T], pb[:, 0:T], AF.Sigmoid, [kb, 'bgl'], ['sgm'], bias=bgl[:, 8 + m_:9 + m_])
                    STT(yv[:, 0:T], pa[:, 0:T], bgl[:, m_:m_ + 1], sgm[:, 0:T], ALU.add, ALU.mult, [ka, 'bgl', 'sgm'], ['yv'])
                    TT('pool', v2(yv[:, 0:T], sample), v2(yv[:, 0:T], sample), modbc1(l, 2, m_, T, sample), ALU.mult,
                       ['yv', 'MOD'], ['yv'])
                    TT('pool', xt[:, m_, 0:T], xt[:, m_, 0:T], yv[:, 0:T], ALU.add, ['xt', 'yv'], ['xt'])
                DMA('pool', XFv[:, :, c0:c0 + T], xt[:, :, 0:T], ['xt'], ['XF'])
            P.barrier()
            st3.close()
            so = sb(st, [32, 128, 2])
            for c2, srcS in ((0, Sre), (1, Sim)):
                TR(PS[1][0:32, c2 * 128:(c2 + 1) * 128], srcS[:], ident[:], ['Sc%d' % i_ for i_ in range(32)] + ['ident'], [psk[1]])
            CP('dve', so[:].rearrange("p x c -> p c x"), PS[1][0:32, 0:256].rearrange("p (c x) -> p c x", c=2), [psk[1]], ['so'])
            DMA('pool', O["s5_p"].ap()[oi], so[:].rearrange("p x c -> p (x c)"), ['so'], ['o_s5p'])
            sso = sb(st, [16, 8192])
            ssv = sso[:].rearrange("s (sc x c) -> s sc x c", sc=32, c=2)
            for sc in range(32):
                for c2, srcS in ((0, Sor), (1, Soi)):
                    TR(PS[2][0:16, c2 * 128:(c2 + 1) * 128], srcS[:, sc, :], ident[:], ['So', 'ident'], [psk[2]])
                CP('dve', ssv[:, sc].rearrange("s x c -> s c x"), PS[2][0:16, 0:256].rearrange("s (c x) -> s c x", c=2),
                   [psk[2]], ['sso'])
            DMA('pool', O["s5_s"].ap()[oi], sso[:], ['sso'], ['o_s5s'])

        def final(st):
            fg = sb(st, [128, D])
            DMA('sp', fg[:], I["final_g"].ap().to_broadcast([128, D]), (), ['fg'])
            xf = [sb(st, [128, 8, 128]) for _ in range(2)]
            xtm = [sb(st, [128, D]) for _ in range(2)]
            junk = sb(st, [128, D]); ss = sb(st, [128, 1])
            for n in range(NSUB + 1):
                b = n % 2
                DMA('sp', xf[b][:], XFv[:, :, n * 128:(n + 1) * 128], ['XF'], ['xf%d' % b])
                for hh in range(2):
                    pt = PS[1 + 2 * b + hh]; pk = psk[1 + 2 * b + hh]
                    for q in range(4):
                        TR(pt[:, q * 128:(q + 1) * 128], xf[b][:, hh * 4 + q, :], ident[:], ['xf%d' % b, 'ident'], [pk])
                    CP('dve' if hh == 0 else 'act', xtm[b][:, hh * 512:(hh + 1) * 512], pt[:], [pk], ['xtm%d' % b])
                P.op('act', lambda e, b=b: e.activation(junk[:], xtm[b][:], AF.Square, accum_out=ss[:]),
                     ['xtm%d' % b], ['junk', 'ss'])
                ACT(ss[:], ss[:], AF.Sqrt, ['ss', 'epsT'], ['ss'], scale=1.0 / D, bias=epsT[:])
                RCP(ss[:], ss[:], ['ss'], ['ss'])
                STT(xtm[b][:], xtm[b][:], ss[:], fg[:], ALU.mult, ALU.mult, ['xtm%d' % b, 'ss', 'fg'], ['xtm%d' % b])
                dst = O["y_p"].ap()[n * 128:(n + 1) * 128, :] if n < NSUB else O["y_s"].ap()
                DMA('pool', dst, xtm[b][:], ['xtm%d' % b], ['o_y'])

        for l in range(LAYERS):
            if l % 2 == 0:
                ei = l // 2
                stq = contextlib.ExitStack()
                C.qs = sb(stq, [128, 2304])
                with contextlib.ExitStack() as st:
                    even_E1(l, ei, st)
                    P.barrier()
                with contextlib.ExitStack() as st:
                    even_E2(st)
                    P.barrier()
                with contextlib.ExitStack() as st:
                    even_SA(ei, st)
                    P.barrier()
                stq.close()
                with contextlib.ExitStack() as st:
                    even_E3(l, ei, st)
                    P.barrier()
            else:
                with contextlib.ExitStack() as st:
                    odd_O1(l, l // 2, st)
                    P.barrier()
            with contextlib.ExitStack() as st:
                ffn(l, st)
                P.barrier()
        with contextlib.ExitStack() as st:
            final(st)
        P.emit()
    return nc


def host_consts(SEQ):
    NSUB = SEQ // 128
    half = 8
    inv = (np.float32(500000.0) ** (-(2.0 / 16) * np.arange(half, dtype=np.float32))).astype(np.float32)
    pos_p = np.arange(SEQ, dtype=np.float32)
    ang = pos_p[:, None] * inv[None, :]
    rp = np.concatenate([np.cos(ang), np.sin(ang)], axis=1).astype(np.float32)
    ropep = np.ascontiguousarray(rp.reshape(NSUB, 128, 16).transpose(1, 0, 2))
    pos_s = (2048 + np.arange(8)).astype(np.float32)
    angs = pos_s[:, None] * inv[None, :]
    rs = np.concatenate([np.cos(angs), np.sin(angs)], axis=1).astype(np.float32)
    ropes = np.ascontiguousarray(np.tile(rs, (16, 1)))
    k = np.arange(128)[:, None]; q = np.arange(128)[None, :]
    amask = np.where(np.stack([(k >= q), (k <= q)], axis=1), 0.0, -30000.0).astype(np.float32)
    t = (np.arange(128) % 8)[:, None]
    NEG = np.float32(-1e30)
    m = np.arange(128)[None, :]
    sm = np.zeros((128, 280), np.float32)
    sm[:, 0:128] = np.where(m >= t, 0.0, NEG)
    sm[:, 128:256] = np.where((m == 0) & (t >= 4), NEG, 0.0)
    tp = np.arange(8)[None, :]
    sm[:, 256:264] = np.where(tp <= t, 0.0, NEG)
    sm[:, 264:272] = np.where((tp == t) | (tp == t - 4), 0.0, NEG)
    sm[:, 272:280] = np.where(tp == t, 0.0, NEG)
    m01 = np.ones((128, 128), np.float32); m01[:, 0::8] = 0.0
    return dict(c_ident=np.eye(128, dtype=np.float32), c_ropep=ropep, c_ropes=ropes, c_amask=amask,
                c_smask=sm, c_m01=m01)


_NC_CACHE = {}


def kernel(**inp):
    from concourse.bass_utils import run_bass_kernel_spmd
    f = lambda a: np.ascontiguousarray(np.asarray(a, dtype=np.float32))
    SEQ = inp["x_prompt"].shape[1]
    nb = inp["x_prompt"].shape[0]
    ncores = inp["x_sample"].shape[0] // 16
    if SEQ not in _NC_CACHE:
        _NC_CACHE[SEQ] = build(SEQ)
    nc = _NC_CACHE[SEQ]
    consts = host_consts(SEQ)
    wnames = ["norm_g", "w_ada", "b_ada", "w_in", "conv_w", "conv_b", "conv_ln_g", "conv_ln_b", "w_o",
              "s5_lam_re", "s5_lam_im", "s5_log_dt", "s5_b_re", "s5_b_im", "s5_c_re", "s5_c_im", "s5_d",
              "s5_w_glu", "s5_b_glu", "w_ff_gate", "w_ff_up", "w_ff_down"]
    shared = {k: f(inp[k]) for k in wnames}
    shared["final_g"] = f(inp["final_g"]).reshape(1, D)
    shared.update(consts)
    in_maps = []
    for i in range(ncores):
        sl = slice(16 * i, 16 * (i + 1))
        m = dict(shared)
        m["xp"] = f(inp["x_prompt"][i % nb])
        m["xs"] = f(inp["x_sample"][sl]).reshape(NS, D)
        m["call"] = f(np.concatenate([inp["c_prompt"][i % nb][None], inp["c_sample"][sl]], axis=0))
        m["cconv"] = f(inp["cache_conv"][:, sl]).reshape(2, 480, 512)
        m["ckv0"] = f(inp["cache_kv_g0"][:, sl]).reshape(2, 16, -1, 512)
        m["ckv1"] = f(inp["cache_kv_g1"][:, sl]).reshape(2, 16, -1, 512)
        m["ckv2"] = f(inp["cache_kv_g2"][:, sl]).reshape(2, 16, -1, 512)
        m["st5"] = f(inp["state_s5"][:, sl]).reshape(2, 16, 8192)
        in_maps.append(m)
    res = run_bass_kernel_spmd(nc, in_maps, core_ids=list(range(ncores)))
    kernel.last_res = res
    R = res.results
    keep = [min(w, SEQ) for w, _ in DILS]
    y_p = np.stack([R[i]["y_p"] for i in range(nb)]).reshape(nb, SEQ, D)
    y_s = np.concatenate([R[i]["y_s"].reshape(16, 8, D) for i in range(ncores)], axis=0)
    conv_p = np.stack([R[i]["conv_p"] for i in range(nb)], axis=1)
    kvp = [np.stack([R[i]["kv%d_p" % g].reshape(2, keep[g], 2, 4, 64) for i in range(nb)], axis=1) for g in range(3)]
    s5_p = np.stack([R[i]["s5_p"].reshape(2, 64, 64, 2) for i in range(nb)], axis=1)
    conv_s = np.concatenate([R[i]["conv_s"] for i in range(ncores)], axis=1)
    kvs = [np.concatenate([R[i]["kv%d_s" % g].reshape(2, 16, 8, 2, 4, 64) for i in range(ncores)], axis=1)
           for g in range(3)]
    s5_s = np.concatenate([R[i]["s5_s"].reshape(2, 16, 64, 64, 2) for i in range(ncores)], axis=1)
    outs = (y_p, y_s, conv_p, kvp[0], kvp[1], kvp[2], s5_p, conv_s, kvs[0], kvs[1], kvs[2], s5_s)
    return tuple(np.ascontiguousarray(o, dtype=np.float32) for o in outs)
```
